# Optimizing a Trainium2 kernel written in Bass

```python
import jax, jax.numpy as jnp
from jax import lax
import numpy as np


D_MODEL = 2048
BATCH = 2
SEQ = 16384
DEPTH = 1

N_FOX_HEADS = 8
FOX_HEAD_DIM = D_MODEL // 16
FOX_WIDTH = N_FOX_HEADS * FOX_HEAD_DIM
N_MLSTM_HEADS = 4
MLSTM_V_DIM = D_MODEL // 8
MLSTM_QK_DIM = MLSTM_V_DIM // 2
MLSTM_WIDTH = N_MLSTM_HEADS * MLSTM_V_DIM
MLSTM_QK_WIDTH = N_MLSTM_HEADS * MLSTM_QK_DIM
MIX_WIDTH = FOX_WIDTH + MLSTM_WIDTH
CONV_WIDTH = 4
D_FF = ((8 * D_MODEL // 3 + 255) // 256) * 256
Q_BLOCK = 128
MLSTM_CHUNK = 64
N_ADA = 9
ALPHA = (2 * DEPTH) ** 0.25
BETA = (8 * DEPTH) ** -0.25
LN_EPS = 1e-5

COL_FOX_Q = 0
COL_FOX_K = COL_FOX_Q + FOX_WIDTH
COL_FOX_V = COL_FOX_K + FOX_WIDTH
COL_FOX_F = COL_FOX_V + FOX_WIDTH
COL_MLSTM_Q = COL_FOX_F + N_FOX_HEADS
COL_MLSTM_K = COL_MLSTM_Q + MLSTM_QK_WIDTH
COL_MLSTM_V = COL_MLSTM_K + MLSTM_QK_WIDTH
COL_MLSTM_I = COL_MLSTM_V + MLSTM_WIDTH
COL_MLSTM_F = COL_MLSTM_I + N_MLSTM_HEADS
COL_MLSTM_O = COL_MLSTM_F + N_MLSTM_HEADS
IN_WIDTH = COL_MLSTM_O + MLSTM_WIDTH

kernel_name = 'hymba_fox_mlstm_macaron_deepnorm_adaln'


def _layernorm(x, g=None, b=None):
    xf = x.astype(jnp.float32)
    mu = xf.mean(-1, keepdims=True)
    var = jnp.square(xf - mu).mean(-1, keepdims=True)
    y = (xf - mu) * lax.rsqrt(var + LN_EPS)
    if g is not None:
        y = y * g + b
    return y.astype(x.dtype)


def _modulate(x, shift, scale):
    return _layernorm(x) * (1 + scale) + shift


def _swiglu(h, w_in, w_out):
    gate, up = jnp.split(h @ w_in, 2, axis=-1)
    return (jax.nn.silu(gate) * up) @ w_out


def _fox_attention(q, k, v, log_f):
    B, S = q.shape[:2]
    scale = FOX_HEAD_DIM ** -0.5
    qh = jnp.swapaxes(q, 1, 2)
    kh = jnp.swapaxes(k, 1, 2)
    vh = jnp.swapaxes(v, 1, 2)
    F = jnp.cumsum(jnp.swapaxes(log_f, 1, 2), axis=-1)
    key_pos = jnp.arange(S)

    def block(i):
        start = i * Q_BLOCK
        qb = lax.dynamic_slice_in_dim(qh, start, Q_BLOCK, axis=2)
        Fq = lax.dynamic_slice_in_dim(F, start, Q_BLOCK, axis=2)
        logits = (jnp.einsum('bhqd,bhkd->bhqk', qb, kh).astype(jnp.float32) * scale
                  + Fq[..., :, None] - F[..., None, :])
        q_pos = start + jnp.arange(Q_BLOCK)
        mask = key_pos[None, :] <= q_pos[:, None]
        p = jax.nn.softmax(jnp.where(mask, logits, -jnp.inf), axis=-1).astype(vh.dtype)
        return jnp.einsum('bhqk,bhkd->bhqd', p, vh)

    out = lax.map(block, jnp.arange(S // Q_BLOCK))
    return jnp.transpose(out, (1, 0, 3, 2, 4)).reshape(B, S, FOX_WIDTH)


def _causal_dwconv(u, w, b):
    S = u.shape[1]
    up = jnp.pad(u, ((0, 0), (CONV_WIDTH - 1, 0), (0, 0)))
    return sum(w[j] * up[:, j:j + S] for j in range(CONV_WIDTH)) + b


def _mlstm(q, k, v, i_pre, log_f):
    B, S, H, dk = q.shape
    dv = v.shape[-1]
    L = MLSTM_CHUNK
    nc = S // L

    def chunks4(a):
        return jnp.transpose(a.reshape(B, nc, L, H, a.shape[-1]), (1, 0, 3, 2, 4))

    def chunks3(a):
        return jnp.transpose(a.reshape(B, nc, L, H), (1, 0, 3, 2))

    causal = jnp.tril(jnp.ones((L, L), dtype=bool))

    def step(carry, xs):
        C, n, m = carry
        qc, kc, vc, ic, fc = xs
        b = jnp.cumsum(fc, axis=-1)
        dlog = jnp.where(causal, b[..., :, None] - b[..., None, :] + ic[..., None, :], -jnp.inf)
        m_inter = b + m[..., None]
        m_t = jnp.maximum(m_inter, dlog.max(-1))
        s = jnp.einsum('bhtd,bhsd->bhts', qc, kc).astype(jnp.float32) * jnp.exp(dlog - m_t[..., None])
        inter = jnp.exp(m_inter - m_t)
        num = (inter[..., None] * jnp.einsum('bhtd,bhdv->bhtv', qc, C)
               + jnp.einsum('bhts,bhsv->bhtv', s, vc))
        den = inter * jnp.einsum('bhtd,bhd->bht', qc, n) + s.sum(-1)
        h = num / jnp.maximum(jnp.abs(den), jnp.exp(-m_t))[..., None]
        bL = b[..., -1]
        wlog = bL[..., None] - b + ic
        m_new = jnp.maximum(bL + m, wlog.max(-1))
        decay = jnp.exp(bL + m - m_new)
        wk = kc * jnp.exp(wlog - m_new[..., None])[..., None]
        C_new = decay[..., None, None] * C + jnp.einsum('bhsd,bhsv->bhdv', wk, vc)
        n_new = decay[..., None] * n + wk.sum(2)
        return (C_new, n_new, m_new), h

    init = (jnp.zeros((B, H, dk, dv), jnp.float32),
            jnp.zeros((B, H, dk), jnp.float32),
            jnp.zeros((B, H), jnp.float32))
    xs = (chunks4(q), chunks4(k), chunks4(v), chunks3(i_pre), chunks3(log_f))
    _, h = lax.scan(step, init, xs)
    return jnp.transpose(h, (1, 0, 3, 2, 4)).reshape(B, S, H, dv)


def _token_mixer(h, w_in, fox_f_bias, mlstm_conv_w, mlstm_conv_b, mlstm_i_bias,
                 mlstm_f_bias, mlstm_norm_g, w_out):
    B, S, _ = h.shape
    z = h @ w_in
    fq = z[..., COL_FOX_Q:COL_FOX_K].reshape(B, S, N_FOX_HEADS, FOX_HEAD_DIM)
    fk = z[..., COL_FOX_K:COL_FOX_V].reshape(B, S, N_FOX_HEADS, FOX_HEAD_DIM)
    fv = z[..., COL_FOX_V:COL_FOX_F].reshape(B, S, N_FOX_HEADS, FOX_HEAD_DIM)
    fox_logf = jax.nn.log_sigmoid((z[..., COL_FOX_F:COL_MLSTM_Q] + fox_f_bias).astype(jnp.float32))
    y_fox = _fox_attention(fq, fk, fv, fox_logf)
    qk = jax.nn.silu(_causal_dwconv(z[..., COL_MLSTM_Q:COL_MLSTM_V], mlstm_conv_w, mlstm_conv_b))
    mq = qk[..., :MLSTM_QK_WIDTH].reshape(B, S, N_MLSTM_HEADS, MLSTM_QK_DIM)
    mk = qk[..., MLSTM_QK_WIDTH:].reshape(B, S, N_MLSTM_HEADS, MLSTM_QK_DIM) * (MLSTM_QK_DIM ** -0.5)
    mv = z[..., COL_MLSTM_V:COL_MLSTM_I].reshape(B, S, N_MLSTM_HEADS, MLSTM_V_DIM)
    m_i = (z[..., COL_MLSTM_I:COL_MLSTM_F] + mlstm_i_bias).astype(jnp.float32)
    m_logf = jax.nn.log_sigmoid((z[..., COL_MLSTM_F:COL_MLSTM_O] + mlstm_f_bias).astype(jnp.float32))
    hm = _mlstm(mq, mk, mv, m_i, m_logf)
    hm = hm * lax.rsqrt(jnp.square(hm).mean(-1, keepdims=True) + LN_EPS)
    hm = hm * mlstm_norm_g.reshape(N_MLSTM_HEADS, MLSTM_V_DIM)
    o = jax.nn.sigmoid(z[..., COL_MLSTM_O:IN_WIDTH]).reshape(B, S, N_MLSTM_HEADS, MLSTM_V_DIM)
    y_mlstm = (o * hm).reshape(B, S, MLSTM_WIDTH).astype(h.dtype)
    y = jnp.concatenate([y_fox, y_mlstm], axis=-1)
    return y @ w_out


def _layer(x, c, w_ada, b_ada, ffn1_w_in, ffn1_w_out, ln1_g, ln1_b, w_in, fox_f_bias,
           mlstm_conv_w, mlstm_conv_b, mlstm_i_bias, mlstm_f_bias, mlstm_norm_g, w_out,
           ln2_g, ln2_b, ffn2_w_in, ffn2_w_out, ln3_g, ln3_b):
    ada = (jax.nn.silu(c) @ w_ada + b_ada)[:, None, :]
    sh1, sc1, g1, sh2, sc2, g2, sh3, sc3, g3 = jnp.split(ada, N_ADA, axis=-1)
    h = _swiglu(_modulate(x, sh1, sc1), ffn1_w_in, ffn1_w_out)
    x = _layernorm(ALPHA * x + 0.5 * (1 + g1) * h, ln1_g, ln1_b)
    h = _token_mixer(_modulate(x, sh2, sc2), w_in, fox_f_bias, mlstm_conv_w, mlstm_conv_b,
                     mlstm_i_bias, mlstm_f_bias, mlstm_norm_g, w_out)
    x = _layernorm(ALPHA * x + (1 + g2) * h, ln2_g, ln2_b)
    h = _swiglu(_modulate(x, sh3, sc3), ffn2_w_in, ffn2_w_out)
    x = _layernorm(ALPHA * x + 0.5 * (1 + g3) * h, ln3_g, ln3_b)
    return x


def setup_inputs(seed: int = 0) -> dict:
    key = jax.random.key(seed)
    ks = jax.random.split(key, 24)
    f32 = jnp.float32

    def nrm(k, shape, s):
        return jax.random.normal(k, shape, f32) * s

    D = D_MODEL
    return {
        'x': nrm(ks[0], (BATCH, SEQ, D), 1.0),
        'c': nrm(ks[1], (BATCH, D), 1.0),
        'w_ada': nrm(ks[2], (DEPTH, D, N_ADA * D), 0.1 * D ** -0.5),
        'b_ada': nrm(ks[3], (DEPTH, N_ADA * D), 0.01),
        'ffn1_w_in': nrm(ks[4], (DEPTH, D, 2 * D_FF), D ** -0.5),
        'ffn1_w_out': nrm(ks[5], (DEPTH, D_FF, D), BETA * D_FF ** -0.5),
        'ln1_g': 1.0 + nrm(ks[6], (DEPTH, D), 0.05),
        'ln1_b': nrm(ks[7], (DEPTH, D), 0.02),
        'w_in': nrm(ks[8], (DEPTH, D, IN_WIDTH), D ** -0.5),
        'fox_f_bias': jnp.linspace(1.0, 6.0, N_FOX_HEADS, dtype=f32)[None, :] + nrm(ks[9], (DEPTH, N_FOX_HEADS), 0.1),
        'mlstm_conv_w': nrm(ks[10], (DEPTH, CONV_WIDTH, 2 * MLSTM_QK_WIDTH), CONV_WIDTH ** -0.5),
        'mlstm_conv_b': nrm(ks[11], (DEPTH, 2 * MLSTM_QK_WIDTH), 0.02),
        'mlstm_i_bias': nrm(ks[12], (DEPTH, N_MLSTM_HEADS), 0.1),
        'mlstm_f_bias': jnp.linspace(3.0, 6.0, N_MLSTM_HEADS, dtype=f32)[None, :] + nrm(ks[13], (DEPTH, N_MLSTM_HEADS), 0.1),
        'mlstm_norm_g': 1.0 + nrm(ks[14], (DEPTH, MLSTM_WIDTH), 0.05),
        'w_out': nrm(ks[15], (DEPTH, MIX_WIDTH, D), BETA * MIX_WIDTH ** -0.5),
        'ln2_g': 1.0 + nrm(ks[16], (DEPTH, D), 0.05),
        'ln2_b': nrm(ks[17], (DEPTH, D), 0.02),
        'ffn2_w_in': nrm(ks[18], (DEPTH, D, 2 * D_FF), D ** -0.5),
        'ffn2_w_out': nrm(ks[19], (DEPTH, D_FF, D), BETA * D_FF ** -0.5),
        'ln3_g': 1.0 + nrm(ks[20], (DEPTH, D), 0.05),
        'ln3_b': nrm(ks[21], (DEPTH, D), 0.02),
    }


def reference(x, c, w_ada, b_ada, ffn1_w_in, ffn1_w_out, ln1_g, ln1_b, w_in, fox_f_bias,
              mlstm_conv_w, mlstm_conv_b, mlstm_i_bias, mlstm_f_bias, mlstm_norm_g, w_out,
              ln2_g, ln2_b, ffn2_w_in, ffn2_w_out, ln3_g, ln3_b):
    for l in range(DEPTH):
        x = _layer(x, c, w_ada[l], b_ada[l], ffn1_w_in[l], ffn1_w_out[l], ln1_g[l], ln1_b[l],
                   w_in[l], fox_f_bias[l], mlstm_conv_w[l], mlstm_conv_b[l], mlstm_i_bias[l],
                   mlstm_f_bias[l], mlstm_norm_g[l], w_out[l], ln2_g[l], ln2_b[l],
                   ffn2_w_in[l], ffn2_w_out[l], ln3_g[l], ln3_b[l])
    return x
```

```python
import contextlib
import os
import numpy as np
import ml_dtypes
import concourse.bass as bass
import concourse.mybir as mybir
from concourse.bass_utils import run_bass_kernel_spmd

F32 = mybir.dt.float32
BF16 = mybir.dt.bfloat16
AF = mybir.ActivationFunctionType
ALU = mybir.AluOpType

D = 2048
S = 16384
TPC = 4096
G = 512
NG = 8
DFF = 5632
NFF = 44
INW = 6160
ALPHA = 2.0 ** 0.25
EPS = 1e-5
NEG = -30000.0
RG = [[0, 1, 2, 3], [4, 5, 6, 7]]


_UC = [0]


def _u(name):
    _UC[0] += 1
    return "%s_u%d" % (name, _UC[0])


class Buf:
    def __init__(self, name):
        self.name = name
        self.w = None
        self.r = []


class Slot:
    def __init__(self, nc, name):
        self.sem = nc.alloc_semaphore(name)
        self.name = name
        self.cnt = 0


class KB:
    def __init__(self, nc):
        self.nc = nc
        self.eng = {'pe': nc.tensor, 'act': nc.scalar, 'dve': nc.vector, 'pool': nc.gpsimd, 'sp': nc.sync}
        self.sem = {e: nc.alloc_semaphore('c_' + e) for e in ('pe', 'act', 'dve', 'pool')}
        self.cnt = {e: 0 for e in self.sem}
        self.seen = {e: {} for e in self.eng}
        self.pend = []
        self.slots = []
        self.nslot = 0

    def slot(self, name):
        s = Slot(self.nc, 'd_%s_%d' % (name, self.nslot))
        self.nslot += 1
        self.slots.append(s)
        return s

    def wait(self, e, tok):
        if tok is None:
            return
        sem, key, val = tok
        if key == e and e == 'pe':
            return
        if self.seen[e].get(key, 0) >= val:
            return
        self.eng[e].wait_ge(sem, val)
        self.seen[e][key] = val

    def _deps(self, e, reads, writes):
        for b in reads:
            self.wait(e, b.w)
        for b in writes:
            self.wait(e, b.w)
            for t in b.r:
                self.wait(e, t)

    def op(self, e, fn, reads=(), writes=(), sig=True):
        self._deps(e, reads, writes)
        ins = fn(self.eng[e])
        if sig:
            self.cnt[e] += 1
            ins.then_inc(self.sem[e], 1)
            tok = (self.sem[e], e, self.cnt[e])
            if e == 'pe':
                for (rb, wb) in self.pend:
                    for b in rb:
                        b.r.append(tok)
                    for b in wb:
                        b.w = tok
                        b.r = []
                self.pend = []
            for b in reads:
                b.r.append(tok)
            for b in writes:
                b.w = tok
                b.r = []
            return tok
        assert e == 'pe'
        self.pend.append((list(reads), list(writes)))
        return None

    def dma(self, q, out, in_, slot, reads=(), writes=(), **kw):
        self._deps(q, reads, writes)
        self.eng[q].dma_start(out=out, in_=in_, **kw).then_inc(slot.sem, 16)
        slot.cnt += 16
        tok = (slot.sem, slot.name, slot.cnt)
        for b in reads:
            b.r.append(tok)
        for b in writes:
            b.w = tok
            b.r = []
        return tok

    def cc(self, ins, outs, slot, reads=(), writes=()):
        self._deps('pool', reads, writes)
        self.nc.gpsimd.collective_compute("AllGather", ALU.bypass, replica_groups=RG,
                                          ins=[ins.opt()], outs=[outs.opt()]).then_inc(slot.sem, 1)
        slot.cnt += 1
        tok = (slot.sem, slot.name, slot.cnt)
        for b in reads:
            b.r.append(tok)
        for b in writes:
            b.w = tok
            b.r = []
        return tok

    def barrier(self, engines=('pe', 'act', 'dve', 'pool', 'sp')):
        toks = [(self.sem[e], e, self.cnt[e]) for e in self.sem if self.cnt[e] > 0]
        toks += [(s.sem, s.name, s.cnt) for s in self.slots if s.cnt > 0]
        for e in engines:
            for t in toks:
                self.wait(e, t)


def build(dbg=False):
    nc = bass.Bass("TRN2", target_bir_lowering=False)
    kb = KB(nc)

    def din(name, shape, dt=F32):
        return nc.dram_tensor(name, shape, dt, kind="ExternalInput").ap()

    def dscr(name, shape, dt):
        return nc.dram_tensor(name, shape, dt, kind="Internal").ap()

    x_in = din("x", [TPC, D])
    cT = din("cT", [128, 16])
    w_ada = din("w_ada", [D, 9 * D])
    b_adaT = din("b_adaT", [128, 144])
    wf = {"w1a": din("ffn1_w_in", [D, 2 * DFF]), "w2a": din("ffn1_w_out", [DFF, D]),
          "wout": din("w_out", [D, D]),
          "w1b": din("ffn2_w_in", [D, 2 * DFF]), "w2b": din("ffn2_w_out", [DFF, D])}
    lnp = din("lnp", [6, D])
    convw = din("convw", [128, 8])
    convb = din("convb", [128, 2])
    normg = din("normg", [1, 256])
    wino_f = din("win_own", [D, 1540])
    gbo = din("gb_own", [1, 4])
    wino = dscr("wino_bf", [D, 1540], BF16)
    ident_d = din("ident", [128, 128])
    tri_d = din("tri", [128, 128])
    sel_d = din("sel127", [128, 128])
    mask_d = din("masks", [128, 4, 512])
    out_d = nc.dram_tensor("out", [TPC, D], F32, kind="ExternalOutput").ap()

    wb = {k: dscr(k + "_bf", list(v.shape), BF16) for k, v in wf.items()}
    x1_d = dscr("x1s", [TPC, D], F32)
    gate_scr = dscr("gate_scr", [48, 128], F32)

    def sb(name, shape, dt):
        return nc.alloc_sbuf_tensor(_u(name), shape, dt)

    ident = sb("ident", [128, 128], F32)
    identb = sb("identb", [128, 128], BF16)
    onesb = sb("onesb", [128, 128], BF16)
    onesf = sb("onesf", [128, 128], F32)
    epsc = sb("epsc", [128, 1], F32)
    tri = sb("tri", [128, 128], F32)
    sel127 = sb("sel127", [128, 128], F32)
    adaT = sb("adaT", [128, 144], F32)
    modsc = sb("modsc", [128, 3, 16], F32)
    B_const = Buf("const")
    B_ada = Buf("ada")

    ps = [nc.alloc_psum_tensor("ps%d" % i, [128, 512], F32) for i in range(8)]
    PB = [Buf("ps%d" % i) for i in range(8)]

    ld = kb.slot("const")
    for t, dsrc in ((ident, ident_d), (tri, tri_d), (sel127, sel_d)):
        kb.dma('sp', t[:], dsrc, ld, writes=[B_const])
    kb.op('dve', lambda e: e.tensor_copy(out=identb[:], in_=ident[:]), reads=[B_const], writes=[B_const])
    kb.op('dve', lambda e: e.memset(onesb[:], 1.0), writes=[B_const])
    kb.op('dve', lambda e: e.memset(onesf[:], 1.0), writes=[B_const])
    kb.op('dve', lambda e: e.memset(epsc[:], EPS), writes=[B_const])

    WB = {k: Buf("wb_" + k) for k in wf}
    wslot = {k: kb.slot("wc_" + k) for k in wf}

    def cast_weight(k, nchunk):
        src, dst = wf[k], wb[k]
        rows = src.shape[0]
        r = rows // nchunk
        for i in range(nchunk):
            kb.dma('pool', dst[i * r:(i + 1) * r, :], src[i * r:(i + 1) * r, :], wslot[k], writes=[],
                   max_dma_last_dim=8192)
        WB[k].w = (wslot[k].sem, wslot[k].name, wslot[k].cnt)

    cast_weight("w1a", 8)
    cast_weight("w2a", 4)
    wf["wino"] = wino_f
    wb["wino"] = wino
    WB["wino"] = Buf("wb_wino")
    wslot["wino"] = kb.slot("wc_wino")
    cast_weight("wino", 2)

    _es = contextlib.ExitStack()
    aw0 = _es.enter_context(nc.sbuf_tensor(_u("ada_w0"), [128, 16, 512], F32))
    aw1 = _es.enter_context(nc.sbuf_tensor(_u("ada_w1"), [128, 16, 512], F32))
    scT = _es.enter_context(nc.sbuf_tensor(_u("scT"), [128, 16], F32))
    badaT = _es.enter_context(nc.sbuf_tensor(_u("badaT"), [128, 144], F32))
    gT = _es.enter_context(nc.sbuf_tensor(_u("gT"), [128, 48], F32))
    gTT = _es.enter_context(nc.sbuf_tensor(_u("gTT"), [48, 128], F32))
    with _es:
        aws = [aw0, aw1]
        AWB = [Buf("aw0"), Buf("aw1")]
        awslot = [kb.slot("aw0"), kb.slot("aw1")]
        B_sc = Buf("scT")
        ld2 = kb.slot("ld2")
        ld3 = kb.slot("ld3")
        kb.dma('sp', scT[:], cT, ld2, writes=[B_sc])
        kb.dma('sp', badaT[:], b_adaT, ld2, writes=[B_sc])
        kb.op('act', lambda e: e.activation(out=scT[:], in_=scT[:], func=AF.Silu), reads=[B_sc], writes=[B_sc])
        w_ada_v = w_ada.rearrange("(kc p) n -> p kc n", p=128)
        for blk in range(36):
            a = aws[blk % 2]
            kb.dma('sp', a[:], w_ada_v[:, :, blk * 512:(blk + 1) * 512], awslot[blk % 2], writes=[AWB[blk % 2]])
            for cc in range(4):
                ch = blk * 4 + cc
                for kc in range(16):
                    last = (kc == 15)
                    kb.op('pe', lambda e, a=a, cc=cc, kc=kc, ch=ch: e.matmul(
                        ps[0][:, ch:ch + 1], lhsT=a[:, kc, cc * 128:(cc + 1) * 128], rhs=scT[:, kc:kc + 1],
                        start=(kc == 0), stop=(kc == 15)),
                        reads=[AWB[blk % 2], B_sc], writes=[PB[0]], sig=(last and cc == 3))
        kb.op('dve', lambda e: e.tensor_tensor(out=adaT[:], in0=ps[0][:, 0:144], in1=badaT[:], op=ALU.add),
              reads=[PB[0], B_sc], writes=[B_ada])
        for k in range(3):
            kb.op('dve', lambda e, k=k: e.tensor_scalar(out=modsc[:, k, :], in0=adaT[:, (3 * k + 1) * 16:(3 * k + 2) * 16],
                                                       scalar1=1.0, scalar2=None, op0=ALU.add),
                  reads=[B_ada], writes=[B_ada])
        for k, coef in enumerate((0.5, 1.0, 0.5)):
            kb.op('dve', lambda e, k=k, coef=coef: e.tensor_scalar(
                out=gT[:, k * 16:(k + 1) * 16], in0=adaT[:, (3 * k + 2) * 16:(3 * k + 3) * 16],
                scalar1=1.0, scalar2=coef, op0=ALU.add, op1=ALU.mult), reads=[B_ada], writes=[B_sc])
        kb.op('pe', lambda e: e.transpose(ps[1][0:48, 0:128], gT[:, 0:48], ident[:]), reads=[B_sc, B_const], writes=[PB[1]])
        kb.op('dve', lambda e: e.tensor_copy(out=gTT[:], in_=ps[1][0:48, 0:128]), reads=[PB[1]], writes=[B_sc])
        B_gscr = Buf("gscr")
        kb.dma('sp', gate_scr, gTT[:], ld3, reads=[B_sc], writes=[B_gscr])
        kb.barrier()

    cast_weight("wout", 2)
    cast_weight("w1b", 8)
    cast_weight("w2b", 4)
    STOP = os.environ.get("KSTOP", "")
    if STOP == "cast":
        kb.barrier()
        return nc

    gate_rows = gate_scr.rearrange("(k c) p -> k (c p)", k=3)

    def ffn_phase(phase):
        _es = contextlib.ExitStack()
        xt = _es.enter_context(nc.sbuf_tensor(_u("xt"), [128, 4, D], F32))
        xn = _es.enter_context(nc.sbuf_tensor(_u("xn"), [128, D], F32))
        hT = _es.enter_context(nc.sbuf_tensor(_u("hT"), [128, 16, G], BF16))
        HT = _es.enter_context(nc.sbuf_tensor(_u("HT"), [128, NFF, G], BF16))
        wA0 = _es.enter_context(nc.sbuf_tensor(_u("wA0"), [128, 16, 512], BF16))
        wA1 = _es.enter_context(nc.sbuf_tensor(_u("wA1"), [128, 16, 512], BF16))
        wA2 = _es.enter_context(nc.sbuf_tensor(_u("wA2"), [128, 16, 512], BF16))
        wB0 = _es.enter_context(nc.sbuf_tensor(_u("wB0"), [128, 1024], BF16))
        wB1 = _es.enter_context(nc.sbuf_tensor(_u("wB1"), [128, 1024], BF16))
        wB2 = _es.enter_context(nc.sbuf_tensor(_u("wB2"), [128, 1024], BF16))
        wB3 = _es.enter_context(nc.sbuf_tensor(_u("wB3"), [128, 1024], BF16))
        tabs = _es.enter_context(nc.sbuf_tensor(_u("tabs"), [128, 3, D], F32))
        tmp = _es.enter_context(nc.sbuf_tensor(_u("tmp"), [128, 2, 512], F32))
        sg = _es.enter_context(nc.sbuf_tensor(_u("sg"), [128, 2, 512], F32))
        stat = _es.enter_context(nc.sbuf_tensor(_u("stat"), [128, 4, 6], F32))
        mv = _es.enter_context(nc.sbuf_tensor(_u("mv"), [128, 4], F32))
        gst = _es.enter_context(nc.sbuf_tensor(_u("gst"), [128, 4, 16], F32))
        gso = _es.enter_context(nc.sbuf_tensor(_u("gso"), [128, 4, 16], F32))
        wg = _es.enter_context(nc.sbuf_tensor(_u("wg"), [128, 16, 16], BF16))
        with _es:
            stage = HT[:, 0:24, :]
            wA = [wA0, wA1, wA2]
            WA = [Buf("wA%d" % i) for i in range(3)]
            wAs = [kb.slot("wA%d" % i) for i in range(3)]
            wBt = [wB0, wB1, wB2, wB3]
            WBb = [Buf("wB%d" % i) for i in range(4)]
            wBs = [kb.slot("wB%d" % i) for i in range(4)]
            st = {"a": 0, "b": 0}
            B_xt = [Buf("xt%d" % i) for i in range(4)]
            B_xn = Buf("xn")
            B_hT = Buf("hT")
            B_HT = [Buf("HT%d" % i) for i in range(NFF)]
            B_tabs = Buf("tabs")
            B_tmp = [Buf("tmp0"), Buf("tmp1")]
            B_sg = [Buf("sg0"), Buf("sg1")]
            B_stat = Buf("stat")
            B_stage = Buf("stage")
            B_gst = Buf("gst")
            B_gso = Buf("gso")
            B_gso = Buf("gso")
            B_wg = Buf("wg")
            xs = kb.slot("xld")
            tbs = kb.slot("tabs")
            sts = kb.slot("store")
            sts_a = kb.slot("store_a")
            sts_b = kb.slot("store_b")
            sts_t = kb.slot("store_t")
            wgs = kb.slot("wg")
            ccs = kb.slot("cc")
            B_x1d = Buf("x1d")

            def loadA(src_ap_fn, wkey):
                i = st["a"] % 3
                st["a"] += 1
                for (o, s_) in src_ap_fn(wA[i]):
                    kb.dma('sp', o, s_, wAs[i], reads=[WB[wkey]], writes=[WA[i]])
                return i

            def load_tabs(rows):
                for i, r in enumerate(rows):
                    kb.dma('sp', tabs[:, i, :], r.partition_broadcast(128), tbs, reads=[B_gscr], writes=[B_tabs])

            def ln_stats(tt, col):
                for q in range(4):
                    kb.op('dve', lambda e, q=q: e.bn_stats(out=stat[:, q, :], in_=xt[:, tt, q * 512:(q + 1) * 512]),
                          reads=[B_xt[tt]], writes=[B_stat])
                kb.op('dve', lambda e: e.bn_aggr(out=mv[:, 0:2], in_=stat[:].rearrange("p a b -> p (a b)")),
                      reads=[B_stat], writes=[B_stat])
                kb.op('act', lambda e: e.activation(out=mv[:, 3:4], in_=mv[:, 1:2], func=AF.Sqrt, bias=epsc[:, 0:1]), reads=[B_stat, B_const], writes=[B_stat])
                kb.op('dve', lambda e: e.reciprocal(out=mv[:, 2:3], in_=mv[:, 3:4]), reads=[B_stat], writes=[B_stat])

            def mod_transpose(k):
                for tt in range(4):
                    ln_stats(tt, 0)
                    kb.op('dve', lambda e, tt=tt: e.tensor_scalar(out=xn[:], in0=xt[:, tt, :], scalar1=mv[:, 0:1],
                                                                 scalar2=mv[:, 2:3], op0=ALU.subtract, op1=ALU.mult),
                          reads=[B_xt[tt], B_stat], writes=[B_xn])
                    for q in range(4):
                        pi = 4 + (q % 2)
                        for c4 in range(4):
                            kc = q * 4 + c4
                            kb.op('pe', lambda e, kc=kc, c4=c4, pi=pi: e.transpose(
                                ps[pi][:, c4 * 128:(c4 + 1) * 128], xn[:, kc * 128:(kc + 1) * 128], ident[:]),
                                reads=[B_xn, B_const], writes=[PB[pi]], sig=(c4 == 3))
                        for c4 in range(4):
                            kc = q * 4 + c4
                            kb.op('act', lambda e, kc=kc, c4=c4, pi=pi, tt=tt: e.activation(
                                out=hT[:, kc, tt * 128:(tt + 1) * 128], in_=ps[pi][:, c4 * 128:(c4 + 1) * 128],
                                func=AF.Identity, scale=modsc[:, k, kc:kc + 1],
                                bias=adaT[:, (3 * k) * 16 + kc:(3 * k) * 16 + kc + 1]),
                                reads=[PB[pi], B_ada], writes=[B_hT])

            def ffn(w1k, w2k, k, gt_row, lng_row, lnb_row):
                w1 = wb[w1k].rearrange("(kc p) n -> p kc n", p=128)
                w2 = wb[w2k]
                mod_transpose(k)
                load_tabs([gate_rows[gt_row:gt_row + 1, :], lnp[lng_row:lng_row + 1, :], lnp[lnb_row:lnb_row + 1, :]])
                def srcs(i2):
                    return lambda t: [(t[:, :, 0:256], w1[:, :, i2 * 256:(i2 + 1) * 256]),
                                      (t[:, :, 256:512], w1[:, :, DFF + i2 * 256:DFF + (i2 + 1) * 256])]
                pre = [loadA(srcs(0), w1k), loadA(srcs(1), w1k)]
                for i2 in range(22):
                    wi = pre[i2]
                    if i2 + 2 < 22:
                        pre.append(loadA(srcs(i2 + 2), w1k))
                    for c in range(2):
                        i = i2 * 2 + c
                        pg, pu = (0, 1) if (i % 2 == 0) else (2, 3)
                        for kc in range(16):
                            kb.op('pe', lambda e, wi=wi, kc=kc, c=c, pg=pg: e.matmul(
                                ps[pg][:, :], lhsT=wA[wi][:, kc, c * 128:(c + 1) * 128], rhs=hT[:, kc, :],
                                start=(kc == 0), stop=(kc == 15)), reads=[WA[wi], B_hT], writes=[PB[pg]], sig=(kc == 15))
                        for kc in range(16):
                            kb.op('pe', lambda e, wi=wi, kc=kc, c=c, pu=pu: e.matmul(
                                ps[pu][:, :], lhsT=wA[wi][:, kc, 256 + c * 128:256 + (c + 1) * 128], rhs=hT[:, kc, :],
                                start=(kc == 0), stop=(kc == 15)), reads=[WA[wi], B_hT], writes=[PB[pu]], sig=(kc == 15))
                        kb.op('act', lambda e, pg=pg, i=i: e.activation(out=sg[:, i % 2, :], in_=ps[pg][:, :], func=AF.Silu),
                              reads=[PB[pg]], writes=[B_sg[i % 2]])
                        kb.op('dve', lambda e, pu=pu, i=i: e.tensor_tensor(out=HT[:, i, :], in0=sg[:, i % 2, :], in1=ps[pu][:, :],
                                                                          op=ALU.mult),
                              reads=[PB[pu], B_sg[i % 2]], writes=[B_HT[i]] + ([B_stage] if i < 24 else []))
                def loadB(dh, i):
                    s_ = st["b"] % 4
                    st["b"] += 1
                    kb.dma('sp', wBt[s_][:], w2[i * 128:(i + 1) * 128, dh * 1024:(dh + 1) * 1024], wBs[s_],
                           reads=[WB[w2k]], writes=[WBb[s_]])
                    return s_
                seq = [(dh, i) for dh in range(2) for i in range(NFF)]
                preb = [loadB(*seq[0]), loadB(*seq[1]), loadB(*seq[2])]
                for n, (dh, i) in enumerate(seq):
                    s_ = preb[n]
                    if n + 3 < len(seq):
                        preb.append(loadB(*seq[n + 3]))
                    for tt in range(4):
                        for dgi in range(2):
                            pi = tt * 2 + dgi
                            kb.op('pe', lambda e, s_=s_, tt=tt, dgi=dgi, pi=pi, i=i: e.matmul(
                                ps[pi][:, :], lhsT=HT[:, i, tt * 128:(tt + 1) * 128], rhs=wBt[s_][:, dgi * 512:(dgi + 1) * 512],
                                start=(i == 0), stop=(i == NFF - 1)),
                                reads=[WBb[s_], B_HT[i]], writes=[PB[pi]], sig=(i == NFF - 1 or (tt == 3 and dgi == 1)))
                    if i == NFF - 1:
                        for tt in range(4):
                            for dgi in range(2):
                                pi = tt * 2 + dgi
                                c0 = dh * 1024 + dgi * 512
                                residual(tt, pi, c0)
                for tt in range(4):
                    ln_affine(tt)

            def residual(tt, pi, c0):
                j = pi % 2
                kb.op('dve', lambda e: e.tensor_tensor(out=tmp[:, j, :], in0=ps[pi][:, :], in1=tabs[:, 0, c0:c0 + 512], op=ALU.mult),
                      reads=[PB[pi], B_tabs], writes=[B_tmp[j]])
                kb.op('dve', lambda e: e.scalar_tensor_tensor(out=xt[:, tt, c0:c0 + 512], in0=xt[:, tt, c0:c0 + 512], scalar=ALPHA,
                                                             in1=tmp[:, j, :], op0=ALU.mult, op1=ALU.add),
                      reads=[B_tmp[j]], writes=[B_xt[tt]])

            def ln_affine(tt):
                ln_stats(tt, 0)
                kb.op('dve', lambda e: e.tensor_scalar(out=xt[:, tt, :], in0=xt[:, tt, :], scalar1=mv[:, 0:1], scalar2=mv[:, 2:3],
                                                       op0=ALU.subtract, op1=ALU.mult), reads=[B_stat], writes=[B_xt[tt]])
                kb.op('dve', lambda e: e.tensor_tensor(out=xt[:, tt, :], in0=xt[:, tt, :], in1=tabs[:, 1, :], op=ALU.mult),
                      reads=[B_tabs], writes=[B_xt[tt]])
                kb.op('dve', lambda e: e.tensor_tensor(out=xt[:, tt, :], in0=xt[:, tt, :], in1=tabs[:, 2, :], op=ALU.add),
                      reads=[B_tabs], writes=[B_xt[tt]])


            def send_h2(g):
                mod_transpose(1)
                kb.dma('pool', H2s[g].rearrange("kc p t -> p kc t"), hT[:], sts_a, reads=[B_hT], writes=[B_h2s[g]])
                for kc in range(16):
                    kb.cc(H2s[g][kc], H2g[g][kc], ccs, reads=[B_h2s[g]], writes=[B_h2g[g]])

            def attn_out(g):
                woutv = wb["wout"].rearrange("(kc p) n -> p kc n", p=128)
                for c in range(4):
                    pass
                for rb in range(4):
                    kb.dma('sp', hT[:].rearrange("p (r rb) t -> p rb r t", rb=4)[:, rb], myY[g, rb].rearrange("(r p) t -> p r t", p=128), xs,
                           reads=[B_yg], writes=[B_hT])
                load_tabs([gate_rows[1:2, :], lnp[2:3, :], lnp[3:4, :]])
                fch = []
                for r in range(4):
                    fch += [2 * r, 2 * r + 1, 8 + 2 * r, 9 + 2 * r]
                for dg in range(4):
                    wi = loadA(lambda t, dg=dg: [(t[:], woutv[:, :, dg * 512:(dg + 1) * 512])], "wout")
                    for tt in range(4):
                        pi = 4 * (dg % 2) + tt
                        for ch in range(16):
                            kb.op('pe', lambda e, wi=wi, ch=ch, tt=tt, pi=pi: e.matmul(
                                ps[pi][:, :], lhsT=hT[:, ch, tt * 128:(tt + 1) * 128], rhs=wA[wi][:, fch[ch], :],
                                start=(ch == 0), stop=(ch == 15)), reads=[WA[wi], B_hT], writes=[PB[pi]], sig=(ch == 15))
                        residual(tt, pi, dg * 512)
                for tt in range(4):
                    ln_affine(tt)

            for g in range(NG):
                rows = slice(g * G, (g + 1) * G)
                if phase == 1:
                    kb.dma('sp', xt[:], x_in[rows, :].rearrange("(tt p) d -> p tt d", p=128), xs, writes=B_xt)
                    ffn("w1a", "w2a", 0, 0, 0, 1)
                    if STOP == "g0ffn":
                        kb.dma('pool', out_d[rows, :].rearrange("(tt p) d -> p tt d", p=128), xt[:], sts, reads=B_xt, writes=[B_outd])
                        break
                    kb.dma('pool', x1_d[rows, :].rearrange("(tt p) d -> p tt d", p=128), xt[:], sts, reads=B_xt, writes=[B_x1d])
                    if STOP == "g0":
                        kb.dma('pool', out_d[rows, :].rearrange("(tt p) d -> p tt d", p=128), xt[:], sts, reads=B_xt, writes=[B_outd])
                    send_h2(g)
                    if STOP == "g0":
                        break
                else:
                    kb.dma('sp', xt[:], x1_d[rows, :].rearrange("(tt p) d -> p tt d", p=128), xs, reads=[B_x1d], writes=B_xt)
                    attn_out(g)
                    ffn("w1b", "w2b", 2, 2, 4, 5)
                    kb.dma('pool', out_d[rows, :].rearrange("(tt p) d -> p tt d", p=128), xt[:], sts, reads=B_xt, writes=[B_outd])
            kb.barrier()

    jsp = nc.sync.partition_id() % 4
    H2s = [dscr("h2s%d" % g, [16, 128, G], BF16) for g in range(NG)]
    H2g = [dscr("h2g%d" % g, [16, 512, G], BF16) for g in range(NG)]
    B_h2s = [Buf("h2s%d" % g) for g in range(NG)]
    B_h2g = [Buf("h2g%d" % g) for g in range(NG)]
    FM_d = dscr("fm_d", [6, 128, S], BF16)
    TM_d = dscr("tm_d", [S, 768], BF16)
    GT_d = dscr("gt_d", [S, 4], F32)
    YS = dscr("ys_p", [32, 4, 128, G], BF16)
    YG = dscr("yg_p", [32, 4, 512, G], BF16)
    myY = dscr("myY", [8, 4, 512, G], BF16)
    B_fm = Buf("fm")
    B_yg = Buf("yg")
    B_outd = Buf("outd")

    ffn_phase(1)
    if STOP in ("g0ffn", "g0", "p1a"):
        kb.barrier()
        return nc
    zproj_own(nc, kb, ps, PB, H2g, B_h2g, wino, gbo, FM_d, TM_d, GT_d, B_fm)
    if STOP == "p1":
        dF = nc.dram_tensor("dbgF", [6, 128, S], BF16, kind="ExternalOutput").ap()
        dT = nc.dram_tensor("dbgT", [S, 768], BF16, kind="ExternalOutput").ap()
        d32 = nc.dram_tensor("dbg32", [S, 4], F32, kind="ExternalOutput").ap()
        dx1 = nc.dram_tensor("dbgx1", [TPC, D], F32, kind="ExternalOutput").ap()
        dsl = kb.slot("dbg")
        kb.dma('sp', dF, FM_d, dsl, reads=[B_fm])
        kb.dma('sp', dT, TM_d, dsl, reads=[B_fm])
        kb.dma('sp', d32, GT_d, dsl, reads=[B_fm])
        kb.dma('sp', dx1, x1_d, dsl)
        kb.barrier()
        return nc
    attention(nc, kb, ps, PB, FM_d, TM_d, GT_d, B_fm, YS, YG, B_yg, ident, identb, onesb, onesf, tri, sel127,
              B_const, mask_d, convw, convb, normg, epsc)
    if STOP == "attn":
        dY = nc.dram_tensor("dbgY", [32, 4, 128, G], BF16, kind="ExternalOutput").ap()
        dsl = kb.slot("dbg")
        kb.dma('sp', dY, YS, dsl)
        kb.barrier()
        return nc
    ysl = kb.slot("myY")
    for t in range(8):
        kb.dma('sp', myY[t], YG[jsp * 8 + t], ysl, reads=[B_yg], writes=[B_yg])
    ffn_phase(3)
    kb.barrier()
    return nc


def zproj_own(nc, kb, ps, PB, H2g, B_h2g, wino, gbo, FM_d, TM_d, GT_d, B_fm):
    _es = contextlib.ExitStack()
    wo = _es.enter_context(nc.sbuf_tensor(_u("wo"), [128, 16, 1540], BF16))
    h2a = _es.enter_context(nc.sbuf_tensor(_u("h2a"), [128, 16, G], BF16))
    h2b = _es.enter_context(nc.sbuf_tensor(_u("h2b"), [128, 16, G], BF16))
    stFa = _es.enter_context(nc.sbuf_tensor(_u("stFa"), [128, 6, G], BF16))
    stFb = _es.enter_context(nc.sbuf_tensor(_u("stFb"), [128, 6, G], BF16))
    stTa = _es.enter_context(nc.sbuf_tensor(_u("stTa"), [128, 4, 768], BF16))
    stTb = _es.enter_context(nc.sbuf_tensor(_u("stTb"), [128, 4, 768], BF16))
    stGa = _es.enter_context(nc.sbuf_tensor(_u("stGa"), [128, 4, 4], F32))
    stGb = _es.enter_context(nc.sbuf_tensor(_u("stGb"), [128, 4, 4], F32))
    gb = _es.enter_context(nc.sbuf_tensor(_u("gbo"), [128, 4], F32))
    with _es:
        h2 = [h2a, h2b]
        stF = [stFa, stFb]
        stT = [stTa, stTb]
        stG = [stGa, stGb]
        B_h2 = [Buf("h2a"), Buf("h2b")]
        B_stF = [Buf("stFa"), Buf("stFb")]
        B_stT = [Buf("stTa"), Buf("stTb")]
        B_stG = [Buf("stGa"), Buf("stGb")]
        B_wo = Buf("wo")
        wsl = kb.slot("wo")
        hsl = [kb.slot("h2a"), kb.slot("h2b")]
        ssl = [kb.slot("zsta"), kb.slot("zstb")]
        kb.dma('sp', wo[:], wino.rearrange("(kc p) n -> p kc n", p=128), wsl, reads=[], writes=[B_wo])
        kb.dma('sp', gb[:], gbo.partition_broadcast(128), wsl, writes=[B_wo])
        FMv = FM_d.rearrange("c p t -> p c t")

        def load(tg):
            r, g = tg // 8, tg % 8
            b = tg % 2
            kb.dma('sp', h2[b][:], H2g[g][:, r * 128:(r + 1) * 128, :].rearrange("kc p t -> p kc t"), hsl[b],
                   reads=[B_h2g[g]], writes=[B_h2[b]])
        load(0)
        _ZS = os.environ.get('ZPSKIP', '')
        NB_ = int(os.environ.get('ZPNB', '260'))
        for tg in range(int(os.environ.get('ZPN', '32'))):
            b = tg % 2
            tok0 = tg * G
            if tg + 1 < 32:
                load(tg + 1)
            for c in range(0 if 'fm' in _ZS else 6):
                pi = c % 4
                for kc in range(16):
                    kb.op('pe', lambda e, c=c, kc=kc, pi=pi, b=b: e.matmul(ps[pi][:, :], lhsT=wo[:, kc, c * 128:(c + 1) * 128], rhs=h2[b][:, kc, :],
                                                                      start=(kc == 0), stop=(kc == 15)),
                          reads=[B_wo, B_h2[b]], writes=[PB[pi]], sig=(kc == 15))
                if c % 2 == 0:
                    kb.op('act', lambda e, c=c, pi=pi, b=b: e.activation(out=stF[b][:, c, :], in_=ps[pi][:, :], func=AF.Copy),
                          reads=[PB[pi]], writes=[B_stF[b]])
                else:
                    kb.op('dve', lambda e, c=c, pi=pi, b=b: e.tensor_copy(out=stF[b][:, c, :], in_=ps[pi][:, :]),
                          reads=[PB[pi]], writes=[B_stF[b]])
            if 'fm' not in _ZS:
                kb.dma('pool', FMv[:, :, tok0:tok0 + G], stF[b][:], ssl[b], reads=[B_stF[b]], writes=[B_fm])
            for tt in range(0 if 'tm' in _ZS else 4):
                pa, pb_ = (4, 6) if tt % 2 == 0 else (5, 7)
                for kc in range(16):
                    kb.op('pe', lambda e, kc=kc, tt=tt, pa=pa, b=b: e.matmul(ps[pa][:, :], lhsT=h2[b][:, kc, tt * 128:(tt + 1) * 128], rhs=wo[:, kc, 768:1280],
                                                                        start=(kc == 0), stop=(kc == 15)),
                          reads=[B_wo, B_h2[b]], writes=[PB[pa]], sig=(kc == 15))
                for kc in range(16):
                    kb.op('pe', lambda e, kc=kc, tt=tt, pb_=pb_, b=b: e.matmul(ps[pb_][:, 0:NB_], lhsT=h2[b][:, kc, tt * 128:(tt + 1) * 128], rhs=wo[:, kc, 1280:1280 + NB_],
                                                                          start=(kc == 0), stop=(kc == 15)),
                          reads=[B_wo, B_h2[b]], writes=[PB[pb_]], sig=(kc == 15))
                kb.op('dve', lambda e, tt=tt, pa=pa, b=b: e.tensor_copy(out=stT[b][:, tt, 0:512], in_=ps[pa][:, :]), reads=[PB[pa]], writes=[B_stT[b]])
                if 'sig' not in _ZS:
                    kb.op('act', lambda e, tt=tt, pb_=pb_, b=b: e.activation(out=stT[b][:, tt, 512:768], in_=ps[pb_][:, 0:256], func=AF.Sigmoid),
                          reads=[PB[pb_]], writes=[B_stT[b]])
                if 'gadd' not in _ZS:
                    kb.op('act', lambda e, tt=tt, pb_=pb_, b=b: e.activation(out=stG[b][:, tt, :], in_=ps[pb_][:, 256:260], func=AF.Copy),
                          reads=[PB[pb_]], writes=[B_stG[b]])
                    kb.op('dve', lambda e, tt=tt, b=b: e.tensor_tensor(out=stG[b][:, tt, :], in0=stG[b][:, tt, :], in1=gb[:], op=ALU.add),
                          reads=[B_wo], writes=[B_stG[b]])
            for (a_, b_) in (() if ('tm' in _ZS or 'ls' in _ZS) else ((0, 2), (3, 4))):
                kb.op('act', lambda e, a_=a_, b_=b_, b=b: e.activation(out=stG[b][:, :, a_:b_], in_=stG[b][:, :, a_:b_], func=AF.Exp, scale=-1.0),
                      reads=[B_stG[b]], writes=[B_stG[b]])
                kb.op('act', lambda e, a_=a_, b_=b_, b=b: e.activation(out=stG[b][:, :, a_:b_], in_=stG[b][:, :, a_:b_], func=AF.Ln, bias=1.0),
                      reads=[B_stG[b]], writes=[B_stG[b]])
                kb.op('dve', lambda e, a_=a_, b_=b_, b=b: e.tensor_scalar(out=stG[b][:, :, a_:b_], in0=stG[b][:, :, a_:b_], scalar1=-1.0, scalar2=None,
                                                                         op0=ALU.mult), reads=[B_stG[b]], writes=[B_stG[b]])
            if 'tm' not in _ZS and 'tst' not in _ZS:
              kb.dma('pool', TM_d[tok0:tok0 + G, :].rearrange("(tt p) c -> p tt c", p=128), stT[b][:], ssl[b], reads=[B_stT[b]], writes=[B_fm])
            if 'tm' not in _ZS and 'gst' not in _ZS:
              kb.dma('pool', GT_d[tok0:tok0 + G, :].rearrange("(tt p) c -> p tt c", p=128), stG[b][:], ssl[b], reads=[B_stG[b]], writes=[B_fm])
        kb.barrier()


def attention(nc, kb, ps, PB, FM_d, TM_d, GT_d, B_fm, YS, YG, B_yg, ident, identb, onesb, onesf, tri, sel127,
              B_const, mask_d, convw_d, convb_d, normg_d, epsc):
    TMv = TM_d.rearrange("(t p) c -> p t c", p=128)
    GTv = GT_d.rearrange("(t p) c -> p t c", p=128)
    B_ysp = Buf("ysp")
    SCALE = 128.0 ** -0.5
    NT = 128
    ccs = kb.slot("ccy")
    sts = kb.slot("ysto")

    _es = contextlib.ExitStack()
    KT = _es.enter_context(nc.sbuf_tensor(_u("KT"), [128, S], BF16))
    QT = _es.enter_context(nc.sbuf_tensor(_u("QT"), [128, S], BF16))
    V = _es.enter_context(nc.sbuf_tensor(_u("V"), [128, NT, 128], BF16))
    lf = _es.enter_context(nc.sbuf_tensor(_u("lf"), [128, NT, 2], F32))
    Fm = _es.enter_context(nc.sbuf_tensor(_u("Fm"), [128, 2, NT], F32))
    scan = _es.enter_context(nc.sbuf_tensor(_u("scan"), [128, 2, 2 * NT], F32))
    offb = _es.enter_context(nc.sbuf_tensor(_u("offb"), [128, 2 * NT], F32))
    negF = _es.enter_context(nc.sbuf_tensor(_u("negF"), [128, 2, NT], F32))
    masks = _es.enter_context(nc.sbuf_tensor(_u("masks"), [128, 4, 512], F32))
    lg = _es.enter_context(nc.sbuf_tensor(_u("lg"), [128, 3, 512], F32))
    PT = _es.enter_context(nc.sbuf_tensor(_u("PT"), [128, 3, 512], BF16))
    fq = _es.enter_context(nc.sbuf_tensor(_u("fq"), [128, 2, 512], F32))
    rden = _es.enter_context(nc.sbuf_tensor(_u("rden"), [128, 512], F32))
    yst = _es.enter_context(nc.sbuf_tensor(_u("yst"), [128, 2, 512], BF16))
    with _es:
        lds = kb.slot("fxld")
        mks = kb.slot("fxmask")
        B_KT, B_QT, B_V, B_lf, B_F, B_scan, B_mask = Buf("KT"), Buf("QT"), Buf("V"), Buf("lf"), Buf("F"), Buf("scan"), Buf("mask")
        B_lg = [Buf("lg%d" % i) for i in range(3)]
        B_PT = [Buf("PT%d" % i) for i in range(3)]
        B_fq = [Buf("fq0"), Buf("fq1")]
        B_rden = Buf("rden")
        B_yst = [Buf("yst0"), Buf("yst1")]
        mks0 = kb.slot("fxmask0")
        kb.dma('sp', masks[:], mask_d, mks0, writes=[B_mask])
        for i8 in range(8):
            kb.dma('sp', lf[:, i8 * 16:(i8 + 1) * 16, :], GTv[:, i8 * 16:(i8 + 1) * 16, 0:2], mks, reads=[B_fm], writes=[B_lf])
        lfv = lf[:].rearrange("p t h -> p h t")
        kb.op('dve', lambda e: e.tensor_copy(out=scan[:, 0, :].rearrange("p (h t) -> p h t", h=2), in_=lfv), reads=[B_lf], writes=[B_scan])
        kb.op('pe', lambda e: e.matmul(ps[0][:, 0:256], lhsT=tri[:], rhs=scan[:, 0, :], start=True, stop=True),
              reads=[B_scan, B_const], writes=[PB[0]])
        kb.op('dve', lambda e: e.tensor_copy(out=Fm[:].rearrange("p h t -> p (h t)"), in_=ps[0][:, 0:256]), reads=[PB[0]], writes=[B_F])
        kb.op('pe', lambda e: e.matmul(ps[1][:, 0:256], lhsT=sel127[:], rhs=Fm[:].rearrange("p h t -> p (h t)"), start=True, stop=True),
              reads=[B_F, B_const], writes=[PB[1]])
        kb.op('dve', lambda e: e.tensor_copy(out=scan[:, 0, :], in_=ps[1][:, 0:256]), reads=[PB[1]], writes=[B_scan])
        kb.op('dve', lambda e: e.tensor_copy(out=offb[:], in_=scan[:, 0, :]), reads=[B_scan], writes=[B_scan])
        cur = 0
        sh = 1
        while sh < NT:
            nxt = 1 - cur
            sc3 = scan[:, cur, :].rearrange("p (h t) -> p h t", h=2)
            sn3 = scan[:, nxt, :].rearrange("p (h t) -> p h t", h=2)
            kb.op('dve', lambda e, sc3=sc3, sn3=sn3, sh=sh: e.tensor_copy(out=sn3[:, :, 0:sh], in_=sc3[:, :, 0:sh]),
                  reads=[B_scan], writes=[B_scan])
            kb.op('dve', lambda e, sc3=sc3, sn3=sn3, sh=sh: e.tensor_tensor(out=sn3[:, :, sh:NT], in0=sc3[:, :, sh:NT], in1=sc3[:, :, 0:NT - sh],
                                                                          op=ALU.add), reads=[B_scan], writes=[B_scan])
            cur = nxt
            sh *= 2
        kb.op('dve', lambda e: e.tensor_tensor(out=offb[:], in0=scan[:, cur, :], in1=offb[:], op=ALU.subtract), reads=[B_scan], writes=[B_scan])
        kb.op('dve', lambda e: e.tensor_tensor(out=Fm[:].rearrange("p h t -> p (h t)"), in0=Fm[:].rearrange("p h t -> p (h t)"), in1=offb[:],
                                               op=ALU.add), reads=[B_scan], writes=[B_F])
        kb.op('dve', lambda e: e.tensor_scalar(out=negF[:].rearrange("p h t -> p (h t)"), in0=Fm[:].rearrange("p h t -> p (h t)"),
                                               scalar1=-1.0, scalar2=None, op0=ALU.mult), reads=[B_F], writes=[B_F])

        for hh in range(2):
            kb.dma('sp', QT[:], FM_d[hh], lds, reads=[B_fm], writes=[B_QT])
            kb.dma('sp', KT[:], FM_d[2 + hh], lds, reads=[B_fm], writes=[B_KT])
            for i8 in range(8):
                kb.dma('sp', V[:, i8 * 16:(i8 + 1) * 16, :], TMv[:, i8 * 16:(i8 + 1) * 16, hh * 128:(hh + 1) * 128], lds, reads=[B_fm], writes=[B_V])
            for qg in range(32):
                nk = 4 * qg + 4
                fb = qg % 2
                for c4 in range(4):
                    kb.op('pe', lambda e, c4=c4, qg=qg: e.matmul(
                        ps[7][:, c4 * 128:(c4 + 1) * 128], lhsT=Fm[:, hh, 4 * qg + c4:4 * qg + c4 + 1].to_broadcast([128, 128]),
                        rhs=ident[:], start=True, stop=True), reads=[B_F, B_const], writes=[PB[7]], sig=(c4 == 3))
                kb.op('act', lambda e, fb=fb: e.activation(out=fq[:, fb, :], in_=ps[7][:, :], func=AF.Copy), reads=[PB[7]], writes=[B_fq[fb]])
                po, pd = (3, 4) if qg % 2 == 0 else (5, 6)
                for kt in range(nk):
                    si = kt % 3
                    kb.op('pe', lambda e, kt=kt, qg=qg, si=si: e.matmul(ps[si][:, :], lhsT=KT[:, kt * 128:(kt + 1) * 128],
                                                                      rhs=QT[:, qg * 512:(qg + 1) * 512], start=True, stop=True),
                          reads=[B_KT, B_QT], writes=[PB[si]])
                    kb.op('dve', lambda e, si=si, fb=fb: e.scalar_tensor_tensor(out=lg[:, si, :], in0=ps[si][:, :], scalar=SCALE, in1=fq[:, fb, :],
                                                                              op0=ALU.mult, op1=ALU.add),
                          reads=[PB[si], B_fq[fb]], writes=[B_lg[si]])
                    if kt >= 4 * qg:
                        dd = kt - 4 * qg
                        kb.op('dve', lambda e, si=si, dd=dd: e.tensor_tensor(out=lg[:, si, :], in0=lg[:, si, :], in1=masks[:, dd, :], op=ALU.add),
                              reads=[B_mask], writes=[B_lg[si]])
                    kb.op('act', lambda e, si=si, kt=kt: e.activation(out=PT[:, si, :], in_=lg[:, si, :], func=AF.Exp, bias=negF[:, hh, kt:kt + 1]),
                          reads=[B_lg[si], B_F], writes=[B_PT[si]])
                    last = (kt == nk - 1)
                    kb.op('pe', lambda e, si=si, kt=kt, po=po, last=last: e.matmul(ps[po][:, :], lhsT=V[:, kt, :], rhs=PT[:, si, :],
                                                                                start=(kt == 0), stop=last),
                          reads=[B_V, B_PT[si]], writes=[PB[po]], sig=False)
                    kb.op('pe', lambda e, si=si, kt=kt, pd=pd, last=last: e.matmul(ps[pd][:, :], lhsT=onesb[:], rhs=PT[:, si, :],
                                                                                start=(kt == 0), stop=last),
                          reads=[B_const, B_PT[si]], writes=[PB[pd]], sig=True)
                kb.op('dve', lambda e, pd=pd: e.reciprocal(out=rden[:], in_=ps[pd][:, :]), reads=[PB[pd]], writes=[B_rden])
                kb.op('dve', lambda e, po=po, fb=fb: e.tensor_tensor(out=yst[:, fb, :], in0=ps[po][:, :], in1=rden[:], op=ALU.mult),
                      reads=[PB[po], B_rden], writes=[B_yst[fb]])
                kb.dma('pool', YS[qg, hh], yst[:, fb, :], sts, reads=[B_yst[fb]], writes=[B_ysp])
                kb.cc(YS[qg, hh], YG[qg, hh], ccs, reads=[B_ysp], writes=[B_yg])
        kb.barrier()

    QW = 4096
    _es = contextlib.ExitStack()
    zq = _es.enter_context(nc.sbuf_tensor(_u("zq"), [128, QW + 3], BF16))
    zk = _es.enter_context(nc.sbuf_tensor(_u("zk"), [128, QW + 3], BF16))
    qT = _es.enter_context(nc.sbuf_tensor(_u("qT"), [128, QW], BF16))
    kT = _es.enter_context(nc.sbuf_tensor(_u("kT"), [128, QW], BF16))
    v1 = _es.enter_context(nc.sbuf_tensor(_u("v1"), [128, 32, 257], BF16))
    og = _es.enter_context(nc.sbuf_tensor(_u("og"), [128, 32, 256], BF16))
    gi = _es.enter_context(nc.sbuf_tensor(_u("gi"), [128, 32, 2], F32))
    gsm = _es.enter_context(nc.sbuf_tensor(_u("gsm"), [128, 8, 32], F32))
    cw = _es.enter_context(nc.sbuf_tensor(_u("cw"), [128, 8], F32))
    cb = _es.enter_context(nc.sbuf_tensor(_u("cb"), [128, 2], F32))
    ngt = _es.enter_context(nc.sbuf_tensor(_u("ngt"), [128, 256], F32))
    acc = _es.enter_context(nc.sbuf_tensor(_u("acc"), [128, 512], F32))
    trim = _es.enter_context(nc.sbuf_tensor(_u("trim"), [128, 128], F32))
    e1 = _es.enter_context(nc.sbuf_tensor(_u("e1"), [128, 128], F32))
    DT = _es.enter_context(nc.sbuf_tensor(_u("DT"), [128, 128], F32))
    SD = _es.enter_context(nc.sbuf_tensor(_u("SD"), [128, 128], BF16))
    qc = _es.enter_context(nc.sbuf_tensor(_u("qc"), [128, 257], F32))
    nd = _es.enter_context(nc.sbuf_tensor(_u("nd"), [128, 257], F32))
    sm = _es.enter_context(nc.sbuf_tensor(_u("sm"), [128, 8], F32))
    hh_ = _es.enter_context(nc.sbuf_tensor(_u("hh_"), [128, 256], F32))
    sq = _es.enter_context(nc.sbuf_tensor(_u("sq"), [128, 256], F32))
    go = _es.enter_context(nc.sbuf_tensor(_u("go"), [128, 256], F32))
    yy = _es.enter_context(nc.sbuf_tensor(_u("yy"), [128, 256], F32))
    kw = _es.enter_context(nc.sbuf_tensor(_u("kw"), [128, 128], BF16))
    Cs = _es.enter_context(nc.sbuf_tensor(_u("Cs"), [128, 257], F32))
    Cb = _es.enter_context(nc.sbuf_tensor(_u("Cb"), [128, 257], BF16))
    yTs = _es.enter_context(nc.sbuf_tensor(_u("yTs"), [128, 2, QW], BF16))
    with _es:
        lds = kb.slot("mlld")
        B_z, B_qk, B_v1, B_og, B_gi, B_gsm, B_c = Buf("z"), Buf("qk"), Buf("v1"), Buf("og"), Buf("gi"), Buf("gsm"), Buf("mc")
        B_acc, B_e1, B_DT, B_SD, B_qc, B_nd, B_sm, B_h, B_sq, B_go, B_yy, B_kw, B_C, B_Cb, B_yT = [Buf(n) for n in
            ("acc", "e1", "DT", "SD", "qc", "nd", "sm", "h", "sq", "go", "yy", "kw", "C", "Cb", "yT")]
        kb.dma('sp', cw[:], convw_d, lds, writes=[B_c])
        kb.dma('sp', cb[:], convb_d, lds, writes=[B_c])
        kb.dma('sp', ngt[:], normg_d.partition_broadcast(128), lds, writes=[B_c])
        kb.dma('sp', trim[:], mask_d[:, 0, 0:128], lds, writes=[B_c])
        kb.op('dve', lambda e: e.memset(Cs[:], 0.0), writes=[B_C])
        kb.op('dve', lambda e: e.memset(Cb[:], 0.0), writes=[B_Cb])
        kb.op('dve', lambda e: e.memset(zq[:, 0:3], 0.0), writes=[B_z])
        kb.op('dve', lambda e: e.memset(zk[:, 0:3], 0.0), writes=[B_z])
        for c in range(4):
            if c > 0:
                kb.op('dve', lambda e: e.tensor_copy(out=zq[:, 0:3], in_=zq[:, QW:QW + 3]), reads=[B_z], writes=[B_z])
                kb.op('dve', lambda e: e.tensor_copy(out=zk[:, 0:3], in_=zk[:, QW:QW + 3]), reads=[B_z], writes=[B_z])
            kb.dma('sp', zq[:, 3:3 + QW], FM_d[4][:, c * QW:(c + 1) * QW], lds, reads=[B_fm], writes=[B_z])
            kb.dma('sp', zk[:, 3:3 + QW], FM_d[5][:, c * QW:(c + 1) * QW], lds, reads=[B_fm], writes=[B_z])
            for i4 in range(4):
                tsl = slice(c * 32 + i4 * 8, c * 32 + (i4 + 1) * 8)
                kb.dma('sp', v1[:, i4 * 8:(i4 + 1) * 8, 0:256], TMv[:, tsl, 256:512], lds, reads=[B_fm], writes=[B_v1])
                kb.dma('sp', og[:, i4 * 8:(i4 + 1) * 8, :], TMv[:, tsl, 512:768], lds, reads=[B_fm], writes=[B_og])
                kb.dma('sp', gi[:, i4 * 8:(i4 + 1) * 8, :], GTv[:, tsl, 2:4], lds, reads=[B_fm], writes=[B_gi])
            kb.op('dve', lambda e: e.memset(v1[:, :, 256:257], 1.0), writes=[B_v1])
            for which, (zz, oo) in enumerate(((zq, qT), (zk, kT))):
                for pc in range(QW // 512):
                    t0 = pc * 512
                    kb.op('dve', lambda e, zz=zz, t0=t0: e.tensor_scalar(out=acc[:], in0=zz[:, t0:t0 + 512], scalar1=cw[:, which * 4:which * 4 + 1],
                                                                       scalar2=None, op0=ALU.mult), reads=[B_z, B_c], writes=[B_acc])
                    for j_ in range(1, 4):
                        kb.op('dve', lambda e, zz=zz, t0=t0, j_=j_: e.scalar_tensor_tensor(
                            out=acc[:], in0=zz[:, t0 + j_:t0 + j_ + 512], scalar=cw[:, which * 4 + j_:which * 4 + j_ + 1], in1=acc[:],
                            op0=ALU.mult, op1=ALU.add), reads=[B_z, B_c], writes=[B_acc])
                    if which == 0:
                        kb.op('act', lambda e, oo=oo, t0=t0: e.activation(out=oo[:, t0:t0 + 512], in_=acc[:], func=AF.Silu, bias=cb[:, 0:1]),
                              reads=[B_acc, B_c], writes=[B_qk])
                    else:
                        kb.op('act', lambda e: e.activation(out=acc[:], in_=acc[:], func=AF.Silu, bias=cb[:, 1:2]), reads=[B_c], writes=[B_acc])
                        kb.op('dve', lambda e, oo=oo, t0=t0: e.tensor_scalar(out=oo[:, t0:t0 + 512], in0=acc[:], scalar1=SCALE, scalar2=None, op0=ALU.mult),
                              reads=[B_acc], writes=[B_qk])
            giv = gi[:].rearrange("p t c -> p c t")
            kb.op('dve', lambda e: e.tensor_copy(out=gsm[:, 6, :], in_=giv[:, 1, :]), reads=[B_gi], writes=[B_gsm])
            kb.op('dve', lambda e: e.tensor_copy(out=gsm[:, 7, :], in_=giv[:, 0, :]), reads=[B_gi], writes=[B_gsm])
            kb.op('pe', lambda e: e.matmul(ps[0][:, 0:32], lhsT=tri[:], rhs=gsm[:, 6, :], start=True, stop=True), reads=[B_gsm, B_const], writes=[PB[0]])
            kb.op('pe', lambda e: e.matmul(ps[1][:, 0:32], lhsT=onesf[:], rhs=gsm[:, 6, :], start=True, stop=True), reads=[B_gsm, B_const], writes=[PB[1]])
            kb.op('dve', lambda e: e.tensor_copy(out=gsm[:, 0, :], in_=ps[0][:, 0:32]), reads=[PB[0]], writes=[B_gsm])
            kb.op('dve', lambda e: e.tensor_copy(out=gsm[:, 1, :], in_=ps[1][:, 0:32]), reads=[PB[1]], writes=[B_gsm])
            kb.op('dve', lambda e: e.tensor_tensor(out=gsm[:, 2, :], in0=gsm[:, 7, :], in1=gsm[:, 0, :], op=ALU.subtract), reads=[B_gsm], writes=[B_gsm])
            kb.op('dve', lambda e: e.tensor_tensor(out=gsm[:, 3, :], in0=gsm[:, 2, :], in1=gsm[:, 1, :], op=ALU.add), reads=[B_gsm], writes=[B_gsm])
            kb.op('act', lambda e: e.activation(out=gsm[:, 3, :], in_=gsm[:, 3, :], func=AF.Exp), reads=[B_gsm], writes=[B_gsm])
            kb.op('act', lambda e: e.activation(out=gsm[:, 4, :], in_=gsm[:, 0, :], func=AF.Exp), reads=[B_gsm], writes=[B_gsm])
            kb.op('act', lambda e: e.activation(out=gsm[:, 5, :], in_=gsm[:, 1, :], func=AF.Exp), reads=[B_gsm], writes=[B_gsm])
            for ch in range(32):
                cs = slice(ch * 128, (ch + 1) * 128)
                kb.op('pe', lambda e, cs=cs: e.matmul(ps[2][:, 0:128], lhsT=kT[:, cs], rhs=qT[:, cs], start=True, stop=True),
                      reads=[B_qk], writes=[PB[2]])
                kb.op('pe', lambda e, ch=ch: e.matmul(ps[3][:, 0:128], lhsT=gsm[:, 0, ch:ch + 1].to_broadcast([128, 128]), rhs=ident[:],
                                                    start=True, stop=True), reads=[B_gsm, B_const], writes=[PB[3]])
                kb.op('dve', lambda e: e.tensor_tensor(out=e1[:], in0=ps[3][:, 0:128], in1=trim[:], op=ALU.add), reads=[PB[3], B_c], writes=[B_e1])
                kb.op('act', lambda e, ch=ch: e.activation(out=DT[:], in_=e1[:], func=AF.Exp, bias=gsm[:, 2, ch:ch + 1]),
                      reads=[B_e1, B_gsm], writes=[B_DT])
                kb.op('dve', lambda e: e.tensor_tensor(out=SD[:], in0=ps[2][:, 0:128], in1=DT[:], op=ALU.mult), reads=[PB[2], B_DT], writes=[B_SD])
                kb.op('pe', lambda e, ch=ch: e.matmul(ps[4][:, 0:257], lhsT=SD[:], rhs=v1[:, ch, :], start=True, stop=True),
                      reads=[B_SD, B_v1], writes=[PB[4]])
                kb.op('pe', lambda e, cs=cs: e.matmul(ps[5][:, 0:257], lhsT=qT[:, cs], rhs=Cb[:], start=True, stop=True),
                      reads=[B_qk, B_Cb], writes=[PB[5]])
                kb.op('act', lambda e, ch=ch: e.activation(out=qc[:], in_=ps[5][:, 0:257], func=AF.Identity, scale=gsm[:, 4, ch:ch + 1]),
                      reads=[PB[5], B_gsm], writes=[B_qc])
                kb.op('dve', lambda e: e.tensor_tensor(out=nd[:], in0=ps[4][:, 0:257], in1=qc[:], op=ALU.add), reads=[PB[4], B_qc], writes=[B_nd])
                kb.op('act', lambda e: e.activation(out=sm[:, 5:6], in_=nd[:, 256:257], func=AF.Abs), reads=[B_nd], writes=[B_sm])
                kb.op('dve', lambda e: e.tensor_scalar(out=sm[:, 0:1], in0=sm[:, 5:6], scalar1=1.0, scalar2=None, op0=ALU.max),
                      reads=[B_sm], writes=[B_sm])
                kb.op('dve', lambda e: e.reciprocal(out=sm[:, 1:2], in_=sm[:, 0:1]), reads=[B_sm], writes=[B_sm])
                kb.op('dve', lambda e: e.tensor_scalar(out=hh_[:], in0=nd[:, 0:256], scalar1=sm[:, 1:2], scalar2=None, op0=ALU.mult),
                      reads=[B_nd, B_sm], writes=[B_h])
                kb.op('act', lambda e: e.activation(out=sq[:], in_=hh_[:], func=AF.Square, accum_out=sm[:, 2:3]), reads=[B_h], writes=[B_sq, B_sm])
                kb.op('act', lambda e: e.activation(out=sm[:, 4:5], in_=sm[:, 2:3], func=AF.Sqrt, scale=1.0 / 256.0, bias=epsc[:, 0:1]),
                      reads=[B_sm, B_const], writes=[B_sm])
                kb.op('dve', lambda e: e.reciprocal(out=sm[:, 3:4], in_=sm[:, 4:5]), reads=[B_sm], writes=[B_sm])
                kb.op('dve', lambda e, ch=ch: e.tensor_tensor(out=go[:], in0=og[:, ch, :], in1=ngt[:], op=ALU.mult), reads=[B_og, B_c], writes=[B_go])
                kb.op('dve', lambda e: e.scalar_tensor_tensor(out=yy[:], in0=hh_[:], scalar=sm[:, 3:4], in1=go[:], op0=ALU.mult, op1=ALU.mult),
                      reads=[B_h, B_sm, B_go], writes=[B_yy])
                for half in range(2):
                    kb.op('pe', lambda e, half=half: e.transpose(ps[6][:, half * 128:(half + 1) * 128], yy[:, half * 128:(half + 1) * 128], ident[:]),
                          reads=[B_yy, B_const], writes=[PB[6]], sig=(half == 1))
                kb.op('act', lambda e, cs=cs: e.activation(out=yTs[:, :, cs], in_=ps[6][:, 0:256].rearrange("p (a b) -> p a b", a=2), func=AF.Copy),
                      reads=[PB[6]], writes=[B_yT])
                kb.op('pe', lambda e, cs=cs: e.matmul(ps[7][:, 0:128], lhsT=kT[:, cs], rhs=identb[:], start=True, stop=True),
                      reads=[B_qk, B_const], writes=[PB[7]])
                kb.op('act', lambda e, ch=ch: e.activation(out=kw[:], in_=ps[7][:, 0:128], func=AF.Identity, scale=gsm[:, 3, ch:ch + 1]),
                      reads=[PB[7], B_gsm], writes=[B_kw])
                kb.op('pe', lambda e, ch=ch: e.matmul(ps[0][:, 0:257], lhsT=kw[:], rhs=v1[:, ch, :], start=True, stop=True),
                      reads=[B_kw, B_v1], writes=[PB[0]])
                kb.op('dve', lambda e, ch=ch: e.scalar_tensor_tensor(out=Cs[:], in0=Cs[:], scalar=gsm[:, 5, ch:ch + 1], in1=ps[0][:, 0:257],
                                                                    op0=ALU.mult, op1=ALU.add), reads=[PB[0], B_gsm], writes=[B_C])
                kb.op('dve', lambda e: e.tensor_copy(out=Cb[:], in_=Cs[:]), reads=[B_C], writes=[B_Cb])
            for tb in range(8):
                for a_ in range(2):
                    kb.dma('pool', YS[c * 8 + tb, 2 + a_], yTs[:, a_, tb * G:(tb + 1) * G], sts, reads=[B_yT], writes=[B_ysp])
                    kb.cc(YS[c * 8 + tb, 2 + a_], YG[c * 8 + tb, 2 + a_], ccs, reads=[B_ysp], writes=[B_yg])
        kb.barrier()


_CACHE = {}


def kernel(**inputs):
    f32 = np.float32
    x = np.asarray(inputs["x"], f32)
    c = np.asarray(inputs["c"], f32)
    sq = lambda k: np.ascontiguousarray(np.asarray(inputs[k], f32)[0])
    if "nc" not in _CACHE:
        _CACHE["nc"] = build()
    nc = _CACHE["nc"]
    lnp = np.stack([sq("ln1_g"), sq("ln1_b"), sq("ln2_g"), sq("ln2_b"), sq("ln3_g"), sq("ln3_b")], 0)
    gbias = np.concatenate([sq("fox_f_bias"), sq("mlstm_i_bias"), sq("mlstm_f_bias")])[None, :]
    conv_w = sq("mlstm_conv_w")
    conv_b = sq("mlstm_conv_b")
    norm_g = sq("mlstm_norm_g")
    b_adaT = np.ascontiguousarray(sq("b_ada").reshape(144, 128).T)
    ident = np.eye(128, dtype=f32)
    tri = np.triu(np.ones((128, 128), f32))
    sel = np.zeros((128, 128), f32)
    sel[127, :] = 1.0
    kk = np.arange(128)[:, None, None] + 128 * np.arange(4)[None, :, None]
    qq = np.arange(512)[None, None, :]
    masks = np.where(kk <= qq, 0.0, NEG).astype(f32)
    w_in_full = sq("w_in")
    shared = {"w_ada": sq("w_ada"), "b_adaT": b_adaT, "ffn1_w_in": sq("ffn1_w_in"), "ffn1_w_out": sq("ffn1_w_out"),
              "w_out": sq("w_out"), "ffn2_w_in": sq("ffn2_w_in"), "ffn2_w_out": sq("ffn2_w_out"),
              "lnp": lnp, "ident": ident, "tri": tri, "sel127": sel, "masks": masks}
    in_maps = []
    for core in range(8):
        b, j = core // 4, core % 4
        m = dict(shared)
        m["x"] = np.ascontiguousarray(x[b, j * TPC:(j + 1) * TPC, :])
        m["cT"] = np.ascontiguousarray(c[b].reshape(16, 128).T)
        cw = np.stack([conv_w[:, j * 128:(j + 1) * 128].T, conv_w[:, 512 + j * 128:512 + (j + 1) * 128].T], 1)
        m["convw"] = np.ascontiguousarray(cw.reshape(128, 8))
        m["convb"] = np.ascontiguousarray(np.stack([conv_b[j * 128:(j + 1) * 128], conv_b[512 + j * 128:512 + (j + 1) * 128]], 1))
        m["normg"] = np.ascontiguousarray(norm_g[j * 256:(j + 1) * 256][None, :])
        cols = np.concatenate([np.arange(2 * j * 128, 2 * j * 128 + 256), np.arange(1024 + 2 * j * 128, 1024 + 2 * j * 128 + 256),
                               np.arange(3080 + j * 128, 3080 + (j + 1) * 128), np.arange(3592 + j * 128, 3592 + (j + 1) * 128),
                               np.arange(2048 + 2 * j * 128, 2048 + 2 * j * 128 + 256), np.arange(4104 + j * 256, 4104 + (j + 1) * 256),
                               np.arange(5136 + j * 256, 5136 + (j + 1) * 256),
                               np.array([3072 + 2 * j, 3072 + 2 * j + 1, 5128 + j, 5132 + j])])
        m["win_own"] = np.ascontiguousarray(w_in_full[:, cols])
        m["gb_own"] = np.ascontiguousarray(gbias[0, [2 * j, 2 * j + 1, 8 + j, 12 + j]][None, :])
        in_maps.append(m)
    res = run_bass_kernel_spmd(nc, in_maps, core_ids=list(range(8)))
    out = np.empty((2, S, D), f32)
    for core in range(8):
        b, j = core // 4, core % 4
        out[b, j * TPC:(j + 1) * TPC, :] = res.results[core]["out"]
    if os.environ.get("KSTOP", ""):
        _CACHE["dbg"] = [{k: v for k, v in r.items() if k != "out"} for r in res.results]
    return out
```

```python
import contextlib
import os
import numpy as np
import ml_dtypes
import concourse.bass as bass
import concourse.mybir as mybir
from concourse.bass_utils import run_bass_kernel_spmd

F32 = mybir.dt.float32
BF16 = mybir.dt.bfloat16
AF = mybir.ActivationFunctionType
ALU = mybir.AluOpType

D = 2048
S = 16384
TPC = 4096
G = 512
NG = 8
DFF = 5632
NFF = 44
INW = 6160
ALPHA = 2.0 ** 0.25
EPS = 1e-5
NEG = -30000.0
RG = [[0, 1, 2, 3], [4, 5, 6, 7]]


_UC = [0]


def _u(name):
    _UC[0] += 1
    return "%s_u%d" % (name, _UC[0])


class Buf:
    def __init__(self, name):
        self.name = name
        self.w = None
        self.r = []


class Slot:
    def __init__(self, nc, name):
        self.sem = nc.alloc_semaphore(name)
        self.name = name
        self.cnt = 0


class KB:
    def __init__(self, nc):
        self.nc = nc
        self.eng = {'pe': nc.tensor, 'act': nc.scalar, 'dve': nc.vector, 'pool': nc.gpsimd, 'sp': nc.sync}
        self.sem = {e: nc.alloc_semaphore('c_' + e) for e in ('pe', 'act', 'dve', 'pool')}
        self.cnt = {e: 0 for e in self.sem}
        self.seen = {e: {} for e in self.eng}
        self.pend = []
        self.slots = []
        self.nslot = 0

    def slot(self, name):
        s = Slot(self.nc, 'd_%s_%d' % (name, self.nslot))
        self.nslot += 1
        self.slots.append(s)
        return s

    def wait(self, e, tok):
        if tok is None:
            return
        sem, key, val = tok
        if key == e and e == 'pe':
            return
        if self.seen[e].get(key, 0) >= val:
            return
        self.eng[e].wait_ge(sem, val)
        self.seen[e][key] = val

    def _deps(self, e, reads, writes):
        for b in reads:
            self.wait(e, b.w)
        for b in writes:
            self.wait(e, b.w)
            for t in b.r:
                self.wait(e, t)

    def op(self, e, fn, reads=(), writes=(), sig=True):
        self._deps(e, reads, writes)
        ins = fn(self.eng[e])
        if sig:
            self.cnt[e] += 1
            ins.then_inc(self.sem[e], 1)
            tok = (self.sem[e], e, self.cnt[e])
            if e == 'pe':
                for (rb, wb) in self.pend:
                    for b in rb:
                        b.r.append(tok)
                    for b in wb:
                        b.w = tok
                        b.r = []
                self.pend = []
            for b in reads:
                b.r.append(tok)
            for b in writes:
                b.w = tok
                b.r = []
            return tok
        assert e == 'pe'
        self.pend.append((list(reads), list(writes)))
        return None

    def dma(self, q, out, in_, slot, reads=(), writes=(), **kw):
        self._deps(q, reads, writes)
        self.eng[q].dma_start(out=out, in_=in_, **kw).then_inc(slot.sem, 16)
        slot.cnt += 16
        tok = (slot.sem, slot.name, slot.cnt)
        for b in reads:
            b.r.append(tok)
        for b in writes:
            b.w = tok
            b.r = []
        return tok

    def cc(self, ins, outs, slot, reads=(), writes=()):
        self._deps('pool', reads, writes)
        self.nc.gpsimd.collective_compute("AllGather", ALU.bypass, replica_groups=RG,
                                          ins=[ins.opt()], outs=[outs.opt()]).then_inc(slot.sem, 1)
        slot.cnt += 1
        tok = (slot.sem, slot.name, slot.cnt)
        for b in reads:
            b.r.append(tok)
        for b in writes:
            b.w = tok
            b.r = []
        return tok

    def barrier(self, engines=('pe', 'act', 'dve', 'pool', 'sp')):
        toks = [(self.sem[e], e, self.cnt[e]) for e in self.sem if self.cnt[e] > 0]
        toks += [(s.sem, s.name, s.cnt) for s in self.slots if s.cnt > 0]
        for e in engines:
            for t in toks:
                self.wait(e, t)


def build(dbg=False):
    nc = bass.Bass("TRN2", target_bir_lowering=False)
    kb = KB(nc)

    def din(name, shape, dt=F32):
        return nc.dram_tensor(name, shape, dt, kind="ExternalInput").ap()

    def dscr(name, shape, dt):
        return nc.dram_tensor(name, shape, dt, kind="Internal").ap()

    x_in = din("x", [TPC, D])
    cT = din("cT", [128, 16])
    w_ada = din("w_ada", [D, 9 * D])
    b_adaT = din("b_adaT", [128, 144])
    wf = {"w1a": din("ffn1_w_in", [D, 2 * DFF]), "w2a": din("ffn1_w_out", [DFF, D]),
          "wout": din("w_out", [D, D]),
          "w1b": din("ffn2_w_in", [D, 2 * DFF]), "w2b": din("ffn2_w_out", [DFF, D])}
    lnp = din("lnp", [6, D])
    convw = din("convw", [128, 8])
    convb = din("convb", [128, 2])
    normg = din("normg", [1, 256])
    wino_f = din("win_own", [D, 1540])
    gbo = din("gb_own", [1, 4])
    wino = dscr("wino_bf", [D, 1540], BF16)
    ident_d = din("ident", [128, 128])
    tri_d = din("tri", [128, 128])
    sel_d = din("sel127", [128, 128])
    mask_d = din("masks", [128, 4, 512])
    out_d = nc.dram_tensor("out", [TPC, D], F32, kind="ExternalOutput").ap()

    wb = {k: dscr(k + "_bf", list(v.shape), BF16) for k, v in wf.items()}
    x1_d = dscr("x1s", [TPC, D], F32)
    gate_scr = dscr("gate_scr", [48, 128], F32)

    def sb(name, shape, dt):
        return nc.alloc_sbuf_tensor(_u(name), shape, dt)

    ident = sb("ident", [128, 128], F32)
    identb = sb("identb", [128, 128], BF16)
    onesb = sb("onesb", [128, 128], BF16)
    onesf = sb("onesf", [128, 128], F32)
    epsc = sb("epsc", [128, 1], F32)
    tri = sb("tri", [128, 128], F32)
    sel127 = sb("sel127", [128, 128], F32)
    adaT = sb("adaT", [128, 144], F32)
    modsc = sb("modsc", [128, 3, 16], F32)
    B_const = Buf("const")
    B_ada = Buf("ada")

    ps = [nc.alloc_psum_tensor("ps%d" % i, [128, 512], F32) for i in range(8)]
    PB = [Buf("ps%d" % i) for i in range(8)]

    ld = kb.slot("const")
    for t, dsrc in ((ident, ident_d), (tri, tri_d), (sel127, sel_d)):
        kb.dma('sp', t[:], dsrc, ld, writes=[B_const])
    kb.op('dve', lambda e: e.tensor_copy(out=identb[:], in_=ident[:]), reads=[B_const], writes=[B_const])
    kb.op('dve', lambda e: e.memset(onesb[:], 1.0), writes=[B_const])
    kb.op('dve', lambda e: e.memset(onesf[:], 1.0), writes=[B_const])
    kb.op('dve', lambda e: e.memset(epsc[:], EPS), writes=[B_const])

    WB = {k: Buf("wb_" + k) for k in wf}
    wslot = {k: kb.slot("wc_" + k) for k in wf}

    def cast_weight(k, nchunk):
        src, dst = wf[k], wb[k]
        rows = src.shape[0]
        r = rows // nchunk
        for i in range(nchunk):
            kb.dma('pool', dst[i * r:(i + 1) * r, :], src[i * r:(i + 1) * r, :], wslot[k], writes=[],
                   max_dma_last_dim=8192)
        WB[k].w = (wslot[k].sem, wslot[k].name, wslot[k].cnt)

    cast_weight("w1a", 8)
    cast_weight("w2a", 4)
    wf["wino"] = wino_f
    wb["wino"] = wino
    WB["wino"] = Buf("wb_wino")
    wslot["wino"] = kb.slot("wc_wino")
    cast_weight("wino", 2)

    _es = contextlib.ExitStack()
    aw0 = _es.enter_context(nc.sbuf_tensor(_u("ada_w0"), [128, 16, 512], F32))
    aw1 = _es.enter_context(nc.sbuf_tensor(_u("ada_w1"), [128, 16, 512], F32))
    scT = _es.enter_context(nc.sbuf_tensor(_u("scT"), [128, 16], F32))
    badaT = _es.enter_context(nc.sbuf_tensor(_u("badaT"), [128, 144], F32))
    gT = _es.enter_context(nc.sbuf_tensor(_u("gT"), [128, 48], F32))
    gTT = _es.enter_context(nc.sbuf_tensor(_u("gTT"), [48, 128], F32))
    with _es:
        aws = [aw0, aw1]
        AWB = [Buf("aw0"), Buf("aw1")]
        awslot = [kb.slot("aw0"), kb.slot("aw1")]
        B_sc = Buf("scT")
        ld2 = kb.slot("ld2")
        ld3 = kb.slot("ld3")
        kb.dma('sp', scT[:], cT, ld2, writes=[B_sc])
        kb.dma('sp', badaT[:], b_adaT, ld2, writes=[B_sc])
        kb.op('act', lambda e: e.activation(out=scT[:], in_=scT[:], func=AF.Silu), reads=[B_sc], writes=[B_sc])
        w_ada_v = w_ada.rearrange("(kc p) n -> p kc n", p=128)
        for blk in range(36):
            a = aws[blk % 2]
            kb.dma('sp', a[:], w_ada_v[:, :, blk * 512:(blk + 1) * 512], awslot[blk % 2], writes=[AWB[blk % 2]])
            for cc in range(4):
                ch = blk * 4 + cc
                for kc in range(16):
                    last = (kc == 15)
                    kb.op('pe', lambda e, a=a, cc=cc, kc=kc, ch=ch: e.matmul(
                        ps[0][:, ch:ch + 1], lhsT=a[:, kc, cc * 128:(cc + 1) * 128], rhs=scT[:, kc:kc + 1],
                        start=(kc == 0), stop=(kc == 15)),
                        reads=[AWB[blk % 2], B_sc], writes=[PB[0]], sig=(last and cc == 3))
        kb.op('dve', lambda e: e.tensor_tensor(out=adaT[:], in0=ps[0][:, 0:144], in1=badaT[:], op=ALU.add),
              reads=[PB[0], B_sc], writes=[B_ada])
        for k in range(3):
            kb.op('dve', lambda e, k=k: e.tensor_scalar(out=modsc[:, k, :], in0=adaT[:, (3 * k + 1) * 16:(3 * k + 2) * 16],
                                                       scalar1=1.0, scalar2=None, op0=ALU.add),
                  reads=[B_ada], writes=[B_ada])
        for k, coef in enumerate((0.5, 1.0, 0.5)):
            kb.op('dve', lambda e, k=k, coef=coef: e.tensor_scalar(
                out=gT[:, k * 16:(k + 1) * 16], in0=adaT[:, (3 * k + 2) * 16:(3 * k + 3) * 16],
                scalar1=1.0, scalar2=coef, op0=ALU.add, op1=ALU.mult), reads=[B_ada], writes=[B_sc])
        kb.op('pe', lambda e: e.transpose(ps[1][0:48, 0:128], gT[:, 0:48], ident[:]), reads=[B_sc, B_const], writes=[PB[1]])
        kb.op('dve', lambda e: e.tensor_copy(out=gTT[:], in_=ps[1][0:48, 0:128]), reads=[PB[1]], writes=[B_sc])
        B_gscr = Buf("gscr")
        kb.dma('sp', gate_scr, gTT[:], ld3, reads=[B_sc], writes=[B_gscr])
        kb.barrier()

    cast_weight("wout", 2)
    cast_weight("w1b", 8)
    cast_weight("w2b", 4)
    STOP = os.environ.get("KSTOP", "")
    if STOP == "cast":
        kb.barrier()
        return nc

    gate_rows = gate_scr.rearrange("(k c) p -> k (c p)", k=3)

    def ffn_phase(phase):
        _es = contextlib.ExitStack()
        xt = _es.enter_context(nc.sbuf_tensor(_u("xt"), [128, 4, D], F32))
        xn = _es.enter_context(nc.sbuf_tensor(_u("xn"), [128, D], F32))
        hT = _es.enter_context(nc.sbuf_tensor(_u("hT"), [128, 16, G], BF16))
        HT = _es.enter_context(nc.sbuf_tensor(_u("HT"), [128, NFF, G], BF16))
        wA0 = _es.enter_context(nc.sbuf_tensor(_u("wA0"), [128, 16, 512], BF16))
        wA1 = _es.enter_context(nc.sbuf_tensor(_u("wA1"), [128, 16, 512], BF16))
        wA2 = _es.enter_context(nc.sbuf_tensor(_u("wA2"), [128, 16, 512], BF16))
        wB0 = _es.enter_context(nc.sbuf_tensor(_u("wB0"), [128, 1024], BF16))
        wB1 = _es.enter_context(nc.sbuf_tensor(_u("wB1"), [128, 1024], BF16))
        wB2 = _es.enter_context(nc.sbuf_tensor(_u("wB2"), [128, 1024], BF16))
        wB3 = _es.enter_context(nc.sbuf_tensor(_u("wB3"), [128, 1024], BF16))
        tabs = _es.enter_context(nc.sbuf_tensor(_u("tabs"), [128, 3, D], F32))
        tmp = _es.enter_context(nc.sbuf_tensor(_u("tmp"), [128, 2, 512], F32))
        sg = _es.enter_context(nc.sbuf_tensor(_u("sg"), [128, 2, 512], F32))
        stat = _es.enter_context(nc.sbuf_tensor(_u("stat"), [128, 4, 6], F32))
        mv = _es.enter_context(nc.sbuf_tensor(_u("mv"), [128, 4], F32))
        gst = _es.enter_context(nc.sbuf_tensor(_u("gst"), [128, 4, 16], F32))
        gso = _es.enter_context(nc.sbuf_tensor(_u("gso"), [128, 4, 16], F32))
        wg = _es.enter_context(nc.sbuf_tensor(_u("wg"), [128, 16, 16], BF16))
        with _es:
            stage = HT[:, 0:24, :]
            wA = [wA0, wA1, wA2]
            WA = [Buf("wA%d" % i) for i in range(3)]
            wAs = [kb.slot("wA%d" % i) for i in range(3)]
            wBt = [wB0, wB1, wB2, wB3]
            WBb = [Buf("wB%d" % i) for i in range(4)]
            wBs = [kb.slot("wB%d" % i) for i in range(4)]
            st = {"a": 0, "b": 0}
            B_xt = [Buf("xt%d" % i) for i in range(4)]
            B_xn = Buf("xn")
            B_hT = Buf("hT")
            B_HT = [Buf("HT%d" % i) for i in range(NFF)]
            B_tabs = Buf("tabs")
            B_tmp = [Buf("tmp0"), Buf("tmp1")]
            B_sg = [Buf("sg0"), Buf("sg1")]
            B_stat = Buf("stat")
            B_stage = Buf("stage")
            B_gst = Buf("gst")
            B_gso = Buf("gso")
            B_gso = Buf("gso")
            B_wg = Buf("wg")
            xs = kb.slot("xld")
            tbs = kb.slot("tabs")
            sts = kb.slot("store")
            sts_a = kb.slot("store_a")
            sts_b = kb.slot("store_b")
            sts_t = kb.slot("store_t")
            wgs = kb.slot("wg")
            ccs = kb.slot("cc")
            B_x1d = Buf("x1d")

            def loadA(src_ap_fn, wkey):
                i = st["a"] % 3
                st["a"] += 1
                for (o, s_) in src_ap_fn(wA[i]):
                    kb.dma('sp', o, s_, wAs[i], reads=[WB[wkey]], writes=[WA[i]])
                return i

            def load_tabs(rows):
                for i, r in enumerate(rows):
                    kb.dma('sp', tabs[:, i, :], r.partition_broadcast(128), tbs, reads=[B_gscr], writes=[B_tabs])

            def ln_stats(tt, col):
                for q in range(4):
                    kb.op('dve', lambda e, q=q: e.bn_stats(out=stat[:, q, :], in_=xt[:, tt, q * 512:(q + 1) * 512]),
                          reads=[B_xt[tt]], writes=[B_stat])
                kb.op('dve', lambda e: e.bn_aggr(out=mv[:, 0:2], in_=stat[:].rearrange("p a b -> p (a b)")),
                      reads=[B_stat], writes=[B_stat])
                kb.op('act', lambda e: e.activation(out=mv[:, 3:4], in_=mv[:, 1:2], func=AF.Sqrt, bias=epsc[:, 0:1]), reads=[B_stat, B_const], writes=[B_stat])
                kb.op('dve', lambda e: e.reciprocal(out=mv[:, 2:3], in_=mv[:, 3:4]), reads=[B_stat], writes=[B_stat])

            def mod_transpose(k):
                for tt in range(4):
                    ln_stats(tt, 0)
                    kb.op('dve', lambda e, tt=tt: e.tensor_scalar(out=xn[:], in0=xt[:, tt, :], scalar1=mv[:, 0:1],
                                                                 scalar2=mv[:, 2:3], op0=ALU.subtract, op1=ALU.mult),
                          reads=[B_xt[tt], B_stat], writes=[B_xn])
                    for q in range(4):
                        pi = 4 + (q % 2)
                        for c4 in range(4):
                            kc = q * 4 + c4
                            kb.op('pe', lambda e, kc=kc, c4=c4, pi=pi: e.transpose(
                                ps[pi][:, c4 * 128:(c4 + 1) * 128], xn[:, kc * 128:(kc + 1) * 128], ident[:]),
                                reads=[B_xn, B_const], writes=[PB[pi]], sig=(c4 == 3))
                        for c4 in range(4):
                            kc = q * 4 + c4
                            kb.op('act', lambda e, kc=kc, c4=c4, pi=pi, tt=tt: e.activation(
                                out=hT[:, kc, tt * 128:(tt + 1) * 128], in_=ps[pi][:, c4 * 128:(c4 + 1) * 128],
                                func=AF.Identity, scale=modsc[:, k, kc:kc + 1],
                                bias=adaT[:, (3 * k) * 16 + kc:(3 * k) * 16 + kc + 1]),
                                reads=[PB[pi], B_ada], writes=[B_hT])

            def ffn(w1k, w2k, k, gt_row, lng_row, lnb_row):
                w1 = wb[w1k].rearrange("(kc p) n -> p kc n", p=128)
                w2 = wb[w2k]
                mod_transpose(k)
                load_tabs([gate_rows[gt_row:gt_row + 1, :], lnp[lng_row:lng_row + 1, :], lnp[lnb_row:lnb_row + 1, :]])
                def srcs(i2):
                    return lambda t: [(t[:, :, 0:256], w1[:, :, i2 * 256:(i2 + 1) * 256]),
                                      (t[:, :, 256:512], w1[:, :, DFF + i2 * 256:DFF + (i2 + 1) * 256])]
                pre = [loadA(srcs(0), w1k), loadA(srcs(1), w1k)]
                for i2 in range(22):
                    wi = pre[i2]
                    if i2 + 2 < 22:
                        pre.append(loadA(srcs(i2 + 2), w1k))
                    for c in range(2):
                        i = i2 * 2 + c
                        pg, pu = (0, 1) if (i % 2 == 0) else (2, 3)
                        for kc in range(16):
                            kb.op('pe', lambda e, wi=wi, kc=kc, c=c, pg=pg: e.matmul(
                                ps[pg][:, :], lhsT=wA[wi][:, kc, c * 128:(c + 1) * 128], rhs=hT[:, kc, :],
                                start=(kc == 0), stop=(kc == 15)), reads=[WA[wi], B_hT], writes=[PB[pg]], sig=(kc == 15))
                        for kc in range(16):
                            kb.op('pe', lambda e, wi=wi, kc=kc, c=c, pu=pu: e.matmul(
                                ps[pu][:, :], lhsT=wA[wi][:, kc, 256 + c * 128:256 + (c + 1) * 128], rhs=hT[:, kc, :],
                                start=(kc == 0), stop=(kc == 15)), reads=[WA[wi], B_hT], writes=[PB[pu]], sig=(kc == 15))
                        kb.op('act', lambda e, pg=pg, i=i: e.activation(out=sg[:, i % 2, :], in_=ps[pg][:, :], func=AF.Silu),
                              reads=[PB[pg]], writes=[B_sg[i % 2]])
                        kb.op('dve', lambda e, pu=pu, i=i: e.tensor_tensor(out=HT[:, i, :], in0=sg[:, i % 2, :], in1=ps[pu][:, :],
                                                                          op=ALU.mult),
                              reads=[PB[pu], B_sg[i % 2]], writes=[B_HT[i]] + ([B_stage] if i < 24 else []))
                def loadB(dh, i):
                    s_ = st["b"] % 4
                    st["b"] += 1
                    kb.dma('sp', wBt[s_][:], w2[i * 128:(i + 1) * 128, dh * 1024:(dh + 1) * 1024], wBs[s_],
                           reads=[WB[w2k]], writes=[WBb[s_]])
                    return s_
                seq = [(dh, i) for dh in range(2) for i in range(NFF)]
                preb = [loadB(*seq[0]), loadB(*seq[1]), loadB(*seq[2])]
                for n, (dh, i) in enumerate(seq):
                    s_ = preb[n]
                    if n + 3 < len(seq):
                        preb.append(loadB(*seq[n + 3]))
                    for tt in range(4):
                        for dgi in range(2):
                            pi = tt * 2 + dgi
                            kb.op('pe', lambda e, s_=s_, tt=tt, dgi=dgi, pi=pi, i=i: e.matmul(
                                ps[pi][:, :], lhsT=HT[:, i, tt * 128:(tt + 1) * 128], rhs=wBt[s_][:, dgi * 512:(dgi + 1) * 512],
                                start=(i == 0), stop=(i == NFF - 1)),
                                reads=[WBb[s_], B_HT[i]], writes=[PB[pi]], sig=(i == NFF - 1 or (tt == 3 and dgi == 1)))
                    if i == NFF - 1:
                        for tt in range(4):
                            for dgi in range(2):
                                pi = tt * 2 + dgi
                                c0 = dh * 1024 + dgi * 512
                                residual(tt, pi, c0)
                for tt in range(4):
                    ln_affine(tt)

            def residual(tt, pi, c0):
                j = pi % 2
                kb.op('dve', lambda e: e.tensor_tensor(out=tmp[:, j, :], in0=ps[pi][:, :], in1=tabs[:, 0, c0:c0 + 512], op=ALU.mult),
                      reads=[PB[pi], B_tabs], writes=[B_tmp[j]])
                kb.op('dve', lambda e: e.scalar_tensor_tensor(out=xt[:, tt, c0:c0 + 512], in0=xt[:, tt, c0:c0 + 512], scalar=ALPHA,
                                                             in1=tmp[:, j, :], op0=ALU.mult, op1=ALU.add),
                      reads=[B_tmp[j]], writes=[B_xt[tt]])

            def ln_affine(tt):
                ln_stats(tt, 0)
                kb.op('dve', lambda e: e.tensor_scalar(out=xt[:, tt, :], in0=xt[:, tt, :], scalar1=mv[:, 0:1], scalar2=mv[:, 2:3],
                                                       op0=ALU.subtract, op1=ALU.mult), reads=[B_stat], writes=[B_xt[tt]])
                kb.op('dve', lambda e: e.tensor_tensor(out=xt[:, tt, :], in0=xt[:, tt, :], in1=tabs[:, 1, :], op=ALU.mult),
                      reads=[B_tabs], writes=[B_xt[tt]])
                kb.op('dve', lambda e: e.tensor_tensor(out=xt[:, tt, :], in0=xt[:, tt, :], in1=tabs[:, 2, :], op=ALU.add),
                      reads=[B_tabs], writes=[B_xt[tt]])


            def send_h2(g):
                mod_transpose(1)
                kb.dma('pool', H2s[g].rearrange("kc p t -> p kc t"), hT[:], sts_a, reads=[B_hT], writes=[B_h2s[g]])
                for kc in range(16):
                    kb.cc(H2s[g][kc], H2g[g][kc], ccs, reads=[B_h2s[g]], writes=[B_h2g[g]])

            def attn_out(g):
                woutv = wb["wout"].rearrange("(kc p) n -> p kc n", p=128)
                for c in range(4):
                    pass
                for rb in range(4):
                    kb.dma('sp', hT[:].rearrange("p (r rb) t -> p rb r t", rb=4)[:, rb], myY[g, rb].rearrange("(r p) t -> p r t", p=128), xs,
                           reads=[B_yg], writes=[B_hT])
                load_tabs([gate_rows[1:2, :], lnp[2:3, :], lnp[3:4, :]])
                fch = []
                for r in range(4):
                    fch += [2 * r, 2 * r + 1, 8 + 2 * r, 9 + 2 * r]
                for dg in range(4):
                    wi = loadA(lambda t, dg=dg: [(t[:], woutv[:, :, dg * 512:(dg + 1) * 512])], "wout")
                    for tt in range(4):
                        pi = 4 * (dg % 2) + tt
                        for ch in range(16):
                            kb.op('pe', lambda e, wi=wi, ch=ch, tt=tt, pi=pi: e.matmul(
                                ps[pi][:, :], lhsT=hT[:, ch, tt * 128:(tt + 1) * 128], rhs=wA[wi][:, fch[ch], :],
                                start=(ch == 0), stop=(ch == 15)), reads=[WA[wi], B_hT], writes=[PB[pi]], sig=(ch == 15))
                        residual(tt, pi, dg * 512)
                for tt in range(4):
                    ln_affine(tt)

            for g in range(NG):
                rows = slice(g * G, (g + 1) * G)
                if phase == 1:
                    kb.dma('sp', xt[:], x_in[rows, :].rearrange("(tt p) d -> p tt d", p=128), xs, writes=B_xt)
                    ffn("w1a", "w2a", 0, 0, 0, 1)
                    if STOP == "g0ffn":
                        kb.dma('pool', out_d[rows, :].rearrange("(tt p) d -> p tt d", p=128), xt[:], sts, reads=B_xt, writes=[B_outd])
                        break
                    kb.dma('pool', x1_d[rows, :].rearrange("(tt p) d -> p tt d", p=128), xt[:], sts, reads=B_xt, writes=[B_x1d])
                    if STOP == "g0":
                        kb.dma('pool', out_d[rows, :].rearrange("(tt p) d -> p tt d", p=128), xt[:], sts, reads=B_xt, writes=[B_outd])
                    send_h2(g)
                    if STOP == "g0":
                        break
                else:
                    kb.dma('sp', xt[:], x1_d[rows, :].rearrange("(tt p) d -> p tt d", p=128), xs, reads=[B_x1d], writes=B_xt)
                    attn_out(g)
                    ffn("w1b", "w2b", 2, 2, 4, 5)
                    kb.dma('pool', out_d[rows, :].rearrange("(tt p) d -> p tt d", p=128), xt[:], sts, reads=B_xt, writes=[B_outd])
            kb.barrier()

    jsp = nc.sync.partition_id() % 4
    H2s = [dscr("h2s%d" % g, [16, 128, G], BF16) for g in range(NG)]
    H2g = [dscr("h2g%d" % g, [16, 512, G], BF16) for g in range(NG)]
    B_h2s = [Buf("h2s%d" % g) for g in range(NG)]
    B_h2g = [Buf("h2g%d" % g) for g in range(NG)]
    FM_d = dscr("fm_d", [6, 128, S], BF16)
    TM_d = dscr("tm_d", [S, 768], BF16)
    GT_d = dscr("gt_d", [S, 4], F32)
    YS = dscr("ys_p", [32, 4, 128, G], BF16)
    YG = dscr("yg_p", [32, 4, 512, G], BF16)
    myY = dscr("myY", [8, 4, 512, G], BF16)
    B_fm = Buf("fm")
    B_yg = Buf("yg")
    B_outd = Buf("outd")

    ffn_phase(1)
    if STOP in ("g0ffn", "g0", "p1a"):
        kb.barrier()
        return nc
    zproj_own(nc, kb, ps, PB, H2g, B_h2g, wino, gbo, FM_d, TM_d, GT_d, B_fm)
    if STOP == "p1":
        dF = nc.dram_tensor("dbgF", [6, 128, S], BF16, kind="ExternalOutput").ap()
        dT = nc.dram_tensor("dbgT", [S, 768], BF16, kind="ExternalOutput").ap()
        d32 = nc.dram_tensor("dbg32", [S, 4], F32, kind="ExternalOutput").ap()
        dx1 = nc.dram_tensor("dbgx1", [TPC, D], F32, kind="ExternalOutput").ap()
        dsl = kb.slot("dbg")
        kb.dma('sp', dF, FM_d, dsl, reads=[B_fm])
        kb.dma('sp', dT, TM_d, dsl, reads=[B_fm])
        kb.dma('sp', d32, GT_d, dsl, reads=[B_fm])
        kb.dma('sp', dx1, x1_d, dsl)
        kb.barrier()
        return nc
    attention(nc, kb, ps, PB, FM_d, TM_d, GT_d, B_fm, YS, YG, B_yg, ident, identb, onesb, onesf, tri, sel127,
              B_const, mask_d, convw, convb, normg, epsc)
    if STOP == "attn":
        dY = nc.dram_tensor("dbgY", [32, 4, 128, G], BF16, kind="ExternalOutput").ap()
        dsl = kb.slot("dbg")
        kb.dma('sp', dY, YS, dsl)
        kb.barrier()
        return nc
    ysl = kb.slot("myY")
    for t in range(8):
        kb.dma('sp', myY[t], YG[jsp * 8 + t], ysl, reads=[B_yg], writes=[B_yg])
    ffn_phase(3)
    kb.barrier()
    return nc


def zproj_own(nc, kb, ps, PB, H2g, B_h2g, wino, gbo, FM_d, TM_d, GT_d, B_fm):
    _es = contextlib.ExitStack()
    wo = _es.enter_context(nc.sbuf_tensor(_u("wo"), [128, 16, 1540], BF16))
    h2a = _es.enter_context(nc.sbuf_tensor(_u("h2a"), [128, 16, G], BF16))
    h2b = _es.enter_context(nc.sbuf_tensor(_u("h2b"), [128, 16, G], BF16))
    stFa = _es.enter_context(nc.sbuf_tensor(_u("stFa"), [128, 6, G], BF16))
    stFb = _es.enter_context(nc.sbuf_tensor(_u("stFb"), [128, 6, G], BF16))
    stTa = _es.enter_context(nc.sbuf_tensor(_u("stTa"), [128, 4, 768], BF16))
    stTb = _es.enter_context(nc.sbuf_tensor(_u("stTb"), [128, 4, 768], BF16))
    stGa = _es.enter_context(nc.sbuf_tensor(_u("stGa"), [128, 4, 4], F32))
    stGb = _es.enter_context(nc.sbuf_tensor(_u("stGb"), [128, 4, 4], F32))
    gb = _es.enter_context(nc.sbuf_tensor(_u("gbo"), [128, 4], F32))
    with _es:
        h2 = [h2a, h2b]
        stF = [stFa, stFb]
        stT = [stTa, stTb]
        stG = [stGa, stGb]
        B_h2 = [Buf("h2a"), Buf("h2b")]
        B_stF = [Buf("stFa"), Buf("stFb")]
        B_stT = [Buf("stTa"), Buf("stTb")]
        B_stG = [Buf("stGa"), Buf("stGb")]
        B_wo = Buf("wo")
        wsl = kb.slot("wo")
        hsl = [kb.slot("h2a"), kb.slot("h2b")]
        ssl = [kb.slot("zsta"), kb.slot("zstb")]
        kb.dma('sp', wo[:], wino.rearrange("(kc p) n -> p kc n", p=128), wsl, reads=[], writes=[B_wo])
        kb.dma('sp', gb[:], gbo.partition_broadcast(128), wsl, writes=[B_wo])
        FMv = FM_d.rearrange("c p t -> p c t")

        def load(tg):
            r, g = tg // 8, tg % 8
            b = tg % 2
            kb.dma('sp', h2[b][:], H2g[g][:, r * 128:(r + 1) * 128, :].rearrange("kc p t -> p kc t"), hsl[b],
                   reads=[B_h2g[g]], writes=[B_h2[b]])
        load(0)
        _ZS = os.environ.get('ZPSKIP', '')
        NB_ = int(os.environ.get('ZPNB', '260'))
        for tg in range(int(os.environ.get('ZPN', '32'))):
            b = tg % 2
            tok0 = tg * G
            if tg + 1 < 32:
                load(tg + 1)
            for c in range(0 if 'fm' in _ZS else 6):
                pi = c % 4
                for kc in range(16):
                    kb.op('pe', lambda e, c=c, kc=kc, pi=pi, b=b: e.matmul(ps[pi][:, :], lhsT=wo[:, kc, c * 128:(c + 1) * 128], rhs=h2[b][:, kc, :],
                                                                      start=(kc == 0), stop=(kc == 15)),
                          reads=[B_wo, B_h2[b]], writes=[PB[pi]], sig=(kc == 15))
                if c % 2 == 0:
                    kb.op('act', lambda e, c=c, pi=pi, b=b: e.activation(out=stF[b][:, c, :], in_=ps[pi][:, :], func=AF.Copy),
                          reads=[PB[pi]], writes=[B_stF[b]])
                else:
                    kb.op('dve', lambda e, c=c, pi=pi, b=b: e.tensor_copy(out=stF[b][:, c, :], in_=ps[pi][:, :]),
                          reads=[PB[pi]], writes=[B_stF[b]])
            if 'fm' not in _ZS:
                kb.dma('pool', FMv[:, :, tok0:tok0 + G], stF[b][:], ssl[b], reads=[B_stF[b]], writes=[B_fm])
            for tt in range(0 if 'tm' in _ZS else 4):
                pa, pb_ = (4, 6) if tt % 2 == 0 else (5, 7)
                for kc in range(16):
                    kb.op('pe', lambda e, kc=kc, tt=tt, pa=pa, b=b: e.matmul(ps[pa][:, :], lhsT=h2[b][:, kc, tt * 128:(tt + 1) * 128], rhs=wo[:, kc, 768:1280],
                                                                        start=(kc == 0), stop=(kc == 15)),
                          reads=[B_wo, B_h2[b]], writes=[PB[pa]], sig=(kc == 15))
                for kc in range(16):
                    kb.op('pe', lambda e, kc=kc, tt=tt, pb_=pb_, b=b: e.matmul(ps[pb_][:, 0:NB_], lhsT=h2[b][:, kc, tt * 128:(tt + 1) * 128], rhs=wo[:, kc, 1280:1280 + NB_],
                                                                          start=(kc == 0), stop=(kc == 15)),
                          reads=[B_wo, B_h2[b]], writes=[PB[pb_]], sig=(kc == 15))
                kb.op('dve', lambda e, tt=tt, pa=pa, b=b: e.tensor_copy(out=stT[b][:, tt, 0:512], in_=ps[pa][:, :]), reads=[PB[pa]], writes=[B_stT[b]])
                if 'sig' not in _ZS:
                    kb.op('act', lambda e, tt=tt, pb_=pb_, b=b: e.activation(out=stT[b][:, tt, 512:768], in_=ps[pb_][:, 0:256], func=AF.Sigmoid),
                          reads=[PB[pb_]], writes=[B_stT[b]])
                if 'gadd' not in _ZS:
                    kb.op('act', lambda e, tt=tt, pb_=pb_, b=b: e.activation(out=stG[b][:, tt, :], in_=ps[pb_][:, 256:260], func=AF.Copy),
                          reads=[PB[pb_]], writes=[B_stG[b]])
                    kb.op('dve', lambda e, tt=tt, b=b: e.tensor_tensor(out=stG[b][:, tt, :], in0=stG[b][:, tt, :], in1=gb[:], op=ALU.add),
                          reads=[B_wo], writes=[B_stG[b]])
            for (a_, b_) in (() if ('tm' in _ZS or 'ls' in _ZS) else ((0, 2), (3, 4))):
                kb.op('act', lambda e, a_=a_, b_=b_, b=b: e.activation(out=stG[b][:, :, a_:b_], in_=stG[b][:, :, a_:b_], func=AF.Exp, scale=-1.0),
                      reads=[B_stG[b]], writes=[B_stG[b]])
                kb.op('act', lambda e, a_=a_, b_=b_, b=b: e.activation(out=stG[b][:, :, a_:b_], in_=stG[b][:, :, a_:b_], func=AF.Ln, bias=1.0),
                      reads=[B_stG[b]], writes=[B_stG[b]])
                kb.op('dve', lambda e, a_=a_, b_=b_, b=b: e.tensor_scalar(out=stG[b][:, :, a_:b_], in0=stG[b][:, :, a_:b_], scalar1=-1.0, scalar2=None,
                                                                         op0=ALU.mult), reads=[B_stG[b]], writes=[B_stG[b]])
            if 'tm' not in _ZS and 'tst' not in _ZS:
              kb.dma('pool', TM_d[tok0:tok0 + G, :].rearrange("(tt p) c -> p tt c", p=128), stT[b][:], ssl[b], reads=[B_stT[b]], writes=[B_fm])
            if 'tm' not in _ZS and 'gst' not in _ZS:
              kb.dma('pool', GT_d[tok0:tok0 + G, :].rearrange("(tt p) c -> p tt c", p=128), stG[b][:], ssl[b], reads=[B_stG[b]], writes=[B_fm])
        kb.barrier()


def attention(nc, kb, ps, PB, FM_d, TM_d, GT_d, B_fm, YS, YG, B_yg, ident, identb, onesb, onesf, tri, sel127,
              B_const, mask_d, convw_d, convb_d, normg_d, epsc):
    TMv = TM_d.rearrange("(t p) c -> p t c", p=128)
    GTv = GT_d.rearrange("(t p) c -> p t c", p=128)
    B_ysp = Buf("ysp")
    SCALE = 128.0 ** -0.5
    NT = 128
    ccs = kb.slot("ccy")
    sts = kb.slot("ysto")

    _es = contextlib.ExitStack()
    KT = _es.enter_context(nc.sbuf_tensor(_u("KT"), [128, S], BF16))
    QT = _es.enter_context(nc.sbuf_tensor(_u("QT"), [128, S], BF16))
    V = _es.enter_context(nc.sbuf_tensor(_u("V"), [128, NT, 128], BF16))
    lf = _es.enter_context(nc.sbuf_tensor(_u("lf"), [128, NT, 2], F32))
    Fm = _es.enter_context(nc.sbuf_tensor(_u("Fm"), [128, 2, NT], F32))
    scan = _es.enter_context(nc.sbuf_tensor(_u("scan"), [128, 2, 2 * NT], F32))
    offb = _es.enter_context(nc.sbuf_tensor(_u("offb"), [128, 2 * NT], F32))
    negF = _es.enter_context(nc.sbuf_tensor(_u("negF"), [128, 2, NT], F32))
    masks = _es.enter_context(nc.sbuf_tensor(_u("masks"), [128, 4, 512], F32))
    lg = _es.enter_context(nc.sbuf_tensor(_u("lg"), [128, 3, 512], F32))
    PT = _es.enter_context(nc.sbuf_tensor(_u("PT"), [128, 3, 512], BF16))
    fq = _es.enter_context(nc.sbuf_tensor(_u("fq"), [128, 2, 512], F32))
    rden = _es.enter_context(nc.sbuf_tensor(_u("rden"), [128, 512], F32))
    yst = _es.enter_context(nc.sbuf_tensor(_u("yst"), [128, 2, 512], BF16))
    with _es:
        lds = kb.slot("fxld")
        mks = kb.slot("fxmask")
        B_KT, B_QT, B_V, B_lf, B_F, B_scan, B_mask = Buf("KT"), Buf("QT"), Buf("V"), Buf("lf"), Buf("F"), Buf("scan"), Buf("mask")
        B_lg = [Buf("lg%d" % i) for i in range(3)]
        B_PT = [Buf("PT%d" % i) for i in range(3)]
        B_fq = [Buf("fq0"), Buf("fq1")]
        B_rden = Buf("rden")
        B_yst = [Buf("yst0"), Buf("yst1")]
        mks0 = kb.slot("fxmask0")
        kb.dma('sp', masks[:], mask_d, mks0, writes=[B_mask])
        for i8 in range(8):
            kb.dma('sp', lf[:, i8 * 16:(i8 + 1) * 16, :], GTv[:, i8 * 16:(i8 + 1) * 16, 0:2], mks, reads=[B_fm], writes=[B_lf])
        lfv = lf[:].rearrange("p t h -> p h t")
        kb.op('dve', lambda e: e.tensor_copy(out=scan[:, 0, :].rearrange("p (h t) -> p h t", h=2), in_=lfv), reads=[B_lf], writes=[B_scan])
        kb.op('pe', lambda e: e.matmul(ps[0][:, 0:256], lhsT=tri[:], rhs=scan[:, 0, :], start=True, stop=True),
              reads=[B_scan, B_const], writes=[PB[0]])
        kb.op('dve', lambda e: e.tensor_copy(out=Fm[:].rearrange("p h t -> p (h t)"), in_=ps[0][:, 0:256]), reads=[PB[0]], writes=[B_F])
        kb.op('pe', lambda e: e.matmul(ps[1][:, 0:256], lhsT=sel127[:], rhs=Fm[:].rearrange("p h t -> p (h t)"), start=True, stop=True),
              reads=[B_F, B_const], writes=[PB[1]])
        kb.op('dve', lambda e: e.tensor_copy(out=scan[:, 0, :], in_=ps[1][:, 0:256]), reads=[PB[1]], writes=[B_scan])
        kb.op('dve', lambda e: e.tensor_copy(out=offb[:], in_=scan[:, 0, :]), reads=[B_scan], writes=[B_scan])
        cur = 0
        sh = 1
        while sh < NT:
            nxt = 1 - cur
            sc3 = scan[:, cur, :].rearrange("p (h t) -> p h t", h=2)
            sn3 = scan[:, nxt, :].rearrange("p (h t) -> p h t", h=2)
            kb.op('dve', lambda e, sc3=sc3, sn3=sn3, sh=sh: e.tensor_copy(out=sn3[:, :, 0:sh], in_=sc3[:, :, 0:sh]),
                  reads=[B_scan], writes=[B_scan])
            kb.op('dve', lambda e, sc3=sc3, sn3=sn3, sh=sh: e.tensor_tensor(out=sn3[:, :, sh:NT], in0=sc3[:, :, sh:NT], in1=sc3[:, :, 0:NT - sh],
                                                                          op=ALU.add), reads=[B_scan], writes=[B_scan])
            cur = nxt
            sh *= 2
        kb.op('dve', lambda e: e.tensor_tensor(out=offb[:], in0=scan[:, cur, :], in1=offb[:], op=ALU.subtract), reads=[B_scan], writes=[B_scan])
        kb.op('dve', lambda e: e.tensor_tensor(out=Fm[:].rearrange("p h t -> p (h t)"), in0=Fm[:].rearrange("p h t -> p (h t)"), in1=offb[:],
                                               op=ALU.add), reads=[B_scan], writes=[B_F])
        kb.op('dve', lambda e: e.tensor_scalar(out=negF[:].rearrange("p h t -> p (h t)"), in0=Fm[:].rearrange("p h t -> p (h t)"),
                                               scalar1=-1.0, scalar2=None, op0=ALU.mult), reads=[B_F], writes=[B_F])

        for hh in range(2):
            kb.dma('sp', QT[:], FM_d[hh], lds, reads=[B_fm], writes=[B_QT])
            kb.dma('sp', KT[:], FM_d[2 + hh], lds, reads=[B_fm], writes=[B_KT])
            for i8 in range(8):
                kb.dma('sp', V[:, i8 * 16:(i8 + 1) * 16, :], TMv[:, i8 * 16:(i8 + 1) * 16, hh * 128:(hh + 1) * 128], lds, reads=[B_fm], writes=[B_V])
            pairs = [(qg, kt) for qg in range(32) for kt in range(4 * qg + 4)]
            LA = 3

            def s_stage(n):
                qg, kt = pairs[n]
                fb = qg % 2
                si = n % 3
                if kt == 0:
                    for c4 in range(4):
                        kb.op('pe', lambda e, c4=c4, qg=qg: e.matmul(
                            ps[7][:, c4 * 128:(c4 + 1) * 128], lhsT=Fm[:, hh, 4 * qg + c4:4 * qg + c4 + 1].to_broadcast([128, 128]),
                            rhs=ident[:], start=True, stop=True), reads=[B_F, B_const], writes=[PB[7]], sig=(c4 == 3))
                    kb.op('act', lambda e, fb=fb: e.activation(out=fq[:, fb, :], in_=ps[7][:, :], func=AF.Copy), reads=[PB[7]], writes=[B_fq[fb]])
                kb.op('pe', lambda e, kt=kt, qg=qg, si=si: e.matmul(ps[si][:, :], lhsT=KT[:, kt * 128:(kt + 1) * 128],
                                                                  rhs=QT[:, qg * 512:(qg + 1) * 512], start=True, stop=True),
                      reads=[B_KT, B_QT], writes=[PB[si]])
                kb.op('dve', lambda e, si=si, fb=fb: e.scalar_tensor_tensor(out=lg[:, si, :], in0=ps[si][:, :], scalar=SCALE, in1=fq[:, fb, :],
                                                                          op0=ALU.mult, op1=ALU.add),
                      reads=[PB[si], B_fq[fb]], writes=[B_lg[si]])
                if kt >= 4 * qg:
                    dd = kt - 4 * qg
                    kb.op('dve', lambda e, si=si, dd=dd: e.tensor_tensor(out=lg[:, si, :], in0=lg[:, si, :], in1=masks[:, dd, :], op=ALU.add),
                          reads=[B_mask], writes=[B_lg[si]])
                kb.op('act', lambda e, si=si, kt=kt: e.activation(out=PT[:, si, :], in_=lg[:, si, :], func=AF.Exp, bias=negF[:, hh, kt:kt + 1]),
                      reads=[B_lg[si], B_F], writes=[B_PT[si]])

            def p_stage(n):
                qg, kt = pairs[n]
                fb = qg % 2
                si = n % 3
                nk = 4 * qg + 4
                po, pd = (3, 4) if qg % 2 == 0 else (5, 6)
                last = (kt == nk - 1)
                kb.op('pe', lambda e, si=si, kt=kt, po=po, last=last: e.matmul(ps[po][:, :], lhsT=V[:, kt, :], rhs=PT[:, si, :],
                                                                            start=(kt == 0), stop=last),
                      reads=[B_V, B_PT[si]], writes=[PB[po]], sig=False)
                kb.op('pe', lambda e, si=si, kt=kt, pd=pd, last=last: e.matmul(ps[pd][:, :], lhsT=onesb[:], rhs=PT[:, si, :],
                                                                            start=(kt == 0), stop=last),
                      reads=[B_const, B_PT[si]], writes=[PB[pd]], sig=True)
                if last:
                    kb.op('dve', lambda e, pd=pd: e.reciprocal(out=rden[:], in_=ps[pd][:, :]), reads=[PB[pd]], writes=[B_rden])
                    kb.op('dve', lambda e, po=po, fb=fb: e.tensor_tensor(out=yst[:, fb, :], in0=ps[po][:, :], in1=rden[:], op=ALU.mult),
                          reads=[PB[po], B_rden], writes=[B_yst[fb]])
                    kb.dma('pool', YS[qg, hh], yst[:, fb, :], sts, reads=[B_yst[fb]], writes=[B_ysp])
                    kb.cc(YS[qg, hh], YG[qg, hh], ccs, reads=[B_ysp], writes=[B_yg])

            for n in range(LA):
                s_stage(n)
            for n in range(len(pairs)):
                p_stage(n)
                if n + LA < len(pairs):
                    s_stage(n + LA)
        kb.barrier()

    QW = 4096
    _es = contextlib.ExitStack()
    zq = _es.enter_context(nc.sbuf_tensor(_u("zq"), [128, QW + 3], BF16))
    zk = _es.enter_context(nc.sbuf_tensor(_u("zk"), [128, QW + 3], BF16))
    qT = _es.enter_context(nc.sbuf_tensor(_u("qT"), [128, QW], BF16))
    kT = _es.enter_context(nc.sbuf_tensor(_u("kT"), [128, QW], BF16))
    v1 = _es.enter_context(nc.sbuf_tensor(_u("v1"), [128, 32, 257], BF16))
    og = _es.enter_context(nc.sbuf_tensor(_u("og"), [128, 32, 256], BF16))
    gi = _es.enter_context(nc.sbuf_tensor(_u("gi"), [128, 32, 2], F32))
    gsm = _es.enter_context(nc.sbuf_tensor(_u("gsm"), [128, 8, 32], F32))
    cw = _es.enter_context(nc.sbuf_tensor(_u("cw"), [128, 8], F32))
    cb = _es.enter_context(nc.sbuf_tensor(_u("cb"), [128, 2], F32))
    ngt = _es.enter_context(nc.sbuf_tensor(_u("ngt"), [128, 256], F32))
    acc = _es.enter_context(nc.sbuf_tensor(_u("acc"), [128, 512], F32))
    trim = _es.enter_context(nc.sbuf_tensor(_u("trim"), [128, 128], F32))
    e1 = _es.enter_context(nc.sbuf_tensor(_u("e1"), [128, 128], F32))
    DT = _es.enter_context(nc.sbuf_tensor(_u("DT"), [128, 128], F32))
    SD = _es.enter_context(nc.sbuf_tensor(_u("SD"), [128, 128], BF16))
    qc = _es.enter_context(nc.sbuf_tensor(_u("qc"), [128, 257], F32))
    nd = _es.enter_context(nc.sbuf_tensor(_u("nd"), [128, 257], F32))
    sm = _es.enter_context(nc.sbuf_tensor(_u("sm"), [128, 8], F32))
    hh_ = _es.enter_context(nc.sbuf_tensor(_u("hh_"), [128, 256], F32))
    sq = _es.enter_context(nc.sbuf_tensor(_u("sq"), [128, 256], F32))
    go = _es.enter_context(nc.sbuf_tensor(_u("go"), [128, 256], F32))
    yy = _es.enter_context(nc.sbuf_tensor(_u("yy"), [128, 256], F32))
    kw = _es.enter_context(nc.sbuf_tensor(_u("kw"), [128, 128], BF16))
    Cs = _es.enter_context(nc.sbuf_tensor(_u("Cs"), [128, 257], F32))
    Cb = _es.enter_context(nc.sbuf_tensor(_u("Cb"), [128, 257], BF16))
    yTs = _es.enter_context(nc.sbuf_tensor(_u("yTs"), [128, 2, QW], BF16))
    with _es:
        lds = kb.slot("mlld")
        B_z, B_qk, B_v1, B_og, B_gi, B_gsm, B_c = Buf("z"), Buf("qk"), Buf("v1"), Buf("og"), Buf("gi"), Buf("gsm"), Buf("mc")
        B_acc, B_e1, B_DT, B_SD, B_qc, B_nd, B_sm, B_h, B_sq, B_go, B_yy, B_kw, B_C, B_Cb, B_yT = [Buf(n) for n in
            ("acc", "e1", "DT", "SD", "qc", "nd", "sm", "h", "sq", "go", "yy", "kw", "C", "Cb", "yT")]
        kb.dma('sp', cw[:], convw_d, lds, writes=[B_c])
        kb.dma('sp', cb[:], convb_d, lds, writes=[B_c])
        kb.dma('sp', ngt[:], normg_d.partition_broadcast(128), lds, writes=[B_c])
        kb.dma('sp', trim[:], mask_d[:, 0, 0:128], lds, writes=[B_c])
        kb.op('dve', lambda e: e.memset(Cs[:], 0.0), writes=[B_C])
        kb.op('dve', lambda e: e.memset(Cb[:], 0.0), writes=[B_Cb])
        kb.op('dve', lambda e: e.memset(zq[:, 0:3], 0.0), writes=[B_z])
        kb.op('dve', lambda e: e.memset(zk[:, 0:3], 0.0), writes=[B_z])
        for c in range(4):
            if c > 0:
                kb.op('dve', lambda e: e.tensor_copy(out=zq[:, 0:3], in_=zq[:, QW:QW + 3]), reads=[B_z], writes=[B_z])
                kb.op('dve', lambda e: e.tensor_copy(out=zk[:, 0:3], in_=zk[:, QW:QW + 3]), reads=[B_z], writes=[B_z])
            kb.dma('sp', zq[:, 3:3 + QW], FM_d[4][:, c * QW:(c + 1) * QW], lds, reads=[B_fm], writes=[B_z])
            kb.dma('sp', zk[:, 3:3 + QW], FM_d[5][:, c * QW:(c + 1) * QW], lds, reads=[B_fm], writes=[B_z])
            for i4 in range(4):
                tsl = slice(c * 32 + i4 * 8, c * 32 + (i4 + 1) * 8)
                kb.dma('sp', v1[:, i4 * 8:(i4 + 1) * 8, 0:256], TMv[:, tsl, 256:512], lds, reads=[B_fm], writes=[B_v1])
                kb.dma('sp', og[:, i4 * 8:(i4 + 1) * 8, :], TMv[:, tsl, 512:768], lds, reads=[B_fm], writes=[B_og])
                kb.dma('sp', gi[:, i4 * 8:(i4 + 1) * 8, :], GTv[:, tsl, 2:4], lds, reads=[B_fm], writes=[B_gi])
            kb.op('dve', lambda e: e.memset(v1[:, :, 256:257], 1.0), writes=[B_v1])
            for which, (zz, oo) in enumerate(((zq, qT), (zk, kT))):
                for pc in range(QW // 512):
                    t0 = pc * 512
                    kb.op('dve', lambda e, zz=zz, t0=t0: e.tensor_scalar(out=acc[:], in0=zz[:, t0:t0 + 512], scalar1=cw[:, which * 4:which * 4 + 1],
                                                                       scalar2=None, op0=ALU.mult), reads=[B_z, B_c], writes=[B_acc])
                    for j_ in range(1, 4):
                        kb.op('dve', lambda e, zz=zz, t0=t0, j_=j_: e.scalar_tensor_tensor(
                            out=acc[:], in0=zz[:, t0 + j_:t0 + j_ + 512], scalar=cw[:, which * 4 + j_:which * 4 + j_ + 1], in1=acc[:],
                            op0=ALU.mult, op1=ALU.add), reads=[B_z, B_c], writes=[B_acc])
                    if which == 0:
                        kb.op('act', lambda e, oo=oo, t0=t0: e.activation(out=oo[:, t0:t0 + 512], in_=acc[:], func=AF.Silu, bias=cb[:, 0:1]),
                              reads=[B_acc, B_c], writes=[B_qk])
                    else:
                        kb.op('act', lambda e: e.activation(out=acc[:], in_=acc[:], func=AF.Silu, bias=cb[:, 1:2]), reads=[B_c], writes=[B_acc])
                        kb.op('dve', lambda e, oo=oo, t0=t0: e.tensor_scalar(out=oo[:, t0:t0 + 512], in0=acc[:], scalar1=SCALE, scalar2=None, op0=ALU.mult),
                              reads=[B_acc], writes=[B_qk])
            giv = gi[:].rearrange("p t c -> p c t")
            kb.op('dve', lambda e: e.tensor_copy(out=gsm[:, 6, :], in_=giv[:, 1, :]), reads=[B_gi], writes=[B_gsm])
            kb.op('dve', lambda e: e.tensor_copy(out=gsm[:, 7, :], in_=giv[:, 0, :]), reads=[B_gi], writes=[B_gsm])
            kb.op('pe', lambda e: e.matmul(ps[0][:, 0:32], lhsT=tri[:], rhs=gsm[:, 6, :], start=True, stop=True), reads=[B_gsm, B_const], writes=[PB[0]])
            kb.op('pe', lambda e: e.matmul(ps[1][:, 0:32], lhsT=onesf[:], rhs=gsm[:, 6, :], start=True, stop=True), reads=[B_gsm, B_const], writes=[PB[1]])
            kb.op('dve', lambda e: e.tensor_copy(out=gsm[:, 0, :], in_=ps[0][:, 0:32]), reads=[PB[0]], writes=[B_gsm])
            kb.op('dve', lambda e: e.tensor_copy(out=gsm[:, 1, :], in_=ps[1][:, 0:32]), reads=[PB[1]], writes=[B_gsm])
            kb.op('dve', lambda e: e.tensor_tensor(out=gsm[:, 2, :], in0=gsm[:, 7, :], in1=gsm[:, 0, :], op=ALU.subtract), reads=[B_gsm], writes=[B_gsm])
            kb.op('dve', lambda e: e.tensor_tensor(out=gsm[:, 3, :], in0=gsm[:, 2, :], in1=gsm[:, 1, :], op=ALU.add), reads=[B_gsm], writes=[B_gsm])
            kb.op('act', lambda e: e.activation(out=gsm[:, 3, :], in_=gsm[:, 3, :], func=AF.Exp), reads=[B_gsm], writes=[B_gsm])
            kb.op('act', lambda e: e.activation(out=gsm[:, 4, :], in_=gsm[:, 0, :], func=AF.Exp), reads=[B_gsm], writes=[B_gsm])
            kb.op('act', lambda e: e.activation(out=gsm[:, 5, :], in_=gsm[:, 1, :], func=AF.Exp), reads=[B_gsm], writes=[B_gsm])
            for ch in range(32):
                cs = slice(ch * 128, (ch + 1) * 128)
                kb.op('pe', lambda e, cs=cs: e.matmul(ps[2][:, 0:128], lhsT=kT[:, cs], rhs=qT[:, cs], start=True, stop=True),
                      reads=[B_qk], writes=[PB[2]])
                kb.op('pe', lambda e, ch=ch: e.matmul(ps[3][:, 0:128], lhsT=gsm[:, 0, ch:ch + 1].to_broadcast([128, 128]), rhs=ident[:],
                                                    start=True, stop=True), reads=[B_gsm, B_const], writes=[PB[3]])
                kb.op('dve', lambda e: e.tensor_tensor(out=e1[:], in0=ps[3][:, 0:128], in1=trim[:], op=ALU.add), reads=[PB[3], B_c], writes=[B_e1])
                kb.op('act', lambda e, ch=ch: e.activation(out=DT[:], in_=e1[:], func=AF.Exp, bias=gsm[:, 2, ch:ch + 1]),
                      reads=[B_e1, B_gsm], writes=[B_DT])
                kb.op('dve', lambda e: e.tensor_tensor(out=SD[:], in0=ps[2][:, 0:128], in1=DT[:], op=ALU.mult), reads=[PB[2], B_DT], writes=[B_SD])
                kb.op('pe', lambda e, ch=ch: e.matmul(ps[4][:, 0:257], lhsT=SD[:], rhs=v1[:, ch, :], start=True, stop=True),
                      reads=[B_SD, B_v1], writes=[PB[4]])
                kb.op('pe', lambda e, cs=cs: e.matmul(ps[5][:, 0:257], lhsT=qT[:, cs], rhs=Cb[:], start=True, stop=True),
                      reads=[B_qk, B_Cb], writes=[PB[5]])
                kb.op('act', lambda e, ch=ch: e.activation(out=qc[:], in_=ps[5][:, 0:257], func=AF.Identity, scale=gsm[:, 4, ch:ch + 1]),
                      reads=[PB[5], B_gsm], writes=[B_qc])
                kb.op('dve', lambda e: e.tensor_tensor(out=nd[:], in0=ps[4][:, 0:257], in1=qc[:], op=ALU.add), reads=[PB[4], B_qc], writes=[B_nd])
                kb.op('act', lambda e: e.activation(out=sm[:, 5:6], in_=nd[:, 256:257], func=AF.Abs), reads=[B_nd], writes=[B_sm])
                kb.op('dve', lambda e: e.tensor_scalar(out=sm[:, 0:1], in0=sm[:, 5:6], scalar1=1.0, scalar2=None, op0=ALU.max),
                      reads=[B_sm], writes=[B_sm])
                kb.op('dve', lambda e: e.reciprocal(out=sm[:, 1:2], in_=sm[:, 0:1]), reads=[B_sm], writes=[B_sm])
                kb.op('dve', lambda e: e.tensor_scalar(out=hh_[:], in0=nd[:, 0:256], scalar1=sm[:, 1:2], scalar2=None, op0=ALU.mult),
                      reads=[B_nd, B_sm], writes=[B_h])
                kb.op('act', lambda e: e.activation(out=sq[:], in_=hh_[:], func=AF.Square, accum_out=sm[:, 2:3]), reads=[B_h], writes=[B_sq, B_sm])
                kb.op('act', lambda e: e.activation(out=sm[:, 4:5], in_=sm[:, 2:3], func=AF.Sqrt, scale=1.0 / 256.0, bias=epsc[:, 0:1]),
                      reads=[B_sm, B_const], writes=[B_sm])
                kb.op('dve', lambda e: e.reciprocal(out=sm[:, 3:4], in_=sm[:, 4:5]), reads=[B_sm], writes=[B_sm])
                kb.op('dve', lambda e, ch=ch: e.tensor_tensor(out=go[:], in0=og[:, ch, :], in1=ngt[:], op=ALU.mult), reads=[B_og, B_c], writes=[B_go])
                kb.op('dve', lambda e: e.scalar_tensor_tensor(out=yy[:], in0=hh_[:], scalar=sm[:, 3:4], in1=go[:], op0=ALU.mult, op1=ALU.mult),
                      reads=[B_h, B_sm, B_go], writes=[B_yy])
                for half in range(2):
                    kb.op('pe', lambda e, half=half: e.transpose(ps[6][:, half * 128:(half + 1) * 128], yy[:, half * 128:(half + 1) * 128], ident[:]),
                          reads=[B_yy, B_const], writes=[PB[6]], sig=(half == 1))
                kb.op('act', lambda e, cs=cs: e.activation(out=yTs[:, :, cs], in_=ps[6][:, 0:256].rearrange("p (a b) -> p a b", a=2), func=AF.Copy),
                      reads=[PB[6]], writes=[B_yT])
                kb.op('pe', lambda e, cs=cs: e.matmul(ps[7][:, 0:128], lhsT=kT[:, cs], rhs=identb[:], start=True, stop=True),
                      reads=[B_qk, B_const], writes=[PB[7]])
                kb.op('act', lambda e, ch=ch: e.activation(out=kw[:], in_=ps[7][:, 0:128], func=AF.Identity, scale=gsm[:, 3, ch:ch + 1]),
                      reads=[PB[7], B_gsm], writes=[B_kw])
                kb.op('pe', lambda e, ch=ch: e.matmul(ps[0][:, 0:257], lhsT=kw[:], rhs=v1[:, ch, :], start=True, stop=True),
                      reads=[B_kw, B_v1], writes=[PB[0]])
                kb.op('dve', lambda e, ch=ch: e.scalar_tensor_tensor(out=Cs[:], in0=Cs[:], scalar=gsm[:, 5, ch:ch + 1], in1=ps[0][:, 0:257],
                                                                    op0=ALU.mult, op1=ALU.add), reads=[PB[0], B_gsm], writes=[B_C])
                kb.op('dve', lambda e: e.tensor_copy(out=Cb[:], in_=Cs[:]), reads=[B_C], writes=[B_Cb])
            for tb in range(8):
                for a_ in range(2):
                    kb.dma('pool', YS[c * 8 + tb, 2 + a_], yTs[:, a_, tb * G:(tb + 1) * G], sts, reads=[B_yT], writes=[B_ysp])
                    kb.cc(YS[c * 8 + tb, 2 + a_], YG[c * 8 + tb, 2 + a_], ccs, reads=[B_ysp], writes=[B_yg])
        kb.barrier()


_CACHE = {}


def kernel(**inputs):
    f32 = np.float32
    x = np.asarray(inputs["x"], f32)
    c = np.asarray(inputs["c"], f32)
    sq = lambda k: np.ascontiguousarray(np.asarray(inputs[k], f32)[0])
    if "nc" not in _CACHE:
        _CACHE["nc"] = build()
    nc = _CACHE["nc"]
    lnp = np.stack([sq("ln1_g"), sq("ln1_b"), sq("ln2_g"), sq("ln2_b"), sq("ln3_g"), sq("ln3_b")], 0)
    gbias = np.concatenate([sq("fox_f_bias"), sq("mlstm_i_bias"), sq("mlstm_f_bias")])[None, :]
    conv_w = sq("mlstm_conv_w")
    conv_b = sq("mlstm_conv_b")
    norm_g = sq("mlstm_norm_g")
    b_adaT = np.ascontiguousarray(sq("b_ada").reshape(144, 128).T)
    ident = np.eye(128, dtype=f32)
    tri = np.triu(np.ones((128, 128), f32))
    sel = np.zeros((128, 128), f32)
    sel[127, :] = 1.0
    kk = np.arange(128)[:, None, None] + 128 * np.arange(4)[None, :, None]
    qq = np.arange(512)[None, None, :]
    masks = np.where(kk <= qq, 0.0, NEG).astype(f32)
    w_in_full = sq("w_in")
    shared = {"w_ada": sq("w_ada"), "b_adaT": b_adaT, "ffn1_w_in": sq("ffn1_w_in"), "ffn1_w_out": sq("ffn1_w_out"),
              "w_out": sq("w_out"), "ffn2_w_in": sq("ffn2_w_in"), "ffn2_w_out": sq("ffn2_w_out"),
              "lnp": lnp, "ident": ident, "tri": tri, "sel127": sel, "masks": masks}
    in_maps = []
    for core in range(8):
        b, j = core // 4, core % 4
        m = dict(shared)
        m["x"] = np.ascontiguousarray(x[b, j * TPC:(j + 1) * TPC, :])
        m["cT"] = np.ascontiguousarray(c[b].reshape(16, 128).T)
        cw = np.stack([conv_w[:, j * 128:(j + 1) * 128].T, conv_w[:, 512 + j * 128:512 + (j + 1) * 128].T], 1)
        m["convw"] = np.ascontiguousarray(cw.reshape(128, 8))
        m["convb"] = np.ascontiguousarray(np.stack([conv_b[j * 128:(j + 1) * 128], conv_b[512 + j * 128:512 + (j + 1) * 128]], 1))
        m["normg"] = np.ascontiguousarray(norm_g[j * 256:(j + 1) * 256][None, :])
        cols = np.concatenate([np.arange(2 * j * 128, 2 * j * 128 + 256), np.arange(1024 + 2 * j * 128, 1024 + 2 * j * 128 + 256),
                               np.arange(3080 + j * 128, 3080 + (j + 1) * 128), np.arange(3592 + j * 128, 3592 + (j + 1) * 128),
                               np.arange(2048 + 2 * j * 128, 2048 + 2 * j * 128 + 256), np.arange(4104 + j * 256, 4104 + (j + 1) * 256),
                               np.arange(5136 + j * 256, 5136 + (j + 1) * 256),
                               np.array([3072 + 2 * j, 3072 + 2 * j + 1, 5128 + j, 5132 + j])])
        m["win_own"] = np.ascontiguousarray(w_in_full[:, cols])
        m["gb_own"] = np.ascontiguousarray(gbias[0, [2 * j, 2 * j + 1, 8 + j, 12 + j]][None, :])
        in_maps.append(m)
    res = run_bass_kernel_spmd(nc, in_maps, core_ids=list(range(8)))
    out = np.empty((2, S, D), f32)
    for core in range(8):
        b, j = core // 4, core % 4
        out[b, j * TPC:(j + 1) * TPC, :] = res.results[core]["out"]
    if os.environ.get("KSTOP", ""):
        _CACHE["dbg"] = [{k: v for k, v in r.items() if k != "out"} for r in res.results]
    return out
```

```python
import contextlib
import os
import numpy as np
import ml_dtypes
import concourse.bass as bass
import concourse.mybir as mybir
from concourse.bass_utils import run_bass_kernel_spmd

F32 = mybir.dt.float32
BF16 = mybir.dt.bfloat16
AF = mybir.ActivationFunctionType
ALU = mybir.AluOpType

D = 2048
S = 16384
TPC = 4096
G = 512
NG = 8
DFF = 5632
NFF = 44
INW = 6160
ALPHA = 2.0 ** 0.25
EPS = 1e-5
NEG = -30000.0
RG = [[0, 1, 2, 3], [4, 5, 6, 7]]


_UC = [0]


def _u(name):
    _UC[0] += 1
    return "%s_u%d" % (name, _UC[0])


class Buf:
    def __init__(self, name):
        self.name = name
        self.w = None
        self.r = []


class Slot:
    def __init__(self, nc, name):
        self.sem = nc.alloc_semaphore(name)
        self.name = name
        self.cnt = 0


class KB:
    def __init__(self, nc):
        self.nc = nc
        self.eng = {'pe': nc.tensor, 'act': nc.scalar, 'dve': nc.vector, 'pool': nc.gpsimd, 'sp': nc.sync}
        self.sem = {e: nc.alloc_semaphore('c_' + e) for e in ('pe', 'act', 'dve', 'pool')}
        self.cnt = {e: 0 for e in self.sem}
        self.seen = {e: {} for e in self.eng}
        self.pend = []
        self.slots = []
        self.nslot = 0

    def slot(self, name):
        s = Slot(self.nc, 'd_%s_%d' % (name, self.nslot))
        self.nslot += 1
        self.slots.append(s)
        return s

    def wait(self, e, tok):
        if tok is None:
            return
        sem, key, val = tok
        if key == e and e == 'pe':
            return
        if self.seen[e].get(key, 0) >= val:
            return
        self.eng[e].wait_ge(sem, val)
        self.seen[e][key] = val

    def _deps(self, e, reads, writes):
        for b in reads:
            self.wait(e, b.w)
        for b in writes:
            self.wait(e, b.w)
            for t in b.r:
                self.wait(e, t)

    def op(self, e, fn, reads=(), writes=(), sig=True):
        self._deps(e, reads, writes)
        ins = fn(self.eng[e])
        if sig:
            self.cnt[e] += 1
            ins.then_inc(self.sem[e], 1)
            tok = (self.sem[e], e, self.cnt[e])
            if e == 'pe':
                for (rb, wb) in self.pend:
                    for b in rb:
                        b.r.append(tok)
                    for b in wb:
                        b.w = tok
                        b.r = []
                self.pend = []
            for b in reads:
                b.r.append(tok)
            for b in writes:
                b.w = tok
                b.r = []
            return tok
        assert e == 'pe'
        self.pend.append((list(reads), list(writes)))
        return None

    def dma(self, q, out, in_, slot, reads=(), writes=(), **kw):
        self._deps(q, reads, writes)
        self.eng[q].dma_start(out=out, in_=in_, **kw).then_inc(slot.sem, 16)
        slot.cnt += 16
        tok = (slot.sem, slot.name, slot.cnt)
        for b in reads:
            b.r.append(tok)
        for b in writes:
            b.w = tok
            b.r = []
        return tok

    def cc(self, ins, outs, slot, reads=(), writes=()):
        self._deps('pool', reads, writes)
        self.nc.gpsimd.collective_compute("AllGather", ALU.bypass, replica_groups=RG,
                                          ins=[ins.opt()], outs=[outs.opt()]).then_inc(slot.sem, 1)
        slot.cnt += 1
        tok = (slot.sem, slot.name, slot.cnt)
        for b in reads:
            b.r.append(tok)
        for b in writes:
            b.w = tok
            b.r = []
        return tok

    def barrier(self, engines=('pe', 'act', 'dve', 'pool', 'sp')):
        toks = [(self.sem[e], e, self.cnt[e]) for e in self.sem if self.cnt[e] > 0]
        toks += [(s.sem, s.name, s.cnt) for s in self.slots if s.cnt > 0]
        for e in engines:
            for t in toks:
                self.wait(e, t)


def build(dbg=False):
    nc = bass.Bass("TRN2", target_bir_lowering=False)
    kb = KB(nc)

    def din(name, shape, dt=F32):
        return nc.dram_tensor(name, shape, dt, kind="ExternalInput").ap()

    def dscr(name, shape, dt):
        return nc.dram_tensor(name, shape, dt, kind="Internal").ap()

    x_in = din("x", [TPC, D])
    cT = din("cT", [128, 16])
    w_ada = din("w_ada", [D, 9 * D])
    b_adaT = din("b_adaT", [128, 144])
    wf = {"w1a": din("ffn1_w_in", [D, 2 * DFF]), "w2a": din("ffn1_w_out", [DFF, D]),
          "wout": din("w_out", [D, D]),
          "w1b": din("ffn2_w_in", [D, 2 * DFF]), "w2b": din("ffn2_w_out", [DFF, D])}
    lnp = din("lnp", [6, D])
    convw = din("convw", [128, 8])
    convb = din("convb", [128, 2])
    normg = din("normg", [1, 256])
    wino_f = din("win_own", [D, 1540])
    gbo = din("gb_own", [1, 4])
    wino = dscr("wino_bf", [D, 1540], BF16)
    ident_d = din("ident", [128, 128])
    tri_d = din("tri", [128, 128])
    sel_d = din("sel127", [128, 128])
    mask_d = din("masks", [128, 4, 512])
    out_d = nc.dram_tensor("out", [TPC, D], F32, kind="ExternalOutput").ap()

    wb = {k: dscr(k + "_bf", list(v.shape), BF16) for k, v in wf.items()}
    x1_d = dscr("x1s", [TPC, D], F32)
    gate_scr = dscr("gate_scr", [48, 128], F32)

    def sb(name, shape, dt):
        return nc.alloc_sbuf_tensor(_u(name), shape, dt)

    ident = sb("ident", [128, 128], F32)
    identb = sb("identb", [128, 128], BF16)
    onesb = sb("onesb", [128, 128], BF16)
    onesf = sb("onesf", [128, 128], F32)
    epsc = sb("epsc", [128, 1], F32)
    tri = sb("tri", [128, 128], F32)
    sel127 = sb("sel127", [128, 128], F32)
    adaT = sb("adaT", [128, 144], F32)
    modsc = sb("modsc", [128, 3, 16], F32)
    B_const = Buf("const")
    B_ada = Buf("ada")

    ps = [nc.alloc_psum_tensor("ps%d" % i, [128, 512], F32) for i in range(8)]
    PB = [Buf("ps%d" % i) for i in range(8)]

    ld = kb.slot("const")
    for t, dsrc in ((ident, ident_d), (tri, tri_d), (sel127, sel_d)):
        kb.dma('sp', t[:], dsrc, ld, writes=[B_const])
    kb.op('dve', lambda e: e.tensor_copy(out=identb[:], in_=ident[:]), reads=[B_const], writes=[B_const])
    kb.op('dve', lambda e: e.memset(onesb[:], 1.0), writes=[B_const])
    kb.op('dve', lambda e: e.memset(onesf[:], 1.0), writes=[B_const])
    kb.op('dve', lambda e: e.memset(epsc[:], EPS), writes=[B_const])

    WB = {k: Buf("wb_" + k) for k in wf}
    wslot = {k: kb.slot("wc_" + k) for k in wf}

    def cast_weight(k, nchunk):
        src, dst = wf[k], wb[k]
        rows = src.shape[0]
        r = rows // nchunk
        for i in range(nchunk):
            kb.dma('pool', dst[i * r:(i + 1) * r, :], src[i * r:(i + 1) * r, :], wslot[k], writes=[],
                   max_dma_last_dim=8192)
        WB[k].w = (wslot[k].sem, wslot[k].name, wslot[k].cnt)

    cast_weight("w1a", 8)
    cast_weight("w2a", 4)
    wf["wino"] = wino_f
    wb["wino"] = wino
    WB["wino"] = Buf("wb_wino")
    wslot["wino"] = kb.slot("wc_wino")
    cast_weight("wino", 2)

    _es = contextlib.ExitStack()
    aw0 = _es.enter_context(nc.sbuf_tensor(_u("ada_w0"), [128, 16, 512], F32))
    aw1 = _es.enter_context(nc.sbuf_tensor(_u("ada_w1"), [128, 16, 512], F32))
    scT = _es.enter_context(nc.sbuf_tensor(_u("scT"), [128, 16], F32))
    badaT = _es.enter_context(nc.sbuf_tensor(_u("badaT"), [128, 144], F32))
    gT = _es.enter_context(nc.sbuf_tensor(_u("gT"), [128, 48], F32))
    gTT = _es.enter_context(nc.sbuf_tensor(_u("gTT"), [48, 128], F32))
    with _es:
        aws = [aw0, aw1]
        AWB = [Buf("aw0"), Buf("aw1")]
        awslot = [kb.slot("aw0"), kb.slot("aw1")]
        B_sc = Buf("scT")
        ld2 = kb.slot("ld2")
        ld3 = kb.slot("ld3")
        kb.dma('sp', scT[:], cT, ld2, writes=[B_sc])
        kb.dma('sp', badaT[:], b_adaT, ld2, writes=[B_sc])
        kb.op('act', lambda e: e.activation(out=scT[:], in_=scT[:], func=AF.Silu), reads=[B_sc], writes=[B_sc])
        w_ada_v = w_ada.rearrange("(kc p) n -> p kc n", p=128)
        for blk in range(36):
            a = aws[blk % 2]
            kb.dma('sp', a[:], w_ada_v[:, :, blk * 512:(blk + 1) * 512], awslot[blk % 2], writes=[AWB[blk % 2]])
            for cc in range(4):
                ch = blk * 4 + cc
                for kc in range(16):
                    last = (kc == 15)
                    kb.op('pe', lambda e, a=a, cc=cc, kc=kc, ch=ch: e.matmul(
                        ps[0][:, ch:ch + 1], lhsT=a[:, kc, cc * 128:(cc + 1) * 128], rhs=scT[:, kc:kc + 1],
                        start=(kc == 0), stop=(kc == 15)),
                        reads=[AWB[blk % 2], B_sc], writes=[PB[0]], sig=(last and cc == 3))
        kb.op('dve', lambda e: e.tensor_tensor(out=adaT[:], in0=ps[0][:, 0:144], in1=badaT[:], op=ALU.add),
              reads=[PB[0], B_sc], writes=[B_ada])
        for k in range(3):
            kb.op('dve', lambda e, k=k: e.tensor_scalar(out=modsc[:, k, :], in0=adaT[:, (3 * k + 1) * 16:(3 * k + 2) * 16],
                                                       scalar1=1.0, scalar2=None, op0=ALU.add),
                  reads=[B_ada], writes=[B_ada])
        for k, coef in enumerate((0.5, 1.0, 0.5)):
            kb.op('dve', lambda e, k=k, coef=coef: e.tensor_scalar(
                out=gT[:, k * 16:(k + 1) * 16], in0=adaT[:, (3 * k + 2) * 16:(3 * k + 3) * 16],
                scalar1=1.0, scalar2=coef, op0=ALU.add, op1=ALU.mult), reads=[B_ada], writes=[B_sc])
        kb.op('pe', lambda e: e.transpose(ps[1][0:48, 0:128], gT[:, 0:48], ident[:]), reads=[B_sc, B_const], writes=[PB[1]])
        kb.op('dve', lambda e: e.tensor_copy(out=gTT[:], in_=ps[1][0:48, 0:128]), reads=[PB[1]], writes=[B_sc])
        B_gscr = Buf("gscr")
        kb.dma('sp', gate_scr, gTT[:], ld3, reads=[B_sc], writes=[B_gscr])
        kb.barrier()

    cast_weight("wout", 2)
    cast_weight("w1b", 8)
    cast_weight("w2b", 4)
    STOP = os.environ.get("KSTOP", "")
    if STOP == "cast":
        kb.barrier()
        return nc

    gate_rows = gate_scr.rearrange("(k c) p -> k (c p)", k=3)

    def ffn_phase(phase):
        _es = contextlib.ExitStack()
        xt = _es.enter_context(nc.sbuf_tensor(_u("xt"), [128, 4, D], F32))
        xn = _es.enter_context(nc.sbuf_tensor(_u("xn"), [128, D], F32))
        hT = _es.enter_context(nc.sbuf_tensor(_u("hT"), [128, 16, G], BF16))
        HT = _es.enter_context(nc.sbuf_tensor(_u("HT"), [128, NFF, G], BF16))
        wA0 = _es.enter_context(nc.sbuf_tensor(_u("wA0"), [128, 16, 512], BF16))
        wA1 = _es.enter_context(nc.sbuf_tensor(_u("wA1"), [128, 16, 512], BF16))
        wA2 = _es.enter_context(nc.sbuf_tensor(_u("wA2"), [128, 16, 512], BF16))
        wB0 = _es.enter_context(nc.sbuf_tensor(_u("wB0"), [128, 1024], BF16))
        wB1 = _es.enter_context(nc.sbuf_tensor(_u("wB1"), [128, 1024], BF16))
        wB2 = _es.enter_context(nc.sbuf_tensor(_u("wB2"), [128, 1024], BF16))
        wB3 = _es.enter_context(nc.sbuf_tensor(_u("wB3"), [128, 1024], BF16))
        tabs = _es.enter_context(nc.sbuf_tensor(_u("tabs"), [128, 3, D], F32))
        tmp = _es.enter_context(nc.sbuf_tensor(_u("tmp"), [128, 2, 512], F32))
        sg = _es.enter_context(nc.sbuf_tensor(_u("sg"), [128, 2, 512], F32))
        stat = _es.enter_context(nc.sbuf_tensor(_u("stat"), [128, 4, 6], F32))
        mv = _es.enter_context(nc.sbuf_tensor(_u("mv"), [128, 4], F32))
        gst = _es.enter_context(nc.sbuf_tensor(_u("gst"), [128, 4, 16], F32))
        gso = _es.enter_context(nc.sbuf_tensor(_u("gso"), [128, 4, 16], F32))
        wg = _es.enter_context(nc.sbuf_tensor(_u("wg"), [128, 16, 16], BF16))
        with _es:
            stage = HT[:, 0:24, :]
            wA = [wA0, wA1, wA2]
            WA = [Buf("wA%d" % i) for i in range(3)]
            wAs = [kb.slot("wA%d" % i) for i in range(3)]
            wBt = [wB0, wB1, wB2, wB3]
            WBb = [Buf("wB%d" % i) for i in range(4)]
            wBs = [kb.slot("wB%d" % i) for i in range(4)]
            st = {"a": 0, "b": 0}
            B_xt = [Buf("xt%d" % i) for i in range(4)]
            B_xn = Buf("xn")
            B_hT = Buf("hT")
            B_HT = [Buf("HT%d" % i) for i in range(NFF)]
            B_tabs = Buf("tabs")
            B_tmp = [Buf("tmp0"), Buf("tmp1")]
            B_sg = [Buf("sg0"), Buf("sg1")]
            B_stat = Buf("stat")
            B_stage = Buf("stage")
            B_gst = Buf("gst")
            B_gso = Buf("gso")
            B_gso = Buf("gso")
            B_wg = Buf("wg")
            xs = kb.slot("xld")
            tbs = kb.slot("tabs")
            sts = kb.slot("store")
            sts_a = kb.slot("store_a")
            sts_b = kb.slot("store_b")
            sts_t = kb.slot("store_t")
            wgs = kb.slot("wg")
            ccs = kb.slot("cc")
            B_x1d = Buf("x1d")

            def loadA(src_ap_fn, wkey):
                i = st["a"] % 3
                st["a"] += 1
                for (o, s_) in src_ap_fn(wA[i]):
                    kb.dma('sp', o, s_, wAs[i], reads=[WB[wkey]], writes=[WA[i]])
                return i

            def load_tabs(rows):
                for i, r in enumerate(rows):
                    kb.dma('sp', tabs[:, i, :], r.partition_broadcast(128), tbs, reads=[B_gscr], writes=[B_tabs])

            def ln_stats(tt, col):
                for q in range(4):
                    kb.op('dve', lambda e, q=q: e.bn_stats(out=stat[:, q, :], in_=xt[:, tt, q * 512:(q + 1) * 512]),
                          reads=[B_xt[tt]], writes=[B_stat])
                kb.op('dve', lambda e: e.bn_aggr(out=mv[:, 0:2], in_=stat[:].rearrange("p a b -> p (a b)")),
                      reads=[B_stat], writes=[B_stat])
                kb.op('act', lambda e: e.activation(out=mv[:, 3:4], in_=mv[:, 1:2], func=AF.Sqrt, bias=epsc[:, 0:1]), reads=[B_stat, B_const], writes=[B_stat])
                kb.op('dve', lambda e: e.reciprocal(out=mv[:, 2:3], in_=mv[:, 3:4]), reads=[B_stat], writes=[B_stat])

            def mod_transpose(k):
                for tt in range(4):
                    ln_stats(tt, 0)
                    kb.op('dve', lambda e, tt=tt: e.tensor_scalar(out=xn[:], in0=xt[:, tt, :], scalar1=mv[:, 0:1],
                                                                 scalar2=mv[:, 2:3], op0=ALU.subtract, op1=ALU.mult),
                          reads=[B_xt[tt], B_stat], writes=[B_xn])
                    for q in range(4):
                        pi = 4 + (q % 2)
                        for c4 in range(4):
                            kc = q * 4 + c4
                            kb.op('pe', lambda e, kc=kc, c4=c4, pi=pi: e.transpose(
                                ps[pi][:, c4 * 128:(c4 + 1) * 128], xn[:, kc * 128:(kc + 1) * 128], ident[:]),
                                reads=[B_xn, B_const], writes=[PB[pi]], sig=(c4 == 3))
                        for c4 in range(4):
                            kc = q * 4 + c4
                            kb.op('act', lambda e, kc=kc, c4=c4, pi=pi, tt=tt: e.activation(
                                out=hT[:, kc, tt * 128:(tt + 1) * 128], in_=ps[pi][:, c4 * 128:(c4 + 1) * 128],
                                func=AF.Identity, scale=modsc[:, k, kc:kc + 1],
                                bias=adaT[:, (3 * k) * 16 + kc:(3 * k) * 16 + kc + 1]),
                                reads=[PB[pi], B_ada], writes=[B_hT])

            def ffn(w1k, w2k, k, gt_row, lng_row, lnb_row):
                w1 = wb[w1k].rearrange("(kc p) n -> p kc n", p=128)
                w2 = wb[w2k]
                mod_transpose(k)
                load_tabs([gate_rows[gt_row:gt_row + 1, :], lnp[lng_row:lng_row + 1, :], lnp[lnb_row:lnb_row + 1, :]])
                def srcs(i2):
                    return lambda t: [(t[:, :, 0:256], w1[:, :, i2 * 256:(i2 + 1) * 256]),
                                      (t[:, :, 256:512], w1[:, :, DFF + i2 * 256:DFF + (i2 + 1) * 256])]
                pre = [loadA(srcs(0), w1k), loadA(srcs(1), w1k)]
                for i2 in range(22):
                    wi = pre[i2]
                    if i2 + 2 < 22:
                        pre.append(loadA(srcs(i2 + 2), w1k))
                    for c in range(2):
                        i = i2 * 2 + c
                        pg, pu = (0, 1) if (i % 2 == 0) else (2, 3)
                        for kc in range(16):
                            kb.op('pe', lambda e, wi=wi, kc=kc, c=c, pg=pg: e.matmul(
                                ps[pg][:, :], lhsT=wA[wi][:, kc, c * 128:(c + 1) * 128], rhs=hT[:, kc, :],
                                start=(kc == 0), stop=(kc == 15)), reads=[WA[wi], B_hT], writes=[PB[pg]], sig=(kc == 15))
                        for kc in range(16):
                            kb.op('pe', lambda e, wi=wi, kc=kc, c=c, pu=pu: e.matmul(
                                ps[pu][:, :], lhsT=wA[wi][:, kc, 256 + c * 128:256 + (c + 1) * 128], rhs=hT[:, kc, :],
                                start=(kc == 0), stop=(kc == 15)), reads=[WA[wi], B_hT], writes=[PB[pu]], sig=(kc == 15))
                        kb.op('act', lambda e, pg=pg, i=i: e.activation(out=sg[:, i % 2, :], in_=ps[pg][:, :], func=AF.Silu),
                              reads=[PB[pg]], writes=[B_sg[i % 2]])
                        kb.op('dve', lambda e, pu=pu, i=i: e.tensor_tensor(out=HT[:, i, :], in0=sg[:, i % 2, :], in1=ps[pu][:, :],
                                                                          op=ALU.mult),
                              reads=[PB[pu], B_sg[i % 2]], writes=[B_HT[i]] + ([B_stage] if i < 24 else []))
                def loadB(dh, i):
                    s_ = st["b"] % 4
                    st["b"] += 1
                    kb.dma('sp', wBt[s_][:], w2[i * 128:(i + 1) * 128, dh * 1024:(dh + 1) * 1024], wBs[s_],
                           reads=[WB[w2k]], writes=[WBb[s_]])
                    return s_
                seq = [(dh, i) for dh in range(2) for i in range(NFF)]
                preb = [loadB(*seq[0]), loadB(*seq[1]), loadB(*seq[2])]
                for n, (dh, i) in enumerate(seq):
                    s_ = preb[n]
                    if n + 3 < len(seq):
                        preb.append(loadB(*seq[n + 3]))
                    for tt in range(4):
                        for dgi in range(2):
                            pi = tt * 2 + dgi
                            kb.op('pe', lambda e, s_=s_, tt=tt, dgi=dgi, pi=pi, i=i: e.matmul(
                                ps[pi][:, :], lhsT=HT[:, i, tt * 128:(tt + 1) * 128], rhs=wBt[s_][:, dgi * 512:(dgi + 1) * 512],
                                start=(i == 0), stop=(i == NFF - 1)),
                                reads=[WBb[s_], B_HT[i]], writes=[PB[pi]], sig=(i == NFF - 1 or (tt == 3 and dgi == 1)))
                    if i == NFF - 1:
                        for tt in range(4):
                            for dgi in range(2):
                                pi = tt * 2 + dgi
                                c0 = dh * 1024 + dgi * 512
                                residual(tt, pi, c0)
                for tt in range(4):
                    ln_affine(tt)

            def residual(tt, pi, c0):
                j = pi % 2
                kb.op('dve', lambda e: e.tensor_tensor(out=tmp[:, j, :], in0=ps[pi][:, :], in1=tabs[:, 0, c0:c0 + 512], op=ALU.mult),
                      reads=[PB[pi], B_tabs], writes=[B_tmp[j]])
                kb.op('dve', lambda e: e.scalar_tensor_tensor(out=xt[:, tt, c0:c0 + 512], in0=xt[:, tt, c0:c0 + 512], scalar=ALPHA,
                                                             in1=tmp[:, j, :], op0=ALU.mult, op1=ALU.add),
                      reads=[B_tmp[j]], writes=[B_xt[tt]])

            def ln_affine(tt):
                ln_stats(tt, 0)
                kb.op('dve', lambda e: e.tensor_scalar(out=xt[:, tt, :], in0=xt[:, tt, :], scalar1=mv[:, 0:1], scalar2=mv[:, 2:3],
                                                       op0=ALU.subtract, op1=ALU.mult), reads=[B_stat], writes=[B_xt[tt]])
                kb.op('dve', lambda e: e.tensor_tensor(out=xt[:, tt, :], in0=xt[:, tt, :], in1=tabs[:, 1, :], op=ALU.mult),
                      reads=[B_tabs], writes=[B_xt[tt]])
                kb.op('dve', lambda e: e.tensor_tensor(out=xt[:, tt, :], in0=xt[:, tt, :], in1=tabs[:, 2, :], op=ALU.add),
                      reads=[B_tabs], writes=[B_xt[tt]])


            def send_h2(g):
                mod_transpose(1)
                kb.dma('pool', H2s[g].rearrange("kc p t -> p kc t"), hT[:], sts_a, reads=[B_hT], writes=[B_h2s[g]])
                for kc in range(16):
                    kb.cc(H2s[g][kc], H2g[g][kc], ccs, reads=[B_h2s[g]], writes=[B_h2g[g]])

            def attn_out(g):
                woutv = wb["wout"].rearrange("(kc p) n -> p kc n", p=128)
                for c in range(4):
                    pass
                for rb in range(4):
                    kb.dma('sp', hT[:].rearrange("p (r rb) t -> p rb r t", rb=4)[:, rb], myY[g, rb].rearrange("(r p) t -> p r t", p=128), xs,
                           reads=[B_yg], writes=[B_hT])
                load_tabs([gate_rows[1:2, :], lnp[2:3, :], lnp[3:4, :]])
                fch = []
                for r in range(4):
                    fch += [2 * r, 2 * r + 1, 8 + 2 * r, 9 + 2 * r]
                for dg in range(4):
                    wi = loadA(lambda t, dg=dg: [(t[:], woutv[:, :, dg * 512:(dg + 1) * 512])], "wout")
                    for tt in range(4):
                        pi = 4 * (dg % 2) + tt
                        for ch in range(16):
                            kb.op('pe', lambda e, wi=wi, ch=ch, tt=tt, pi=pi: e.matmul(
                                ps[pi][:, :], lhsT=hT[:, ch, tt * 128:(tt + 1) * 128], rhs=wA[wi][:, fch[ch], :],
                                start=(ch == 0), stop=(ch == 15)), reads=[WA[wi], B_hT], writes=[PB[pi]], sig=(ch == 15))
                        residual(tt, pi, dg * 512)
                for tt in range(4):
                    ln_affine(tt)

            for g in range(NG):
                rows = slice(g * G, (g + 1) * G)
                if phase == 1:
                    kb.dma('sp', xt[:], x_in[rows, :].rearrange("(tt p) d -> p tt d", p=128), xs, writes=B_xt)
                    ffn("w1a", "w2a", 0, 0, 0, 1)
                    if STOP == "g0ffn":
                        kb.dma('pool', out_d[rows, :].rearrange("(tt p) d -> p tt d", p=128), xt[:], sts, reads=B_xt, writes=[B_outd])
                        break
                    kb.dma('pool', x1_d[rows, :].rearrange("(tt p) d -> p tt d", p=128), xt[:], sts, reads=B_xt, writes=[B_x1d])
                    if STOP == "g0":
                        kb.dma('pool', out_d[rows, :].rearrange("(tt p) d -> p tt d", p=128), xt[:], sts, reads=B_xt, writes=[B_outd])
                    send_h2(g)
                    if STOP == "g0":
                        break
                else:
                    kb.dma('sp', xt[:], x1_d[rows, :].rearrange("(tt p) d -> p tt d", p=128), xs, reads=[B_x1d], writes=B_xt)
                    attn_out(g)
                    ffn("w1b", "w2b", 2, 2, 4, 5)
                    kb.dma('pool', out_d[rows, :].rearrange("(tt p) d -> p tt d", p=128), xt[:], sts, reads=B_xt, writes=[B_outd])
            kb.barrier()

    jsp = nc.sync.partition_id() % 4
    H2s = [dscr("h2s%d" % g, [16, 128, G], BF16) for g in range(NG)]
    H2g = [dscr("h2g%d" % g, [16, 512, G], BF16) for g in range(NG)]
    B_h2s = [Buf("h2s%d" % g) for g in range(NG)]
    B_h2g = [Buf("h2g%d" % g) for g in range(NG)]
    FM_d = dscr("fm_d", [6, 128, S], BF16)
    TM_d = dscr("tm_d", [S, 768], BF16)
    GT_d = dscr("gt_d", [S, 4], F32)
    YS = dscr("ys_p", [32, 4, 128, G], BF16)
    YG = dscr("yg_p", [32, 4, 512, G], BF16)
    myY = dscr("myY", [8, 4, 512, G], BF16)
    B_fm = Buf("fm")
    B_yg = Buf("yg")
    B_outd = Buf("outd")

    ffn_phase(1)
    if STOP in ("g0ffn", "g0", "p1a"):
        kb.barrier()
        return nc
    zproj_own(nc, kb, ps, PB, H2g, B_h2g, wino, gbo, FM_d, TM_d, GT_d, B_fm)
    if STOP == "p1":
        dF = nc.dram_tensor("dbgF", [6, 128, S], BF16, kind="ExternalOutput").ap()
        dT = nc.dram_tensor("dbgT", [S, 768], BF16, kind="ExternalOutput").ap()
        d32 = nc.dram_tensor("dbg32", [S, 4], F32, kind="ExternalOutput").ap()
        dx1 = nc.dram_tensor("dbgx1", [TPC, D], F32, kind="ExternalOutput").ap()
        dsl = kb.slot("dbg")
        kb.dma('sp', dF, FM_d, dsl, reads=[B_fm])
        kb.dma('sp', dT, TM_d, dsl, reads=[B_fm])
        kb.dma('sp', d32, GT_d, dsl, reads=[B_fm])
        kb.dma('sp', dx1, x1_d, dsl)
        kb.barrier()
        return nc
    attention(nc, kb, ps, PB, FM_d, TM_d, GT_d, B_fm, YS, YG, B_yg, ident, identb, onesb, onesf, tri, sel127,
              B_const, mask_d, convw, convb, normg, epsc)
    if STOP == "attn":
        dY = nc.dram_tensor("dbgY", [32, 4, 128, G], BF16, kind="ExternalOutput").ap()
        dsl = kb.slot("dbg")
        kb.dma('sp', dY, YS, dsl)
        kb.barrier()
        return nc
    ysl = kb.slot("myY")
    for t in range(8):
        kb.dma('sp', myY[t], YG[jsp * 8 + t], ysl, reads=[B_yg], writes=[B_yg])
    ffn_phase(3)
    kb.barrier()
    return nc


def zproj_own(nc, kb, ps, PB, H2g, B_h2g, wino, gbo, FM_d, TM_d, GT_d, B_fm):
    _es = contextlib.ExitStack()
    wo = _es.enter_context(nc.sbuf_tensor(_u("wo"), [128, 16, 1540], BF16))
    h2a = _es.enter_context(nc.sbuf_tensor(_u("h2a"), [128, 16, G], BF16))
    h2b = _es.enter_context(nc.sbuf_tensor(_u("h2b"), [128, 16, G], BF16))
    stFa = _es.enter_context(nc.sbuf_tensor(_u("stFa"), [128, 6, G], BF16))
    stFb = _es.enter_context(nc.sbuf_tensor(_u("stFb"), [128, 6, G], BF16))
    stTa = _es.enter_context(nc.sbuf_tensor(_u("stTa"), [128, 4, 768], BF16))
    stTb = _es.enter_context(nc.sbuf_tensor(_u("stTb"), [128, 4, 768], BF16))
    stGa = _es.enter_context(nc.sbuf_tensor(_u("stGa"), [128, 4, 4], F32))
    stGb = _es.enter_context(nc.sbuf_tensor(_u("stGb"), [128, 4, 4], F32))
    gb = _es.enter_context(nc.sbuf_tensor(_u("gbo"), [128, 4], F32))
    with _es:
        h2 = [h2a, h2b]
        stF = [stFa, stFb]
        stT = [stTa, stTb]
        stG = [stGa, stGb]
        B_h2 = [Buf("h2a"), Buf("h2b")]
        B_stF = [Buf("stFa"), Buf("stFb")]
        B_stT = [Buf("stTa"), Buf("stTb")]
        B_stG = [Buf("stGa"), Buf("stGb")]
        B_wo = Buf("wo")
        wsl = kb.slot("wo")
        hsl = [kb.slot("h2a"), kb.slot("h2b")]
        ssl = [kb.slot("zsta"), kb.slot("zstb")]
        kb.dma('sp', wo[:], wino.rearrange("(kc p) n -> p kc n", p=128), wsl, reads=[], writes=[B_wo])
        kb.dma('sp', gb[:], gbo.partition_broadcast(128), wsl, writes=[B_wo])
        FMv = FM_d.rearrange("c p t -> p c t")

        def load(tg):
            r, g = tg // 8, tg % 8
            b = tg % 2
            kb.dma('sp', h2[b][:], H2g[g][:, r * 128:(r + 1) * 128, :].rearrange("kc p t -> p kc t"), hsl[b],
                   reads=[B_h2g[g]], writes=[B_h2[b]])
        load(0)
        _ZS = os.environ.get('ZPSKIP', '')
        NB_ = int(os.environ.get('ZPNB', '260'))
        for tg in range(int(os.environ.get('ZPN', '32'))):
            b = tg % 2
            tok0 = tg * G
            if tg + 1 < 32:
                load(tg + 1)
            for c in range(0 if 'fm' in _ZS else 6):
                pi = c % 4
                for kc in range(16):
                    kb.op('pe', lambda e, c=c, kc=kc, pi=pi, b=b: e.matmul(ps[pi][:, :], lhsT=wo[:, kc, c * 128:(c + 1) * 128], rhs=h2[b][:, kc, :],
                                                                      start=(kc == 0), stop=(kc == 15)),
                          reads=[B_wo, B_h2[b]], writes=[PB[pi]], sig=(kc == 15))
                if c % 2 == 0:
                    kb.op('act', lambda e, c=c, pi=pi, b=b: e.activation(out=stF[b][:, c, :], in_=ps[pi][:, :], func=AF.Copy),
                          reads=[PB[pi]], writes=[B_stF[b]])
                else:
                    kb.op('dve', lambda e, c=c, pi=pi, b=b: e.tensor_copy(out=stF[b][:, c, :], in_=ps[pi][:, :]),
                          reads=[PB[pi]], writes=[B_stF[b]])
            if 'fm' not in _ZS:
                kb.dma('pool', FMv[:, :, tok0:tok0 + G], stF[b][:], ssl[b], reads=[B_stF[b]], writes=[B_fm])
            for tt in range(0 if 'tm' in _ZS else 4):
                pa, pb_ = (4, 6) if tt % 2 == 0 else (5, 7)
                for kc in range(16):
                    kb.op('pe', lambda e, kc=kc, tt=tt, pa=pa, b=b: e.matmul(ps[pa][:, :], lhsT=h2[b][:, kc, tt * 128:(tt + 1) * 128], rhs=wo[:, kc, 768:1280],
                                                                        start=(kc == 0), stop=(kc == 15)),
                          reads=[B_wo, B_h2[b]], writes=[PB[pa]], sig=(kc == 15))
                for kc in range(16):
                    kb.op('pe', lambda e, kc=kc, tt=tt, pb_=pb_, b=b: e.matmul(ps[pb_][:, 0:NB_], lhsT=h2[b][:, kc, tt * 128:(tt + 1) * 128], rhs=wo[:, kc, 1280:1280 + NB_],
                                                                          start=(kc == 0), stop=(kc == 15)),
                          reads=[B_wo, B_h2[b]], writes=[PB[pb_]], sig=(kc == 15))
                kb.op('dve', lambda e, tt=tt, pa=pa, b=b: e.tensor_copy(out=stT[b][:, tt, 0:512], in_=ps[pa][:, :]), reads=[PB[pa]], writes=[B_stT[b]])
                if 'sig' not in _ZS:
                    kb.op('act', lambda e, tt=tt, pb_=pb_, b=b: e.activation(out=stT[b][:, tt, 512:768], in_=ps[pb_][:, 0:256], func=AF.Sigmoid),
                          reads=[PB[pb_]], writes=[B_stT[b]])
                if 'gadd' not in _ZS:
                    kb.op('act', lambda e, tt=tt, pb_=pb_, b=b: e.activation(out=stG[b][:, tt, :], in_=ps[pb_][:, 256:260], func=AF.Copy),
                          reads=[PB[pb_]], writes=[B_stG[b]])
                    kb.op('dve', lambda e, tt=tt, b=b: e.tensor_tensor(out=stG[b][:, tt, :], in0=stG[b][:, tt, :], in1=gb[:], op=ALU.add),
                          reads=[B_wo], writes=[B_stG[b]])
            for (a_, b_) in (() if ('tm' in _ZS or 'ls' in _ZS) else ((0, 2), (3, 4))):
                kb.op('act', lambda e, a_=a_, b_=b_, b=b: e.activation(out=stG[b][:, :, a_:b_], in_=stG[b][:, :, a_:b_], func=AF.Exp, scale=-1.0),
                      reads=[B_stG[b]], writes=[B_stG[b]])
                kb.op('act', lambda e, a_=a_, b_=b_, b=b: e.activation(out=stG[b][:, :, a_:b_], in_=stG[b][:, :, a_:b_], func=AF.Ln, bias=1.0),
                      reads=[B_stG[b]], writes=[B_stG[b]])
                kb.op('dve', lambda e, a_=a_, b_=b_, b=b: e.tensor_scalar(out=stG[b][:, :, a_:b_], in0=stG[b][:, :, a_:b_], scalar1=-1.0, scalar2=None,
                                                                         op0=ALU.mult), reads=[B_stG[b]], writes=[B_stG[b]])
            if 'tm' not in _ZS and 'tst' not in _ZS:
              kb.dma('pool', TM_d[tok0:tok0 + G, :].rearrange("(tt p) c -> p tt c", p=128), stT[b][:], ssl[b], reads=[B_stT[b]], writes=[B_fm])
            if 'tm' not in _ZS and 'gst' not in _ZS:
              kb.dma('pool', GT_d[tok0:tok0 + G, :].rearrange("(tt p) c -> p tt c", p=128), stG[b][:], ssl[b], reads=[B_stG[b]], writes=[B_fm])
        kb.barrier()


def attention(nc, kb, ps, PB, FM_d, TM_d, GT_d, B_fm, YS, YG, B_yg, ident, identb, onesb, onesf, tri, sel127,
              B_const, mask_d, convw_d, convb_d, normg_d, epsc):
    TMv = TM_d.rearrange("(t p) c -> p t c", p=128)
    GTv = GT_d.rearrange("(t p) c -> p t c", p=128)
    B_ysp = Buf("ysp")
    SCALE = 128.0 ** -0.5
    NT = 128
    ccs = kb.slot("ccy")
    sts = kb.slot("ysto")

    _es = contextlib.ExitStack()
    KT = _es.enter_context(nc.sbuf_tensor(_u("KT"), [128, S], BF16))
    QT = _es.enter_context(nc.sbuf_tensor(_u("QT"), [128, S], BF16))
    V = _es.enter_context(nc.sbuf_tensor(_u("V"), [128, NT, 128], BF16))
    lf = _es.enter_context(nc.sbuf_tensor(_u("lf"), [128, NT, 2], F32))
    Fm = _es.enter_context(nc.sbuf_tensor(_u("Fm"), [128, 2, NT], F32))
    scan = _es.enter_context(nc.sbuf_tensor(_u("scan"), [128, 2, 2 * NT], F32))
    offb = _es.enter_context(nc.sbuf_tensor(_u("offb"), [128, 2 * NT], F32))
    negF = _es.enter_context(nc.sbuf_tensor(_u("negF"), [128, 2, NT], F32))
    masks = _es.enter_context(nc.sbuf_tensor(_u("masks"), [128, 4, 512], F32))
    lg = _es.enter_context(nc.sbuf_tensor(_u("lg"), [128, 4, 512], F32))
    PT = _es.enter_context(nc.sbuf_tensor(_u("PT"), [128, 4, 512], BF16))
    fq = _es.enter_context(nc.sbuf_tensor(_u("fq"), [128, 2, 512], F32))
    rden = _es.enter_context(nc.sbuf_tensor(_u("rden"), [128, 512], F32))
    yst = _es.enter_context(nc.sbuf_tensor(_u("yst"), [128, 2, 512], BF16))
    with _es:
        lds = kb.slot("fxld")
        mks = kb.slot("fxmask")
        B_KT, B_QT, B_V, B_lf, B_F, B_scan, B_mask = Buf("KT"), Buf("QT"), Buf("V"), Buf("lf"), Buf("F"), Buf("scan"), Buf("mask")
        B_lg = [Buf("lg%d" % i) for i in range(4)]
        B_PT = [Buf("PT%d" % i) for i in range(4)]
        B_fq = [Buf("fq0"), Buf("fq1")]
        B_rden = Buf("rden")
        B_yst = [Buf("yst0"), Buf("yst1")]
        mks0 = kb.slot("fxmask0")
        kb.dma('sp', masks[:], mask_d, mks0, writes=[B_mask])
        for i8 in range(8):
            kb.dma('sp', lf[:, i8 * 16:(i8 + 1) * 16, :], GTv[:, i8 * 16:(i8 + 1) * 16, 0:2], mks, reads=[B_fm], writes=[B_lf])
        lfv = lf[:].rearrange("p t h -> p h t")
        kb.op('dve', lambda e: e.tensor_copy(out=scan[:, 0, :].rearrange("p (h t) -> p h t", h=2), in_=lfv), reads=[B_lf], writes=[B_scan])
        kb.op('pe', lambda e: e.matmul(ps[0][:, 0:256], lhsT=tri[:], rhs=scan[:, 0, :], start=True, stop=True),
              reads=[B_scan, B_const], writes=[PB[0]])
        kb.op('dve', lambda e: e.tensor_copy(out=Fm[:].rearrange("p h t -> p (h t)"), in_=ps[0][:, 0:256]), reads=[PB[0]], writes=[B_F])
        kb.op('pe', lambda e: e.matmul(ps[1][:, 0:256], lhsT=sel127[:], rhs=Fm[:].rearrange("p h t -> p (h t)"), start=True, stop=True),
              reads=[B_F, B_const], writes=[PB[1]])
        kb.op('dve', lambda e: e.tensor_copy(out=scan[:, 0, :], in_=ps[1][:, 0:256]), reads=[PB[1]], writes=[B_scan])
        kb.op('dve', lambda e: e.tensor_copy(out=offb[:], in_=scan[:, 0, :]), reads=[B_scan], writes=[B_scan])
        cur = 0
        sh = 1
        while sh < NT:
            nxt = 1 - cur
            sc3 = scan[:, cur, :].rearrange("p (h t) -> p h t", h=2)
            sn3 = scan[:, nxt, :].rearrange("p (h t) -> p h t", h=2)
            kb.op('dve', lambda e, sc3=sc3, sn3=sn3, sh=sh: e.tensor_copy(out=sn3[:, :, 0:sh], in_=sc3[:, :, 0:sh]),
                  reads=[B_scan], writes=[B_scan])
            kb.op('dve', lambda e, sc3=sc3, sn3=sn3, sh=sh: e.tensor_tensor(out=sn3[:, :, sh:NT], in0=sc3[:, :, sh:NT], in1=sc3[:, :, 0:NT - sh],
                                                                          op=ALU.add), reads=[B_scan], writes=[B_scan])
            cur = nxt
            sh *= 2
        kb.op('dve', lambda e: e.tensor_tensor(out=offb[:], in0=scan[:, cur, :], in1=offb[:], op=ALU.subtract), reads=[B_scan], writes=[B_scan])
        kb.op('dve', lambda e: e.tensor_tensor(out=Fm[:].rearrange("p h t -> p (h t)"), in0=Fm[:].rearrange("p h t -> p (h t)"), in1=offb[:],
                                               op=ALU.add), reads=[B_scan], writes=[B_F])
        kb.op('dve', lambda e: e.tensor_scalar(out=negF[:].rearrange("p h t -> p (h t)"), in0=Fm[:].rearrange("p h t -> p (h t)"),
                                               scalar1=-1.0, scalar2=None, op0=ALU.mult), reads=[B_F], writes=[B_F])

        for hh in range(2):
            kb.dma('sp', QT[:], FM_d[hh], lds, reads=[B_fm], writes=[B_QT])
            kb.dma('sp', KT[:], FM_d[2 + hh], lds, reads=[B_fm], writes=[B_KT])
            for i8 in range(8):
                kb.dma('sp', V[:, i8 * 16:(i8 + 1) * 16, :], TMv[:, i8 * 16:(i8 + 1) * 16, hh * 128:(hh + 1) * 128], lds, reads=[B_fm], writes=[B_V])
            NS = 4
            pairs = []
            for qg in range(32):
                pairs.append((qg, -1))
                pairs += [(qg, kt) for kt in range(4 * qg + 4)]
            LA = NS

            def s_stage(n):
                qg, kt = pairs[n]
                fb = qg % 2
                si = n % NS
                if kt < 0:
                    for c4 in range(4):
                        kb.op('pe', lambda e, c4=c4, qg=qg: e.matmul(
                            ps[si][:, c4 * 128:(c4 + 1) * 128], lhsT=Fm[:, hh, 4 * qg + c4:4 * qg + c4 + 1].to_broadcast([128, 128]),
                            rhs=ident[:], start=True, stop=True), reads=[B_F, B_const], writes=[PB[si]], sig=(c4 == 3))
                    kb.op('dve', lambda e, fb=fb: e.tensor_copy(out=fq[:, fb, :], in_=ps[si][:, :]), reads=[PB[si]], writes=[B_fq[fb]])
                    return
                kb.op('pe', lambda e, kt=kt, qg=qg, si=si: e.matmul(ps[si][:, :], lhsT=KT[:, kt * 128:(kt + 1) * 128],
                                                                  rhs=QT[:, qg * 512:(qg + 1) * 512], start=True, stop=True),
                      reads=[B_KT, B_QT], writes=[PB[si]])
                kb.op('dve', lambda e, si=si, fb=fb: e.scalar_tensor_tensor(out=lg[:, si, :], in0=ps[si][:, :], scalar=SCALE, in1=fq[:, fb, :],
                                                                          op0=ALU.mult, op1=ALU.add),
                      reads=[PB[si], B_fq[fb]], writes=[B_lg[si]])
                if kt >= 4 * qg:
                    dd = kt - 4 * qg
                    kb.op('dve', lambda e, si=si, dd=dd: e.tensor_tensor(out=lg[:, si, :], in0=lg[:, si, :], in1=masks[:, dd, :], op=ALU.add),
                          reads=[B_mask], writes=[B_lg[si]])
                kb.op('act', lambda e, si=si, kt=kt: e.activation(out=PT[:, si, :], in_=lg[:, si, :], func=AF.Exp, bias=negF[:, hh, kt:kt + 1]),
                      reads=[B_lg[si], B_F], writes=[B_PT[si]])

            def p_stage(n):
                qg, kt = pairs[n]
                if kt < 0:
                    return
                fb = qg % 2
                si = n % NS
                nk = 4 * qg + 4
                po, pd = (4, 5) if qg % 2 == 0 else (6, 7)
                last = (kt == nk - 1)
                kb.op('pe', lambda e, si=si, kt=kt, po=po, last=last: e.matmul(ps[po][:, :], lhsT=V[:, kt, :], rhs=PT[:, si, :],
                                                                            start=(kt == 0), stop=last),
                      reads=[B_V, B_PT[si]], writes=[PB[po]], sig=False)
                kb.op('pe', lambda e, si=si, kt=kt, pd=pd, last=last: e.matmul(ps[pd][:, :], lhsT=onesb[:], rhs=PT[:, si, :],
                                                                            start=(kt == 0), stop=last),
                      reads=[B_const, B_PT[si]], writes=[PB[pd]], sig=True)
                if last:
                    kb.op('dve', lambda e, pd=pd: e.reciprocal(out=rden[:], in_=ps[pd][:, :]), reads=[PB[pd]], writes=[B_rden])
                    kb.op('dve', lambda e, po=po, fb=fb: e.tensor_tensor(out=yst[:, fb, :], in0=ps[po][:, :], in1=rden[:], op=ALU.mult),
                          reads=[PB[po], B_rden], writes=[B_yst[fb]])
                    kb.dma('pool', YS[qg, hh], yst[:, fb, :], sts, reads=[B_yst[fb]], writes=[B_ysp])
                    kb.cc(YS[qg, hh], YG[qg, hh], ccs, reads=[B_ysp], writes=[B_yg])

            for n in range(LA):
                s_stage(n)
            for n in range(len(pairs)):
                p_stage(n)
                if n + LA < len(pairs):
                    s_stage(n + LA)
        kb.barrier()

    QW = 4096
    _es = contextlib.ExitStack()
    zq = _es.enter_context(nc.sbuf_tensor(_u("zq"), [128, QW + 3], BF16))
    zk = _es.enter_context(nc.sbuf_tensor(_u("zk"), [128, QW + 3], BF16))
    qT = _es.enter_context(nc.sbuf_tensor(_u("qT"), [128, QW], BF16))
    kT = _es.enter_context(nc.sbuf_tensor(_u("kT"), [128, QW], BF16))
    v1 = _es.enter_context(nc.sbuf_tensor(_u("v1"), [128, 32, 257], BF16))
    og = _es.enter_context(nc.sbuf_tensor(_u("og"), [128, 32, 256], BF16))
    gi = _es.enter_context(nc.sbuf_tensor(_u("gi"), [128, 32, 2], F32))
    gsm = _es.enter_context(nc.sbuf_tensor(_u("gsm"), [128, 8, 32], F32))
    cw = _es.enter_context(nc.sbuf_tensor(_u("cw"), [128, 8], F32))
    cb = _es.enter_context(nc.sbuf_tensor(_u("cb"), [128, 2], F32))
    ngt = _es.enter_context(nc.sbuf_tensor(_u("ngt"), [128, 256], F32))
    acc = _es.enter_context(nc.sbuf_tensor(_u("acc"), [128, 512], F32))
    trim = _es.enter_context(nc.sbuf_tensor(_u("trim"), [128, 128], F32))
    e1 = _es.enter_context(nc.sbuf_tensor(_u("e1"), [128, 128], F32))
    DT = _es.enter_context(nc.sbuf_tensor(_u("DT"), [128, 128], F32))
    SD = _es.enter_context(nc.sbuf_tensor(_u("SD"), [128, 128], BF16))
    qc = _es.enter_context(nc.sbuf_tensor(_u("qc"), [128, 257], F32))
    nd = _es.enter_context(nc.sbuf_tensor(_u("nd"), [128, 257], F32))
    sm = _es.enter_context(nc.sbuf_tensor(_u("sm"), [128, 8], F32))
    hh_ = _es.enter_context(nc.sbuf_tensor(_u("hh_"), [128, 256], F32))
    sq = _es.enter_context(nc.sbuf_tensor(_u("sq"), [128, 256], F32))
    go = _es.enter_context(nc.sbuf_tensor(_u("go"), [128, 256], F32))
    yy = _es.enter_context(nc.sbuf_tensor(_u("yy"), [128, 256], F32))
    kw = _es.enter_context(nc.sbuf_tensor(_u("kw"), [128, 128], BF16))
    Cs = _es.enter_context(nc.sbuf_tensor(_u("Cs"), [128, 257], F32))
    Cb = _es.enter_context(nc.sbuf_tensor(_u("Cb"), [128, 257], BF16))
    yTs = _es.enter_context(nc.sbuf_tensor(_u("yTs"), [128, 2, QW], BF16))
    with _es:
        lds = kb.slot("mlld")
        B_z, B_qk, B_v1, B_og, B_gi, B_gsm, B_c = Buf("z"), Buf("qk"), Buf("v1"), Buf("og"), Buf("gi"), Buf("gsm"), Buf("mc")
        B_acc, B_e1, B_DT, B_SD, B_qc, B_nd, B_sm, B_h, B_sq, B_go, B_yy, B_kw, B_C, B_Cb, B_yT = [Buf(n) for n in
            ("acc", "e1", "DT", "SD", "qc", "nd", "sm", "h", "sq", "go", "yy", "kw", "C", "Cb", "yT")]
        e1p = [e1, _es.enter_context(nc.sbuf_tensor(_u("e1b"), [128, 128], F32))]
        DTp = [DT, _es.enter_context(nc.sbuf_tensor(_u("DTb"), [128, 128], F32))]
        SDp = [SD, _es.enter_context(nc.sbuf_tensor(_u("SDb"), [128, 128], BF16))]
        kwp = [kw, _es.enter_context(nc.sbuf_tensor(_u("kwb"), [128, 128], BF16))]
        B_e1p = [Buf("e1a"), Buf("e1b")]
        B_DTp = [Buf("DTa"), Buf("DTb")]
        B_SDp = [Buf("SDa"), Buf("SDb")]
        B_kwp = [Buf("kwa"), Buf("kwb")]
        kb.dma('sp', cw[:], convw_d, lds, writes=[B_c])
        kb.dma('sp', cb[:], convb_d, lds, writes=[B_c])
        kb.dma('sp', ngt[:], normg_d.partition_broadcast(128), lds, writes=[B_c])
        kb.dma('sp', trim[:], mask_d[:, 0, 0:128], lds, writes=[B_c])
        kb.op('dve', lambda e: e.memset(Cs[:], 0.0), writes=[B_C])
        kb.op('dve', lambda e: e.memset(Cb[:], 0.0), writes=[B_Cb])
        kb.op('dve', lambda e: e.memset(zq[:, 0:3], 0.0), writes=[B_z])
        kb.op('dve', lambda e: e.memset(zk[:, 0:3], 0.0), writes=[B_z])
        for c in range(4):
            if c > 0:
                kb.op('dve', lambda e: e.tensor_copy(out=zq[:, 0:3], in_=zq[:, QW:QW + 3]), reads=[B_z], writes=[B_z])
                kb.op('dve', lambda e: e.tensor_copy(out=zk[:, 0:3], in_=zk[:, QW:QW + 3]), reads=[B_z], writes=[B_z])
            kb.dma('sp', zq[:, 3:3 + QW], FM_d[4][:, c * QW:(c + 1) * QW], lds, reads=[B_fm], writes=[B_z])
            kb.dma('sp', zk[:, 3:3 + QW], FM_d[5][:, c * QW:(c + 1) * QW], lds, reads=[B_fm], writes=[B_z])
            for i4 in range(4):
                tsl = slice(c * 32 + i4 * 8, c * 32 + (i4 + 1) * 8)
                kb.dma('sp', v1[:, i4 * 8:(i4 + 1) * 8, 0:256], TMv[:, tsl, 256:512], lds, reads=[B_fm], writes=[B_v1])
                kb.dma('sp', og[:, i4 * 8:(i4 + 1) * 8, :], TMv[:, tsl, 512:768], lds, reads=[B_fm], writes=[B_og])
                kb.dma('sp', gi[:, i4 * 8:(i4 + 1) * 8, :], GTv[:, tsl, 2:4], lds, reads=[B_fm], writes=[B_gi])
            kb.op('dve', lambda e: e.memset(v1[:, :, 256:257], 1.0), writes=[B_v1])
            for which, (zz, oo) in enumerate(((zq, qT), (zk, kT))):
                for pc in range(QW // 512):
                    t0 = pc * 512
                    kb.op('dve', lambda e, zz=zz, t0=t0: e.tensor_scalar(out=acc[:], in0=zz[:, t0:t0 + 512], scalar1=cw[:, which * 4:which * 4 + 1],
                                                                       scalar2=None, op0=ALU.mult), reads=[B_z, B_c], writes=[B_acc])
                    for j_ in range(1, 4):
                        kb.op('dve', lambda e, zz=zz, t0=t0, j_=j_: e.scalar_tensor_tensor(
                            out=acc[:], in0=zz[:, t0 + j_:t0 + j_ + 512], scalar=cw[:, which * 4 + j_:which * 4 + j_ + 1], in1=acc[:],
                            op0=ALU.mult, op1=ALU.add), reads=[B_z, B_c], writes=[B_acc])
                    if which == 0:
                        kb.op('act', lambda e, oo=oo, t0=t0: e.activation(out=oo[:, t0:t0 + 512], in_=acc[:], func=AF.Silu, bias=cb[:, 0:1]),
                              reads=[B_acc, B_c], writes=[B_qk])
                    else:
                        kb.op('act', lambda e: e.activation(out=acc[:], in_=acc[:], func=AF.Silu, bias=cb[:, 1:2]), reads=[B_c], writes=[B_acc])
                        kb.op('dve', lambda e, oo=oo, t0=t0: e.tensor_scalar(out=oo[:, t0:t0 + 512], in0=acc[:], scalar1=SCALE, scalar2=None, op0=ALU.mult),
                              reads=[B_acc], writes=[B_qk])
            giv = gi[:].rearrange("p t c -> p c t")
            kb.op('dve', lambda e: e.tensor_copy(out=gsm[:, 6, :], in_=giv[:, 1, :]), reads=[B_gi], writes=[B_gsm])
            kb.op('dve', lambda e: e.tensor_copy(out=gsm[:, 7, :], in_=giv[:, 0, :]), reads=[B_gi], writes=[B_gsm])
            kb.op('pe', lambda e: e.matmul(ps[0][:, 0:32], lhsT=tri[:], rhs=gsm[:, 6, :], start=True, stop=True), reads=[B_gsm, B_const], writes=[PB[0]])
            kb.op('pe', lambda e: e.matmul(ps[1][:, 0:32], lhsT=onesf[:], rhs=gsm[:, 6, :], start=True, stop=True), reads=[B_gsm, B_const], writes=[PB[1]])
            kb.op('dve', lambda e: e.tensor_copy(out=gsm[:, 0, :], in_=ps[0][:, 0:32]), reads=[PB[0]], writes=[B_gsm])
            kb.op('dve', lambda e: e.tensor_copy(out=gsm[:, 1, :], in_=ps[1][:, 0:32]), reads=[PB[1]], writes=[B_gsm])
            kb.op('dve', lambda e: e.tensor_tensor(out=gsm[:, 2, :], in0=gsm[:, 7, :], in1=gsm[:, 0, :], op=ALU.subtract), reads=[B_gsm], writes=[B_gsm])
            kb.op('dve', lambda e: e.tensor_tensor(out=gsm[:, 3, :], in0=gsm[:, 2, :], in1=gsm[:, 1, :], op=ALU.add), reads=[B_gsm], writes=[B_gsm])
            kb.op('act', lambda e: e.activation(out=gsm[:, 3, :], in_=gsm[:, 3, :], func=AF.Exp), reads=[B_gsm], writes=[B_gsm])
            kb.op('act', lambda e: e.activation(out=gsm[:, 4, :], in_=gsm[:, 0, :], func=AF.Exp), reads=[B_gsm], writes=[B_gsm])
            kb.op('act', lambda e: e.activation(out=gsm[:, 5, :], in_=gsm[:, 1, :], func=AF.Exp), reads=[B_gsm], writes=[B_gsm])
            def stage_a(ch):
                par = ch % 2
                cs = slice(ch * 128, (ch + 1) * 128)
                kb.op('pe', lambda e: e.matmul(ps[par][:, 0:128], lhsT=kT[:, cs], rhs=qT[:, cs], start=True, stop=True),
                      reads=[B_qk], writes=[PB[par]], sig=False)
                kb.op('pe', lambda e: e.matmul(ps[par][:, 128:256], lhsT=gsm[:, 0, ch:ch + 1].to_broadcast([128, 128]), rhs=ident[:],
                                               start=True, stop=True), reads=[B_gsm, B_const], writes=[PB[par]])
                kb.op('pe', lambda e: e.matmul(ps[2 + par][:, 0:128], lhsT=kT[:, cs], rhs=identb[:], start=True, stop=True),
                      reads=[B_qk, B_const], writes=[PB[2 + par]])
                kb.op('dve', lambda e: e.tensor_tensor(out=e1p[par][:], in0=ps[par][:, 128:256], in1=trim[:], op=ALU.add),
                      reads=[PB[par], B_c], writes=[B_e1p[par]])
                kb.op('act', lambda e: e.activation(out=DTp[par][:], in_=e1p[par][:], func=AF.Exp, bias=gsm[:, 2, ch:ch + 1]),
                      reads=[B_e1p[par], B_gsm], writes=[B_DTp[par]])
                kb.op('dve', lambda e: e.tensor_tensor(out=SDp[par][:], in0=ps[par][:, 0:128], in1=DTp[par][:], op=ALU.mult),
                      reads=[PB[par], B_DTp[par]], writes=[B_SDp[par]])
                kb.op('act', lambda e: e.activation(out=kwp[par][:], in_=ps[2 + par][:, 0:128], func=AF.Identity, scale=gsm[:, 3, ch:ch + 1]),
                      reads=[PB[2 + par], B_gsm], writes=[B_kwp[par]])

            def stage_b(ch):
                par = ch % 2
                cs = slice(ch * 128, (ch + 1) * 128)
                kb.op('pe', lambda e: e.matmul(ps[4][:, 0:257], lhsT=SDp[par][:], rhs=v1[:, ch, :], start=True, stop=True),
                      reads=[B_SDp[par], B_v1], writes=[PB[4]])
                kb.op('pe', lambda e: e.matmul(ps[5][:, 0:257], lhsT=qT[:, cs], rhs=Cb[:], start=True, stop=True),
                      reads=[B_qk, B_Cb], writes=[PB[5]])
                kb.op('pe', lambda e: e.matmul(ps[7][:, 0:257], lhsT=kwp[par][:], rhs=v1[:, ch, :], start=True, stop=True),
                      reads=[B_kwp[par], B_v1], writes=[PB[7]])
                if ch > 0:
                    stage_t(ch - 1)
                kb.op('act', lambda e: e.activation(out=qc[:], in_=ps[5][:, 0:257], func=AF.Identity, scale=gsm[:, 4, ch:ch + 1]),
                      reads=[PB[5], B_gsm], writes=[B_qc])
                kb.op('dve', lambda e: e.scalar_tensor_tensor(out=Cs[:], in0=Cs[:], scalar=gsm[:, 5, ch:ch + 1], in1=ps[7][:, 0:257],
                                                             op0=ALU.mult, op1=ALU.add), reads=[PB[7], B_gsm], writes=[B_C])
                kb.op('dve', lambda e: e.tensor_copy(out=Cb[:], in_=Cs[:]), reads=[B_C], writes=[B_Cb])
                kb.op('dve', lambda e: e.tensor_tensor(out=nd[:], in0=ps[4][:, 0:257], in1=qc[:], op=ALU.add), reads=[PB[4], B_qc], writes=[B_nd])
                kb.op('act', lambda e: e.activation(out=sm[:, 5:6], in_=nd[:, 256:257], func=AF.Abs), reads=[B_nd], writes=[B_sm])
                kb.op('dve', lambda e: e.tensor_scalar(out=sm[:, 0:1], in0=sm[:, 5:6], scalar1=1.0, scalar2=None, op0=ALU.max),
                      reads=[B_sm], writes=[B_sm])
                kb.op('dve', lambda e: e.reciprocal(out=sm[:, 1:2], in_=sm[:, 0:1]), reads=[B_sm], writes=[B_sm])
                kb.op('dve', lambda e: e.tensor_scalar(out=hh_[:], in0=nd[:, 0:256], scalar1=sm[:, 1:2], scalar2=None, op0=ALU.mult),
                      reads=[B_nd, B_sm], writes=[B_h])
                kb.op('act', lambda e: e.activation(out=sq[:], in_=hh_[:], func=AF.Square, accum_out=sm[:, 2:3]), reads=[B_h], writes=[B_sq, B_sm])
                kb.op('act', lambda e: e.activation(out=sm[:, 4:5], in_=sm[:, 2:3], func=AF.Sqrt, scale=1.0 / 256.0, bias=epsc[:, 0:1]),
                      reads=[B_sm, B_const], writes=[B_sm])
                kb.op('dve', lambda e: e.reciprocal(out=sm[:, 3:4], in_=sm[:, 4:5]), reads=[B_sm], writes=[B_sm])
                kb.op('dve', lambda e: e.tensor_tensor(out=go[:], in0=og[:, ch, :], in1=ngt[:], op=ALU.mult), reads=[B_og, B_c], writes=[B_go])
                kb.op('dve', lambda e: e.scalar_tensor_tensor(out=yy[:], in0=hh_[:], scalar=sm[:, 3:4], in1=go[:], op0=ALU.mult, op1=ALU.mult),
                      reads=[B_h, B_sm, B_go], writes=[B_yy])

            def stage_t(ch):
                cs = slice(ch * 128, (ch + 1) * 128)
                for half in range(2):
                    kb.op('pe', lambda e, half=half: e.transpose(ps[6][:, half * 128:(half + 1) * 128], yy[:, half * 128:(half + 1) * 128], ident[:]),
                          reads=[B_yy, B_const], writes=[PB[6]], sig=(half == 1))
                kb.op('act', lambda e: e.activation(out=yTs[:, :, cs], in_=ps[6][:, 0:256].rearrange("p (a b) -> p a b", a=2), func=AF.Copy),
                      reads=[PB[6]], writes=[B_yT])

            stage_a(0)
            for ch in range(32):
                if ch + 1 < 32:
                    stage_a(ch + 1)
                stage_b(ch)
            stage_t(31)
            for tb in range(8):
                for a_ in range(2):
                    kb.dma('pool', YS[c * 8 + tb, 2 + a_], yTs[:, a_, tb * G:(tb + 1) * G], sts, reads=[B_yT], writes=[B_ysp])
                    kb.cc(YS[c * 8 + tb, 2 + a_], YG[c * 8 + tb, 2 + a_], ccs, reads=[B_ysp], writes=[B_yg])
        kb.barrier()


_CACHE = {}


def kernel(**inputs):
    f32 = np.float32
    x = np.asarray(inputs["x"], f32)
    c = np.asarray(inputs["c"], f32)
    sq = lambda k: np.ascontiguousarray(np.asarray(inputs[k], f32)[0])
    if "nc" not in _CACHE:
        _CACHE["nc"] = build()
    nc = _CACHE["nc"]
    lnp = np.stack([sq("ln1_g"), sq("ln1_b"), sq("ln2_g"), sq("ln2_b"), sq("ln3_g"), sq("ln3_b")], 0)
    gbias = np.concatenate([sq("fox_f_bias"), sq("mlstm_i_bias"), sq("mlstm_f_bias")])[None, :]
    conv_w = sq("mlstm_conv_w")
    conv_b = sq("mlstm_conv_b")
    norm_g = sq("mlstm_norm_g")
    b_adaT = np.ascontiguousarray(sq("b_ada").reshape(144, 128).T)
    ident = np.eye(128, dtype=f32)
    tri = np.triu(np.ones((128, 128), f32))
    sel = np.zeros((128, 128), f32)
    sel[127, :] = 1.0
    kk = np.arange(128)[:, None, None] + 128 * np.arange(4)[None, :, None]
    qq = np.arange(512)[None, None, :]
    masks = np.where(kk <= qq, 0.0, NEG).astype(f32)
    w_in_full = sq("w_in")
    shared = {"w_ada": sq("w_ada"), "b_adaT": b_adaT, "ffn1_w_in": sq("ffn1_w_in"), "ffn1_w_out": sq("ffn1_w_out"),
              "w_out": sq("w_out"), "ffn2_w_in": sq("ffn2_w_in"), "ffn2_w_out": sq("ffn2_w_out"),
              "lnp": lnp, "ident": ident, "tri": tri, "sel127": sel, "masks": masks}
    in_maps = []
    for core in range(8):
        b, j = core // 4, core % 4
        m = dict(shared)
        m["x"] = np.ascontiguousarray(x[b, j * TPC:(j + 1) * TPC, :])
        m["cT"] = np.ascontiguousarray(c[b].reshape(16, 128).T)
        cw = np.stack([conv_w[:, j * 128:(j + 1) * 128].T, conv_w[:, 512 + j * 128:512 + (j + 1) * 128].T], 1)
        m["convw"] = np.ascontiguousarray(cw.reshape(128, 8))
        m["convb"] = np.ascontiguousarray(np.stack([conv_b[j * 128:(j + 1) * 128], conv_b[512 + j * 128:512 + (j + 1) * 128]], 1))
        m["normg"] = np.ascontiguousarray(norm_g[j * 256:(j + 1) * 256][None, :])
        cols = np.concatenate([np.arange(2 * j * 128, 2 * j * 128 + 256), np.arange(1024 + 2 * j * 128, 1024 + 2 * j * 128 + 256),
                               np.arange(3080 + j * 128, 3080 + (j + 1) * 128), np.arange(3592 + j * 128, 3592 + (j + 1) * 128),
                               np.arange(2048 + 2 * j * 128, 2048 + 2 * j * 128 + 256), np.arange(4104 + j * 256, 4104 + (j + 1) * 256),
                               np.arange(5136 + j * 256, 5136 + (j + 1) * 256),
                               np.array([3072 + 2 * j, 3072 + 2 * j + 1, 5128 + j, 5132 + j])])
        m["win_own"] = np.ascontiguousarray(w_in_full[:, cols])
        m["gb_own"] = np.ascontiguousarray(gbias[0, [2 * j, 2 * j + 1, 8 + j, 12 + j]][None, :])
        in_maps.append(m)
    res = run_bass_kernel_spmd(nc, in_maps, core_ids=list(range(8)))
    out = np.empty((2, S, D), f32)
    for core in range(8):
        b, j = core // 4, core % 4
        out[b, j * TPC:(j + 1) * TPC, :] = res.results[core]["out"]
    if os.environ.get("KSTOP", ""):
        _CACHE["dbg"] = [{k: v for k, v in r.items() if k != "out"} for r in res.results]
    return out
```

```python
import contextlib
import os
import numpy as np
import ml_dtypes
import concourse.bass as bass
import concourse.mybir as mybir
from concourse.bass_utils import run_bass_kernel_spmd

F32 = mybir.dt.float32
BF16 = mybir.dt.bfloat16
AF = mybir.ActivationFunctionType
ALU = mybir.AluOpType

D = 2048
S = 16384
TPC = 4096
G = 512
NG = 8
DFF = 5632
NFF = 44
INW = 6160
ALPHA = 2.0 ** 0.25
EPS = 1e-5
NEG = -30000.0
RG = [[0, 1, 2, 3], [4, 5, 6, 7]]


_UC = [0]


def _u(name):
    _UC[0] += 1
    return "%s_u%d" % (name, _UC[0])


class Buf:
    def __init__(self, name):
        self.name = name
        self.w = None
        self.r = []


class Slot:
    def __init__(self, nc, name):
        self.sem = nc.alloc_semaphore(name)
        self.name = name
        self.cnt = 0


class KB:
    def __init__(self, nc):
        self.nc = nc
        self.eng = {'pe': nc.tensor, 'act': nc.scalar, 'dve': nc.vector, 'pool': nc.gpsimd, 'sp': nc.sync}
        self.sem = {e: nc.alloc_semaphore('c_' + e) for e in ('pe', 'act', 'dve', 'pool')}
        self.cnt = {e: 0 for e in self.sem}
        self.seen = {e: {} for e in self.eng}
        self.pend = []
        self.slots = []
        self.nslot = 0

    def slot(self, name):
        s = Slot(self.nc, 'd_%s_%d' % (name, self.nslot))
        self.nslot += 1
        self.slots.append(s)
        return s

    def wait(self, e, tok):
        if tok is None:
            return
        sem, key, val = tok
        if key == e and e == 'pe':
            return
        if self.seen[e].get(key, 0) >= val:
            return
        self.eng[e].wait_ge(sem, val)
        self.seen[e][key] = val

    def _deps(self, e, reads, writes):
        for b in reads:
            self.wait(e, b.w)
        for b in writes:
            self.wait(e, b.w)
            for t in b.r:
                self.wait(e, t)

    def op(self, e, fn, reads=(), writes=(), sig=True):
        self._deps(e, reads, writes)
        ins = fn(self.eng[e])
        if sig:
            self.cnt[e] += 1
            ins.then_inc(self.sem[e], 1)
            tok = (self.sem[e], e, self.cnt[e])
            if e == 'pe':
                for (rb, wb) in self.pend:
                    for b in rb:
                        b.r.append(tok)
                    for b in wb:
                        b.w = tok
                        b.r = []
                self.pend = []
            for b in reads:
                b.r.append(tok)
            for b in writes:
                b.w = tok
                b.r = []
            return tok
        assert e == 'pe'
        self.pend.append((list(reads), list(writes)))
        return None

    def dma(self, q, out, in_, slot, reads=(), writes=(), **kw):
        self._deps(q, reads, writes)
        self.eng[q].dma_start(out=out, in_=in_, **kw).then_inc(slot.sem, 16)
        slot.cnt += 16
        tok = (slot.sem, slot.name, slot.cnt)
        for b in reads:
            b.r.append(tok)
        for b in writes:
            b.w = tok
            b.r = []
        return tok

    def cc(self, ins, outs, slot, reads=(), writes=()):
        self._deps('pool', reads, writes)
        self.nc.gpsimd.collective_compute("AllGather", ALU.bypass, replica_groups=RG,
                                          ins=[ins.opt()], outs=[outs.opt()]).then_inc(slot.sem, 1)
        slot.cnt += 1
        tok = (slot.sem, slot.name, slot.cnt)
        for b in reads:
            b.r.append(tok)
        for b in writes:
            b.w = tok
            b.r = []
        return tok

    def barrier(self, engines=('pe', 'act', 'dve', 'pool', 'sp'), exclude=()):
        toks = [(self.sem[e], e, self.cnt[e]) for e in self.sem if self.cnt[e] > 0]
        toks += [(s.sem, s.name, s.cnt) for s in self.slots if s.cnt > 0 and s not in exclude]
        for e in engines:
            for t in toks:
                self.wait(e, t)


def build(dbg=False):
    nc = bass.Bass("TRN2", target_bir_lowering=False)
    kb = KB(nc)

    def din(name, shape, dt=F32):
        return nc.dram_tensor(name, shape, dt, kind="ExternalInput").ap()

    def dscr(name, shape, dt):
        return nc.dram_tensor(name, shape, dt, kind="Internal").ap()

    x_in = din("x", [TPC, D])
    cT = din("cT", [128, 16])
    w_ada = din("w_ada", [D, 9 * D])
    b_adaT = din("b_adaT", [128, 144])
    wf = {"w1a": din("ffn1_w_in", [D, 2 * DFF]), "w2a": din("ffn1_w_out", [DFF, D]),
          "wout": din("w_out", [D, D]),
          "w1b": din("ffn2_w_in", [D, 2 * DFF]), "w2b": din("ffn2_w_out", [DFF, D])}
    lnp = din("lnp", [6, D])
    convw = din("convw", [128, 8])
    convb = din("convb", [128, 2])
    normg = din("normg", [1, 256])
    wino_f = din("win_own", [D, 1540])
    gbo = din("gb_own", [1, 4])
    wino = dscr("wino_bf", [D, 1540], BF16)
    ident_d = din("ident", [128, 128])
    tri_d = din("tri", [128, 128])
    sel_d = din("sel127", [128, 128])
    mask_d = din("masks", [128, 4, 512])
    out_d = nc.dram_tensor("out", [TPC, D], F32, kind="ExternalOutput").ap()

    wb = {k: dscr(k + "_bf", list(v.shape), BF16) for k, v in wf.items()}
    x1_d = dscr("x1s", [TPC, D], F32)
    gate_scr = dscr("gate_scr", [48, 128], F32)

    def sb(name, shape, dt):
        return nc.alloc_sbuf_tensor(_u(name), shape, dt)

    ident = sb("ident", [128, 128], F32)
    identb = sb("identb", [128, 128], BF16)
    onesb = sb("onesb", [128, 128], BF16)
    onesf = sb("onesf", [128, 128], F32)
    epsc = sb("epsc", [128, 1], F32)
    tri = sb("tri", [128, 128], F32)
    sel127 = sb("sel127", [128, 128], F32)
    adaT = sb("adaT", [128, 144], F32)
    modsc = sb("modsc", [128, 3, 16], F32)
    B_const = Buf("const")
    B_ada = Buf("ada")

    ps = [nc.alloc_psum_tensor("ps%d" % i, [128, 512], F32) for i in range(8)]
    PB = [Buf("ps%d" % i) for i in range(8)]

    ld = kb.slot("const")
    for t, dsrc in ((ident, ident_d), (tri, tri_d), (sel127, sel_d)):
        kb.dma('sp', t[:], dsrc, ld, writes=[B_const])
    kb.op('dve', lambda e: e.tensor_copy(out=identb[:], in_=ident[:]), reads=[B_const], writes=[B_const])
    kb.op('dve', lambda e: e.memset(onesb[:], 1.0), writes=[B_const])
    kb.op('dve', lambda e: e.memset(onesf[:], 1.0), writes=[B_const])
    kb.op('dve', lambda e: e.memset(epsc[:], EPS), writes=[B_const])

    WB = {k: Buf("wb_" + k) for k in wf}
    wslot = {k: kb.slot("wc_" + k) for k in wf}

    def cast_weight(k, nchunk):
        src, dst = wf[k], wb[k]
        rows = src.shape[0]
        r = rows // nchunk
        for i in range(nchunk):
            kb.dma('pool', dst[i * r:(i + 1) * r, :], src[i * r:(i + 1) * r, :], wslot[k], writes=[],
                   max_dma_last_dim=8192)
        WB[k].w = (wslot[k].sem, wslot[k].name, wslot[k].cnt)

    cast_weight("w1a", 8)
    cast_weight("w2a", 4)
    wf["wino"] = wino_f
    wb["wino"] = wino
    WB["wino"] = Buf("wb_wino")
    wslot["wino"] = kb.slot("wc_wino")
    cast_weight("wino", 2)

    _es = contextlib.ExitStack()
    aw0 = _es.enter_context(nc.sbuf_tensor(_u("ada_w0"), [128, 16, 512], F32))
    aw1 = _es.enter_context(nc.sbuf_tensor(_u("ada_w1"), [128, 16, 512], F32))
    scT = _es.enter_context(nc.sbuf_tensor(_u("scT"), [128, 16], F32))
    badaT = _es.enter_context(nc.sbuf_tensor(_u("badaT"), [128, 144], F32))
    gT = _es.enter_context(nc.sbuf_tensor(_u("gT"), [128, 48], F32))
    gTT = _es.enter_context(nc.sbuf_tensor(_u("gTT"), [48, 128], F32))
    with _es:
        aws = [aw0, aw1]
        AWB = [Buf("aw0"), Buf("aw1")]
        awslot = [kb.slot("aw0"), kb.slot("aw1")]
        B_sc = Buf("scT")
        ld2 = kb.slot("ld2")
        ld3 = kb.slot("ld3")
        kb.dma('sp', scT[:], cT, ld2, writes=[B_sc])
        kb.dma('sp', badaT[:], b_adaT, ld2, writes=[B_sc])
        kb.op('act', lambda e: e.activation(out=scT[:], in_=scT[:], func=AF.Silu), reads=[B_sc], writes=[B_sc])
        w_ada_v = w_ada.rearrange("(kc p) n -> p kc n", p=128)
        for blk in range(36):
            a = aws[blk % 2]
            kb.dma('sp', a[:], w_ada_v[:, :, blk * 512:(blk + 1) * 512], awslot[blk % 2], writes=[AWB[blk % 2]])
            for cc in range(4):
                ch = blk * 4 + cc
                for kc in range(16):
                    last = (kc == 15)
                    kb.op('pe', lambda e, a=a, cc=cc, kc=kc, ch=ch: e.matmul(
                        ps[0][:, ch:ch + 1], lhsT=a[:, kc, cc * 128:(cc + 1) * 128], rhs=scT[:, kc:kc + 1],
                        start=(kc == 0), stop=(kc == 15)),
                        reads=[AWB[blk % 2], B_sc], writes=[PB[0]], sig=(last and cc == 3))
        kb.op('dve', lambda e: e.tensor_tensor(out=adaT[:], in0=ps[0][:, 0:144], in1=badaT[:], op=ALU.add),
              reads=[PB[0], B_sc], writes=[B_ada])
        for k in range(3):
            kb.op('dve', lambda e, k=k: e.tensor_scalar(out=modsc[:, k, :], in0=adaT[:, (3 * k + 1) * 16:(3 * k + 2) * 16],
                                                       scalar1=1.0, scalar2=None, op0=ALU.add),
                  reads=[B_ada], writes=[B_ada])
        for k, coef in enumerate((0.5, 1.0, 0.5)):
            kb.op('dve', lambda e, k=k, coef=coef: e.tensor_scalar(
                out=gT[:, k * 16:(k + 1) * 16], in0=adaT[:, (3 * k + 2) * 16:(3 * k + 3) * 16],
                scalar1=1.0, scalar2=coef, op0=ALU.add, op1=ALU.mult), reads=[B_ada], writes=[B_sc])
        kb.op('pe', lambda e: e.transpose(ps[1][0:48, 0:128], gT[:, 0:48], ident[:]), reads=[B_sc, B_const], writes=[PB[1]])
        kb.op('dve', lambda e: e.tensor_copy(out=gTT[:], in_=ps[1][0:48, 0:128]), reads=[PB[1]], writes=[B_sc])
        B_gscr = Buf("gscr")
        kb.dma('sp', gate_scr, gTT[:], ld3, reads=[B_sc], writes=[B_gscr])
        kb.barrier(exclude=list(wslot.values()))

    cast_weight("wout", 2)
    cast_weight("w1b", 8)
    cast_weight("w2b", 4)
    STOP = os.environ.get("KSTOP", "")
    if STOP == "cast":
        kb.barrier()
        return nc

    gate_rows = gate_scr.rearrange("(k c) p -> k (c p)", k=3)

    def ffn_phase(phase):
        _es = contextlib.ExitStack()
        xt = _es.enter_context(nc.sbuf_tensor(_u("xt"), [128, 4, D], F32))
        xn = _es.enter_context(nc.sbuf_tensor(_u("xn"), [128, D], F32))
        hT = _es.enter_context(nc.sbuf_tensor(_u("hT"), [128, 16, G], BF16))
        HT = _es.enter_context(nc.sbuf_tensor(_u("HT"), [128, NFF, G], BF16))
        wA0 = _es.enter_context(nc.sbuf_tensor(_u("wA0"), [128, 16, 512], BF16))
        wA1 = _es.enter_context(nc.sbuf_tensor(_u("wA1"), [128, 16, 512], BF16))
        wA2 = _es.enter_context(nc.sbuf_tensor(_u("wA2"), [128, 16, 512], BF16))
        wB0 = _es.enter_context(nc.sbuf_tensor(_u("wB0"), [128, 1024], BF16))
        wB1 = _es.enter_context(nc.sbuf_tensor(_u("wB1"), [128, 1024], BF16))
        wB2 = _es.enter_context(nc.sbuf_tensor(_u("wB2"), [128, 1024], BF16))
        wB3 = _es.enter_context(nc.sbuf_tensor(_u("wB3"), [128, 1024], BF16))
        tabs = _es.enter_context(nc.sbuf_tensor(_u("tabs"), [128, 3, D], F32))
        tmp = _es.enter_context(nc.sbuf_tensor(_u("tmp"), [128, 2, 512], F32))
        sg = _es.enter_context(nc.sbuf_tensor(_u("sg"), [128, 2, 512], F32))
        stat = _es.enter_context(nc.sbuf_tensor(_u("stat"), [128, 4, 6], F32))
        mv = _es.enter_context(nc.sbuf_tensor(_u("mv"), [128, 4], F32))
        gst = _es.enter_context(nc.sbuf_tensor(_u("gst"), [128, 4, 16], F32))
        gso = _es.enter_context(nc.sbuf_tensor(_u("gso"), [128, 4, 16], F32))
        wg = _es.enter_context(nc.sbuf_tensor(_u("wg"), [128, 16, 16], BF16))
        with _es:
            stage = HT[:, 0:24, :]
            wA = [wA0, wA1, wA2]
            WA = [Buf("wA%d" % i) for i in range(3)]
            wAs = [kb.slot("wA%d" % i) for i in range(3)]
            wBt = [wB0, wB1, wB2, wB3]
            WBb = [Buf("wB%d" % i) for i in range(4)]
            wBs = [kb.slot("wB%d" % i) for i in range(4)]
            st = {"a": 0, "b": 0}
            B_xt = [Buf("xt%d" % i) for i in range(4)]
            B_xn = Buf("xn")
            B_hT = Buf("hT")
            B_HT = [Buf("HT%d" % i) for i in range(NFF)]
            B_tabs = Buf("tabs")
            B_tmp = [Buf("tmp0"), Buf("tmp1")]
            B_sg = [Buf("sg0"), Buf("sg1")]
            B_stat = Buf("stat")
            B_stage = Buf("stage")
            B_gst = Buf("gst")
            B_gso = Buf("gso")
            B_gso = Buf("gso")
            B_wg = Buf("wg")
            xs = kb.slot("xld")
            tbs = kb.slot("tabs")
            sts = kb.slot("store")
            sts_a = kb.slot("store_a")
            sts_b = kb.slot("store_b")
            sts_t = kb.slot("store_t")
            wgs = kb.slot("wg")
            ccs = kb.slot("cc")
            B_x1d = Buf("x1d")

            def loadA(src_ap_fn, wkey):
                i = st["a"] % 3
                st["a"] += 1
                for (o, s_) in src_ap_fn(wA[i]):
                    kb.dma('sp', o, s_, wAs[i], reads=[WB[wkey]], writes=[WA[i]])
                return i

            def load_tabs(rows):
                for i, r in enumerate(rows):
                    kb.dma('sp', tabs[:, i, :], r.partition_broadcast(128), tbs, reads=[B_gscr], writes=[B_tabs])

            def ln_stats(tt, col):
                for q in range(4):
                    kb.op('dve', lambda e, q=q: e.bn_stats(out=stat[:, q, :], in_=xt[:, tt, q * 512:(q + 1) * 512]),
                          reads=[B_xt[tt]], writes=[B_stat])
                kb.op('dve', lambda e: e.bn_aggr(out=mv[:, 0:2], in_=stat[:].rearrange("p a b -> p (a b)")),
                      reads=[B_stat], writes=[B_stat])
                kb.op('act', lambda e: e.activation(out=mv[:, 3:4], in_=mv[:, 1:2], func=AF.Sqrt, bias=epsc[:, 0:1]), reads=[B_stat, B_const], writes=[B_stat])
                kb.op('dve', lambda e: e.reciprocal(out=mv[:, 2:3], in_=mv[:, 3:4]), reads=[B_stat], writes=[B_stat])

            def mod_transpose(k):
                for tt in range(4):
                    ln_stats(tt, 0)
                    kb.op('dve', lambda e, tt=tt: e.tensor_scalar(out=xn[:], in0=xt[:, tt, :], scalar1=mv[:, 0:1],
                                                                 scalar2=mv[:, 2:3], op0=ALU.subtract, op1=ALU.mult),
                          reads=[B_xt[tt], B_stat], writes=[B_xn])
                    for q in range(4):
                        pi = 4 + (q % 2)
                        for c4 in range(4):
                            kc = q * 4 + c4
                            kb.op('pe', lambda e, kc=kc, c4=c4, pi=pi: e.transpose(
                                ps[pi][:, c4 * 128:(c4 + 1) * 128], xn[:, kc * 128:(kc + 1) * 128], ident[:]),
                                reads=[B_xn, B_const], writes=[PB[pi]], sig=(c4 == 3))
                        for c4 in range(4):
                            kc = q * 4 + c4
                            kb.op('act', lambda e, kc=kc, c4=c4, pi=pi, tt=tt: e.activation(
                                out=hT[:, kc, tt * 128:(tt + 1) * 128], in_=ps[pi][:, c4 * 128:(c4 + 1) * 128],
                                func=AF.Identity, scale=modsc[:, k, kc:kc + 1],
                                bias=adaT[:, (3 * k) * 16 + kc:(3 * k) * 16 + kc + 1]),
                                reads=[PB[pi], B_ada], writes=[B_hT])

            def ffn(w1k, w2k, k, gt_row, lng_row, lnb_row):
                w1 = wb[w1k].rearrange("(kc p) n -> p kc n", p=128)
                w2 = wb[w2k]
                mod_transpose(k)
                load_tabs([gate_rows[gt_row:gt_row + 1, :], lnp[lng_row:lng_row + 1, :], lnp[lnb_row:lnb_row + 1, :]])
                def srcs(i2):
                    return lambda t: [(t[:, :, 0:256], w1[:, :, i2 * 256:(i2 + 1) * 256]),
                                      (t[:, :, 256:512], w1[:, :, DFF + i2 * 256:DFF + (i2 + 1) * 256])]
                pre = [loadA(srcs(0), w1k), loadA(srcs(1), w1k)]
                for i2 in range(22):
                    wi = pre[i2]
                    if i2 + 2 < 22:
                        pre.append(loadA(srcs(i2 + 2), w1k))
                    for c in range(2):
                        i = i2 * 2 + c
                        pg, pu = (0, 1) if (i % 2 == 0) else (2, 3)
                        for kc in range(16):
                            kb.op('pe', lambda e, wi=wi, kc=kc, c=c, pg=pg: e.matmul(
                                ps[pg][:, :], lhsT=wA[wi][:, kc, c * 128:(c + 1) * 128], rhs=hT[:, kc, :],
                                start=(kc == 0), stop=(kc == 15)), reads=[WA[wi], B_hT], writes=[PB[pg]], sig=(kc == 15))
                        for kc in range(16):
                            kb.op('pe', lambda e, wi=wi, kc=kc, c=c, pu=pu: e.matmul(
                                ps[pu][:, :], lhsT=wA[wi][:, kc, 256 + c * 128:256 + (c + 1) * 128], rhs=hT[:, kc, :],
                                start=(kc == 0), stop=(kc == 15)), reads=[WA[wi], B_hT], writes=[PB[pu]], sig=(kc == 15))
                        kb.op('act', lambda e, pg=pg, i=i: e.activation(out=sg[:, i % 2, :], in_=ps[pg][:, :], func=AF.Silu),
                              reads=[PB[pg]], writes=[B_sg[i % 2]])
                        kb.op('dve', lambda e, pu=pu, i=i: e.tensor_tensor(out=HT[:, i, :], in0=sg[:, i % 2, :], in1=ps[pu][:, :],
                                                                          op=ALU.mult),
                              reads=[PB[pu], B_sg[i % 2]], writes=[B_HT[i]] + ([B_stage] if i < 24 else []))
                def loadB(dh, i):
                    s_ = st["b"] % 4
                    st["b"] += 1
                    kb.dma('sp', wBt[s_][:], w2[i * 128:(i + 1) * 128, dh * 1024:(dh + 1) * 1024], wBs[s_],
                           reads=[WB[w2k]], writes=[WBb[s_]])
                    return s_
                seq = [(dh, i) for dh in range(2) for i in range(NFF)]
                preb = [loadB(*seq[0]), loadB(*seq[1]), loadB(*seq[2])]
                for n, (dh, i) in enumerate(seq):
                    s_ = preb[n]
                    if n + 3 < len(seq):
                        preb.append(loadB(*seq[n + 3]))
                    for tt in range(4):
                        for dgi in range(2):
                            pi = tt * 2 + dgi
                            kb.op('pe', lambda e, s_=s_, tt=tt, dgi=dgi, pi=pi, i=i: e.matmul(
                                ps[pi][:, :], lhsT=HT[:, i, tt * 128:(tt + 1) * 128], rhs=wBt[s_][:, dgi * 512:(dgi + 1) * 512],
                                start=(i == 0), stop=(i == NFF - 1)),
                                reads=[WBb[s_], B_HT[i]], writes=[PB[pi]], sig=(i == NFF - 1 or (tt == 3 and dgi == 1)))
                    if i == NFF - 1:
                        for tt in range(4):
                            for dgi in range(2):
                                pi = tt * 2 + dgi
                                c0 = dh * 1024 + dgi * 512
                                residual(tt, pi, c0)
                for tt in range(4):
                    ln_affine(tt)

            def residual(tt, pi, c0):
                j = pi % 2
                kb.op('dve', lambda e: e.tensor_tensor(out=tmp[:, j, :], in0=ps[pi][:, :], in1=tabs[:, 0, c0:c0 + 512], op=ALU.mult),
                      reads=[PB[pi], B_tabs], writes=[B_tmp[j]])
                kb.op('dve', lambda e: e.scalar_tensor_tensor(out=xt[:, tt, c0:c0 + 512], in0=xt[:, tt, c0:c0 + 512], scalar=ALPHA,
                                                             in1=tmp[:, j, :], op0=ALU.mult, op1=ALU.add),
                      reads=[B_tmp[j]], writes=[B_xt[tt]])

            def ln_affine(tt):
                ln_stats(tt, 0)
                kb.op('dve', lambda e: e.tensor_scalar(out=xt[:, tt, :], in0=xt[:, tt, :], scalar1=mv[:, 0:1], scalar2=mv[:, 2:3],
                                                       op0=ALU.subtract, op1=ALU.mult), reads=[B_stat], writes=[B_xt[tt]])
                kb.op('dve', lambda e: e.tensor_tensor(out=xt[:, tt, :], in0=xt[:, tt, :], in1=tabs[:, 1, :], op=ALU.mult),
                      reads=[B_tabs], writes=[B_xt[tt]])
                kb.op('dve', lambda e: e.tensor_tensor(out=xt[:, tt, :], in0=xt[:, tt, :], in1=tabs[:, 2, :], op=ALU.add),
                      reads=[B_tabs], writes=[B_xt[tt]])


            def send_h2(g):
                mod_transpose(1)
                kb.dma('pool', H2s[g].rearrange("kc p t -> p kc t"), hT[:], sts_a, reads=[B_hT], writes=[B_h2s[g]])
                for kc in range(16):
                    kb.cc(H2s[g][kc], H2g[g][kc], ccs, reads=[B_h2s[g]], writes=[B_h2g[g]])

            def attn_out(g):
                woutv = wb["wout"].rearrange("(kc p) n -> p kc n", p=128)
                for c in range(4):
                    pass
                for rb in range(4):
                    kb.dma('sp', hT[:].rearrange("p (r rb) t -> p rb r t", rb=4)[:, rb], myY[g, rb].rearrange("(r p) t -> p r t", p=128), xs,
                           reads=[B_yg], writes=[B_hT])
                load_tabs([gate_rows[1:2, :], lnp[2:3, :], lnp[3:4, :]])
                fch = []
                for r in range(4):
                    fch += [2 * r, 2 * r + 1, 8 + 2 * r, 9 + 2 * r]
                for dg in range(4):
                    wi = loadA(lambda t, dg=dg: [(t[:], woutv[:, :, dg * 512:(dg + 1) * 512])], "wout")
                    for tt in range(4):
                        pi = 4 * (dg % 2) + tt
                        for ch in range(16):
                            kb.op('pe', lambda e, wi=wi, ch=ch, tt=tt, pi=pi: e.matmul(
                                ps[pi][:, :], lhsT=hT[:, ch, tt * 128:(tt + 1) * 128], rhs=wA[wi][:, fch[ch], :],
                                start=(ch == 0), stop=(ch == 15)), reads=[WA[wi], B_hT], writes=[PB[pi]], sig=(ch == 15))
                        residual(tt, pi, dg * 512)
                for tt in range(4):
                    ln_affine(tt)

            for g in range(NG):
                rows = slice(g * G, (g + 1) * G)
                if phase == 1:
                    kb.dma('sp', xt[:], x_in[rows, :].rearrange("(tt p) d -> p tt d", p=128), xs, writes=B_xt)
                    ffn("w1a", "w2a", 0, 0, 0, 1)
                    if STOP == "g0ffn":
                        kb.dma('pool', out_d[rows, :].rearrange("(tt p) d -> p tt d", p=128), xt[:], sts, reads=B_xt, writes=[B_outd])
                        break
                    kb.dma('pool', x1_d[rows, :].rearrange("(tt p) d -> p tt d", p=128), xt[:], sts, reads=B_xt, writes=[B_x1d])
                    if STOP == "g0":
                        kb.dma('pool', out_d[rows, :].rearrange("(tt p) d -> p tt d", p=128), xt[:], sts, reads=B_xt, writes=[B_outd])
                    send_h2(g)
                    if STOP == "g0":
                        break
                else:
                    kb.dma('sp', xt[:], x1_d[rows, :].rearrange("(tt p) d -> p tt d", p=128), xs, reads=[B_x1d], writes=B_xt)
                    attn_out(g)
                    ffn("w1b", "w2b", 2, 2, 4, 5)
                    kb.dma('pool', out_d[rows, :].rearrange("(tt p) d -> p tt d", p=128), xt[:], sts, reads=B_xt, writes=[B_outd])
            kb.barrier()

    jsp = nc.sync.partition_id() % 4
    H2s = [dscr("h2s%d" % g, [16, 128, G], BF16) for g in range(NG)]
    H2g = [dscr("h2g%d" % g, [16, 512, G], BF16) for g in range(NG)]
    B_h2s = [Buf("h2s%d" % g) for g in range(NG)]
    B_h2g = [Buf("h2g%d" % g) for g in range(NG)]
    FM_d = dscr("fm_d", [6, 128, S], BF16)
    TM_d = dscr("tm_d", [S, 768], BF16)
    GT_d = dscr("gt_d", [S, 4], F32)
    YS = dscr("ys_p", [32, 4, 128, G], BF16)
    YG = dscr("yg_p", [32, 4, 512, G], BF16)
    myY = dscr("myY", [8, 4, 512, G], BF16)
    B_fm = Buf("fm")
    B_yg = Buf("yg")
    B_outd = Buf("outd")

    ffn_phase(1)
    if STOP in ("g0ffn", "g0", "p1a"):
        kb.barrier()
        return nc
    zproj_own(nc, kb, ps, PB, H2g, B_h2g, wino, gbo, FM_d, TM_d, GT_d, B_fm)
    if STOP == "p1":
        dF = nc.dram_tensor("dbgF", [6, 128, S], BF16, kind="ExternalOutput").ap()
        dT = nc.dram_tensor("dbgT", [S, 768], BF16, kind="ExternalOutput").ap()
        d32 = nc.dram_tensor("dbg32", [S, 4], F32, kind="ExternalOutput").ap()
        dx1 = nc.dram_tensor("dbgx1", [TPC, D], F32, kind="ExternalOutput").ap()
        dsl = kb.slot("dbg")
        kb.dma('sp', dF, FM_d, dsl, reads=[B_fm])
        kb.dma('sp', dT, TM_d, dsl, reads=[B_fm])
        kb.dma('sp', d32, GT_d, dsl, reads=[B_fm])
        kb.dma('sp', dx1, x1_d, dsl)
        kb.barrier()
        return nc
    attention(nc, kb, ps, PB, FM_d, TM_d, GT_d, B_fm, YS, YG, B_yg, ident, identb, onesb, onesf, tri, sel127,
              B_const, mask_d, convw, convb, normg, epsc)
    if STOP == "attn":
        dY = nc.dram_tensor("dbgY", [32, 4, 128, G], BF16, kind="ExternalOutput").ap()
        dsl = kb.slot("dbg")
        kb.dma('sp', dY, YS, dsl)
        kb.barrier()
        return nc
    ysl = kb.slot("myY")
    for t in range(8):
        kb.dma('sp', myY[t], YG[jsp * 8 + t], ysl, reads=[B_yg], writes=[B_yg])
    ffn_phase(3)
    kb.barrier()
    return nc


def zproj_own(nc, kb, ps, PB, H2g, B_h2g, wino, gbo, FM_d, TM_d, GT_d, B_fm):
    _es = contextlib.ExitStack()
    wo = _es.enter_context(nc.sbuf_tensor(_u("wo"), [128, 16, 1540], BF16))
    h2a = _es.enter_context(nc.sbuf_tensor(_u("h2a"), [128, 16, G], BF16))
    h2b = _es.enter_context(nc.sbuf_tensor(_u("h2b"), [128, 16, G], BF16))
    stFa = _es.enter_context(nc.sbuf_tensor(_u("stFa"), [128, 6, G], BF16))
    stFb = _es.enter_context(nc.sbuf_tensor(_u("stFb"), [128, 6, G], BF16))
    stTa = _es.enter_context(nc.sbuf_tensor(_u("stTa"), [128, 4, 768], BF16))
    stTb = _es.enter_context(nc.sbuf_tensor(_u("stTb"), [128, 4, 768], BF16))
    stGa = _es.enter_context(nc.sbuf_tensor(_u("stGa"), [128, 4, 4], F32))
    stGb = _es.enter_context(nc.sbuf_tensor(_u("stGb"), [128, 4, 4], F32))
    gb = _es.enter_context(nc.sbuf_tensor(_u("gbo"), [128, 4], F32))
    with _es:
        h2 = [h2a, h2b]
        stF = [stFa, stFb]
        stT = [stTa, stTb]
        stG = [stGa, stGb]
        B_h2 = [Buf("h2a"), Buf("h2b")]
        B_stF = [Buf("stFa"), Buf("stFb")]
        B_stT = [Buf("stTa"), Buf("stTb")]
        B_stG = [Buf("stGa"), Buf("stGb")]
        B_wo = Buf("wo")
        wsl = kb.slot("wo")
        hsl = [kb.slot("h2a"), kb.slot("h2b")]
        ssl = [kb.slot("zsta"), kb.slot("zstb")]
        kb.dma('sp', wo[:], wino.rearrange("(kc p) n -> p kc n", p=128), wsl, reads=[], writes=[B_wo])
        kb.dma('sp', gb[:], gbo.partition_broadcast(128), wsl, writes=[B_wo])
        FMv = FM_d.rearrange("c p t -> p c t")

        def load(tg):
            g, r = tg // 4, tg % 4
            b = tg % 2
            kb.dma('sp', h2[b][:], H2g[g][:, r * 128:(r + 1) * 128, :].rearrange("kc p t -> p kc t"), hsl[b],
                   reads=[B_h2g[g]], writes=[B_h2[b]])
        load(0)
        _ZS = os.environ.get('ZPSKIP', '')
        NB_ = int(os.environ.get('ZPNB', '260'))
        for tg in range(int(os.environ.get('ZPN', '32'))):
            b = tg % 2
            tok0 = ((tg % 4) * 8 + tg // 4) * G
            if tg + 1 < 32:
                load(tg + 1)
            for c in range(0 if 'fm' in _ZS else 6):
                pi = c % 4
                for kc in range(16):
                    kb.op('pe', lambda e, c=c, kc=kc, pi=pi, b=b: e.matmul(ps[pi][:, :], lhsT=wo[:, kc, c * 128:(c + 1) * 128], rhs=h2[b][:, kc, :],
                                                                      start=(kc == 0), stop=(kc == 15)),
                          reads=[B_wo, B_h2[b]], writes=[PB[pi]], sig=(kc == 15))
                if c % 2 == 0:
                    kb.op('act', lambda e, c=c, pi=pi, b=b: e.activation(out=stF[b][:, c, :], in_=ps[pi][:, :], func=AF.Copy),
                          reads=[PB[pi]], writes=[B_stF[b]])
                else:
                    kb.op('dve', lambda e, c=c, pi=pi, b=b: e.tensor_copy(out=stF[b][:, c, :], in_=ps[pi][:, :]),
                          reads=[PB[pi]], writes=[B_stF[b]])
            if 'fm' not in _ZS:
                kb.dma('pool', FMv[:, :, tok0:tok0 + G], stF[b][:], ssl[b], reads=[B_stF[b]], writes=[B_fm])
            for tt in range(0 if 'tm' in _ZS else 4):
                pa, pb_ = (4, 6) if tt % 2 == 0 else (5, 7)
                for kc in range(16):
                    kb.op('pe', lambda e, kc=kc, tt=tt, pa=pa, b=b: e.matmul(ps[pa][:, :], lhsT=h2[b][:, kc, tt * 128:(tt + 1) * 128], rhs=wo[:, kc, 768:1280],
                                                                        start=(kc == 0), stop=(kc == 15)),
                          reads=[B_wo, B_h2[b]], writes=[PB[pa]], sig=(kc == 15))
                for kc in range(16):
                    kb.op('pe', lambda e, kc=kc, tt=tt, pb_=pb_, b=b: e.matmul(ps[pb_][:, 0:NB_], lhsT=h2[b][:, kc, tt * 128:(tt + 1) * 128], rhs=wo[:, kc, 1280:1280 + NB_],
                                                                          start=(kc == 0), stop=(kc == 15)),
                          reads=[B_wo, B_h2[b]], writes=[PB[pb_]], sig=(kc == 15))
                kb.op('dve', lambda e, tt=tt, pa=pa, b=b: e.tensor_copy(out=stT[b][:, tt, 0:512], in_=ps[pa][:, :]), reads=[PB[pa]], writes=[B_stT[b]])
                if 'sig' not in _ZS:
                    kb.op('act', lambda e, tt=tt, pb_=pb_, b=b: e.activation(out=stT[b][:, tt, 512:768], in_=ps[pb_][:, 0:256], func=AF.Sigmoid),
                          reads=[PB[pb_]], writes=[B_stT[b]])
                if 'gadd' not in _ZS:
                    kb.op('act', lambda e, tt=tt, pb_=pb_, b=b: e.activation(out=stG[b][:, tt, :], in_=ps[pb_][:, 256:260], func=AF.Copy),
                          reads=[PB[pb_]], writes=[B_stG[b]])
                    kb.op('dve', lambda e, tt=tt, b=b: e.tensor_tensor(out=stG[b][:, tt, :], in0=stG[b][:, tt, :], in1=gb[:], op=ALU.add),
                          reads=[B_wo], writes=[B_stG[b]])
            for (a_, b_) in (() if ('tm' in _ZS or 'ls' in _ZS) else ((0, 2), (3, 4))):
                kb.op('act', lambda e, a_=a_, b_=b_, b=b: e.activation(out=stG[b][:, :, a_:b_], in_=stG[b][:, :, a_:b_], func=AF.Exp, scale=-1.0),
                      reads=[B_stG[b]], writes=[B_stG[b]])
                kb.op('act', lambda e, a_=a_, b_=b_, b=b: e.activation(out=stG[b][:, :, a_:b_], in_=stG[b][:, :, a_:b_], func=AF.Ln, bias=1.0),
                      reads=[B_stG[b]], writes=[B_stG[b]])
                kb.op('dve', lambda e, a_=a_, b_=b_, b=b: e.tensor_scalar(out=stG[b][:, :, a_:b_], in0=stG[b][:, :, a_:b_], scalar1=-1.0, scalar2=None,
                                                                         op0=ALU.mult), reads=[B_stG[b]], writes=[B_stG[b]])
            if 'tm' not in _ZS and 'tst' not in _ZS:
              kb.dma('pool', TM_d[tok0:tok0 + G, :].rearrange("(tt p) c -> p tt c", p=128), stT[b][:], ssl[b], reads=[B_stT[b]], writes=[B_fm])
            if 'tm' not in _ZS and 'gst' not in _ZS:
              kb.dma('pool', GT_d[tok0:tok0 + G, :].rearrange("(tt p) c -> p tt c", p=128), stG[b][:], ssl[b], reads=[B_stG[b]], writes=[B_fm])
        kb.barrier()


def attention(nc, kb, ps, PB, FM_d, TM_d, GT_d, B_fm, YS, YG, B_yg, ident, identb, onesb, onesf, tri, sel127,
              B_const, mask_d, convw_d, convb_d, normg_d, epsc):
    TMv = TM_d.rearrange("(t p) c -> p t c", p=128)
    GTv = GT_d.rearrange("(t p) c -> p t c", p=128)
    B_ysp = Buf("ysp")
    SCALE = 128.0 ** -0.5
    NT = 128
    ccs = kb.slot("ccy")
    sts = kb.slot("ysto")

    _es = contextlib.ExitStack()
    KT = _es.enter_context(nc.sbuf_tensor(_u("KT"), [128, S], BF16))
    QT = _es.enter_context(nc.sbuf_tensor(_u("QT"), [128, S], BF16))
    V = _es.enter_context(nc.sbuf_tensor(_u("V"), [128, NT, 128], BF16))
    lf = _es.enter_context(nc.sbuf_tensor(_u("lf"), [128, NT, 2], F32))
    Fm = _es.enter_context(nc.sbuf_tensor(_u("Fm"), [128, 2, NT], F32))
    scan = _es.enter_context(nc.sbuf_tensor(_u("scan"), [128, 2, 2 * NT], F32))
    offb = _es.enter_context(nc.sbuf_tensor(_u("offb"), [128, 2 * NT], F32))
    negF = _es.enter_context(nc.sbuf_tensor(_u("negF"), [128, 2, NT], F32))
    masks = _es.enter_context(nc.sbuf_tensor(_u("masks"), [128, 4, 512], F32))
    lg = _es.enter_context(nc.sbuf_tensor(_u("lg"), [128, 4, 512], F32))
    PT = _es.enter_context(nc.sbuf_tensor(_u("PT"), [128, 4, 512], BF16))
    fq = _es.enter_context(nc.sbuf_tensor(_u("fq"), [128, 2, 512], F32))
    rden = _es.enter_context(nc.sbuf_tensor(_u("rden"), [128, 512], F32))
    yst = _es.enter_context(nc.sbuf_tensor(_u("yst"), [128, 2, 512], BF16))
    with _es:
        lds = kb.slot("fxld")
        mks = kb.slot("fxmask")
        B_KT, B_QT, B_V, B_lf, B_F, B_scan, B_mask = Buf("KT"), Buf("QT"), Buf("V"), Buf("lf"), Buf("F"), Buf("scan"), Buf("mask")
        B_lg = [Buf("lg%d" % i) for i in range(4)]
        B_PT = [Buf("PT%d" % i) for i in range(4)]
        B_fq = [Buf("fq0"), Buf("fq1")]
        B_rden = Buf("rden")
        B_yst = [Buf("yst0"), Buf("yst1")]
        mks0 = kb.slot("fxmask0")
        kb.dma('sp', masks[:], mask_d, mks0, writes=[B_mask])
        for i8 in range(8):
            kb.dma('sp', lf[:, i8 * 16:(i8 + 1) * 16, :], GTv[:, i8 * 16:(i8 + 1) * 16, 0:2], mks, reads=[B_fm], writes=[B_lf])
        lfv = lf[:].rearrange("p t h -> p h t")
        kb.op('dve', lambda e: e.tensor_copy(out=scan[:, 0, :].rearrange("p (h t) -> p h t", h=2), in_=lfv), reads=[B_lf], writes=[B_scan])
        kb.op('pe', lambda e: e.matmul(ps[0][:, 0:256], lhsT=tri[:], rhs=scan[:, 0, :], start=True, stop=True),
              reads=[B_scan, B_const], writes=[PB[0]])
        kb.op('dve', lambda e: e.tensor_copy(out=Fm[:].rearrange("p h t -> p (h t)"), in_=ps[0][:, 0:256]), reads=[PB[0]], writes=[B_F])
        kb.op('pe', lambda e: e.matmul(ps[1][:, 0:256], lhsT=sel127[:], rhs=Fm[:].rearrange("p h t -> p (h t)"), start=True, stop=True),
              reads=[B_F, B_const], writes=[PB[1]])
        kb.op('dve', lambda e: e.tensor_copy(out=scan[:, 0, :], in_=ps[1][:, 0:256]), reads=[PB[1]], writes=[B_scan])
        kb.op('dve', lambda e: e.tensor_copy(out=offb[:], in_=scan[:, 0, :]), reads=[B_scan], writes=[B_scan])
        cur = 0
        sh = 1
        while sh < NT:
            nxt = 1 - cur
            sc3 = scan[:, cur, :].rearrange("p (h t) -> p h t", h=2)
            sn3 = scan[:, nxt, :].rearrange("p (h t) -> p h t", h=2)
            kb.op('dve', lambda e, sc3=sc3, sn3=sn3, sh=sh: e.tensor_copy(out=sn3[:, :, 0:sh], in_=sc3[:, :, 0:sh]),
                  reads=[B_scan], writes=[B_scan])
            kb.op('dve', lambda e, sc3=sc3, sn3=sn3, sh=sh: e.tensor_tensor(out=sn3[:, :, sh:NT], in0=sc3[:, :, sh:NT], in1=sc3[:, :, 0:NT - sh],
                                                                          op=ALU.add), reads=[B_scan], writes=[B_scan])
            cur = nxt
            sh *= 2
        kb.op('dve', lambda e: e.tensor_tensor(out=offb[:], in0=scan[:, cur, :], in1=offb[:], op=ALU.subtract), reads=[B_scan], writes=[B_scan])
        kb.op('dve', lambda e: e.tensor_tensor(out=Fm[:].rearrange("p h t -> p (h t)"), in0=Fm[:].rearrange("p h t -> p (h t)"), in1=offb[:],
                                               op=ALU.add), reads=[B_scan], writes=[B_F])
        kb.op('dve', lambda e: e.tensor_scalar(out=negF[:].rearrange("p h t -> p (h t)"), in0=Fm[:].rearrange("p h t -> p (h t)"),
                                               scalar1=-1.0, scalar2=None, op0=ALU.mult), reads=[B_F], writes=[B_F])

        for hh in range(0 if 'fox' in os.environ.get('ATSKIP', '') else 2):
            kb.dma('sp', QT[:], FM_d[hh], lds, reads=[B_fm], writes=[B_QT])
            kb.dma('sp', KT[:], FM_d[2 + hh], lds, reads=[B_fm], writes=[B_KT])
            for i8 in range(8):
                kb.dma('sp', V[:, i8 * 16:(i8 + 1) * 16, :], TMv[:, i8 * 16:(i8 + 1) * 16, hh * 128:(hh + 1) * 128], lds, reads=[B_fm], writes=[B_V])
            NS = 4
            pairs = []
            for qg in range(32):
                pairs.append((qg, -1))
                pairs += [(qg, kt) for kt in range(4 * qg + 4)]
            LA = NS

            def s_stage(n):
                qg, kt = pairs[n]
                fb = qg % 2
                si = n % NS
                if kt < 0:
                    for c4 in range(4):
                        kb.op('pe', lambda e, c4=c4, qg=qg: e.matmul(
                            ps[si][:, c4 * 128:(c4 + 1) * 128], lhsT=Fm[:, hh, 4 * qg + c4:4 * qg + c4 + 1].to_broadcast([128, 128]),
                            rhs=ident[:], start=True, stop=True), reads=[B_F, B_const], writes=[PB[si]], sig=(c4 == 3))
                    kb.op('dve', lambda e, fb=fb: e.tensor_copy(out=fq[:, fb, :], in_=ps[si][:, :]), reads=[PB[si]], writes=[B_fq[fb]])
                    return
                kb.op('pe', lambda e, kt=kt, qg=qg, si=si: e.matmul(ps[si][:, :], lhsT=KT[:, kt * 128:(kt + 1) * 128],
                                                                  rhs=QT[:, qg * 512:(qg + 1) * 512], start=True, stop=True),
                      reads=[B_KT, B_QT], writes=[PB[si]])
                kb.op('dve', lambda e, si=si, fb=fb: e.scalar_tensor_tensor(out=lg[:, si, :], in0=ps[si][:, :], scalar=SCALE, in1=fq[:, fb, :],
                                                                          op0=ALU.mult, op1=ALU.add),
                      reads=[PB[si], B_fq[fb]], writes=[B_lg[si]])
                if kt >= 4 * qg:
                    dd = kt - 4 * qg
                    kb.op('dve', lambda e, si=si, dd=dd: e.tensor_tensor(out=lg[:, si, :], in0=lg[:, si, :], in1=masks[:, dd, :], op=ALU.add),
                          reads=[B_mask], writes=[B_lg[si]])
                kb.op('act', lambda e, si=si, kt=kt: e.activation(out=PT[:, si, :], in_=lg[:, si, :], func=AF.Exp, bias=negF[:, hh, kt:kt + 1]),
                      reads=[B_lg[si], B_F], writes=[B_PT[si]])

            def p_stage(n):
                qg, kt = pairs[n]
                if kt < 0:
                    return
                fb = qg % 2
                si = n % NS
                nk = 4 * qg + 4
                po, pd = (4, 5) if qg % 2 == 0 else (6, 7)
                last = (kt == nk - 1)
                kb.op('pe', lambda e, si=si, kt=kt, po=po, last=last: e.matmul(ps[po][:, :], lhsT=V[:, kt, :], rhs=PT[:, si, :],
                                                                            start=(kt == 0), stop=last),
                      reads=[B_V, B_PT[si]], writes=[PB[po]], sig=False)
                kb.op('pe', lambda e, si=si, kt=kt, pd=pd, last=last: e.matmul(ps[pd][:, :], lhsT=onesb[:], rhs=PT[:, si, :],
                                                                            start=(kt == 0), stop=last),
                      reads=[B_const, B_PT[si]], writes=[PB[pd]], sig=True)
                if last:
                    kb.op('dve', lambda e, pd=pd: e.reciprocal(out=rden[:], in_=ps[pd][:, :]), reads=[PB[pd]], writes=[B_rden])
                    kb.op('dve', lambda e, po=po, fb=fb: e.tensor_tensor(out=yst[:, fb, :], in0=ps[po][:, :], in1=rden[:], op=ALU.mult),
                          reads=[PB[po], B_rden], writes=[B_yst[fb]])
                    kb.dma('pool', YS[qg, hh], yst[:, fb, :], sts, reads=[B_yst[fb]], writes=[B_ysp])
                    kb.cc(YS[qg, hh], YG[qg, hh], ccs, reads=[B_ysp], writes=[B_yg])

            for n in range(LA):
                s_stage(n)
            for n in range(len(pairs)):
                p_stage(n)
                if n + LA < len(pairs):
                    s_stage(n + LA)
        kb.barrier()

    QW = 4096
    _es = contextlib.ExitStack()
    zq = _es.enter_context(nc.sbuf_tensor(_u("zq"), [128, QW + 3], BF16))
    zk = _es.enter_context(nc.sbuf_tensor(_u("zk"), [128, QW + 3], BF16))
    qT = _es.enter_context(nc.sbuf_tensor(_u("qT"), [128, QW], BF16))
    kT = _es.enter_context(nc.sbuf_tensor(_u("kT"), [128, QW], BF16))
    v1 = _es.enter_context(nc.sbuf_tensor(_u("v1"), [128, 32, 257], BF16))
    og = _es.enter_context(nc.sbuf_tensor(_u("og"), [128, 32, 256], BF16))
    gi = _es.enter_context(nc.sbuf_tensor(_u("gi"), [128, 32, 2], F32))
    gsm = _es.enter_context(nc.sbuf_tensor(_u("gsm"), [128, 8, 32], F32))
    cw = _es.enter_context(nc.sbuf_tensor(_u("cw"), [128, 8], F32))
    cb = _es.enter_context(nc.sbuf_tensor(_u("cb"), [128, 2], F32))
    ngt = _es.enter_context(nc.sbuf_tensor(_u("ngt"), [128, 256], F32))
    acc = _es.enter_context(nc.sbuf_tensor(_u("acc"), [128, 512], F32))
    trim = _es.enter_context(nc.sbuf_tensor(_u("trim"), [128, 128], F32))
    e1 = _es.enter_context(nc.sbuf_tensor(_u("e1"), [128, 128], F32))
    DT = _es.enter_context(nc.sbuf_tensor(_u("DT"), [128, 128], F32))
    SD = _es.enter_context(nc.sbuf_tensor(_u("SD"), [128, 128], BF16))
    qc = _es.enter_context(nc.sbuf_tensor(_u("qc"), [128, 257], F32))
    nd = _es.enter_context(nc.sbuf_tensor(_u("nd"), [128, 257], F32))
    sm = _es.enter_context(nc.sbuf_tensor(_u("sm"), [128, 8], F32))
    hh_ = _es.enter_context(nc.sbuf_tensor(_u("hh_"), [128, 256], F32))
    sq = _es.enter_context(nc.sbuf_tensor(_u("sq"), [128, 256], F32))
    go = _es.enter_context(nc.sbuf_tensor(_u("go"), [128, 256], F32))
    yy = _es.enter_context(nc.sbuf_tensor(_u("yy"), [128, 256], F32))
    kw = _es.enter_context(nc.sbuf_tensor(_u("kw"), [128, 128], BF16))
    Cs = _es.enter_context(nc.sbuf_tensor(_u("Cs"), [128, 257], F32))
    Cb = _es.enter_context(nc.sbuf_tensor(_u("Cb"), [128, 257], BF16))
    yTs = _es.enter_context(nc.sbuf_tensor(_u("yTs"), [128, 2, QW], BF16))
    with _es:
        lds = kb.slot("mlld")
        B_z, B_qk, B_v1, B_og, B_gi, B_gsm, B_c = Buf("z"), Buf("qk"), Buf("v1"), Buf("og"), Buf("gi"), Buf("gsm"), Buf("mc")
        B_acc, B_e1, B_DT, B_SD, B_qc, B_nd, B_sm, B_h, B_sq, B_go, B_yy, B_kw, B_C, B_Cb, B_yT = [Buf(n) for n in
            ("acc", "e1", "DT", "SD", "qc", "nd", "sm", "h", "sq", "go", "yy", "kw", "C", "Cb", "yT")]
        B_yTt = [Buf("yT%d" % i) for i in range(8)]
        e1p = [e1, _es.enter_context(nc.sbuf_tensor(_u("e1b"), [128, 128], F32))]
        DTp = [DT, _es.enter_context(nc.sbuf_tensor(_u("DTb"), [128, 128], F32))]
        SDp = [SD, _es.enter_context(nc.sbuf_tensor(_u("SDb"), [128, 128], BF16))]
        kwp = [kw, _es.enter_context(nc.sbuf_tensor(_u("kwb"), [128, 128], BF16))]
        B_e1p = [Buf("e1a"), Buf("e1b")]
        B_DTp = [Buf("DTa"), Buf("DTb")]
        B_SDp = [Buf("SDa"), Buf("SDb")]
        B_kwp = [Buf("kwa"), Buf("kwb")]
        kb.dma('sp', cw[:], convw_d, lds, writes=[B_c])
        kb.dma('sp', cb[:], convb_d, lds, writes=[B_c])
        kb.dma('sp', ngt[:], normg_d.partition_broadcast(128), lds, writes=[B_c])
        kb.dma('sp', trim[:], mask_d[:, 0, 0:128], lds, writes=[B_c])
        kb.op('dve', lambda e: e.memset(Cs[:], 0.0), writes=[B_C])
        kb.op('dve', lambda e: e.memset(Cb[:], 0.0), writes=[B_Cb])
        kb.op('dve', lambda e: e.memset(zq[:, 0:3], 0.0), writes=[B_z])
        kb.op('dve', lambda e: e.memset(zk[:, 0:3], 0.0), writes=[B_z])
        for c in range(0 if 'mlstm' in os.environ.get('ATSKIP', '') else 4):
            if c > 0:
                kb.op('dve', lambda e: e.tensor_copy(out=zq[:, 0:3], in_=zq[:, QW:QW + 3]), reads=[B_z], writes=[B_z])
                kb.op('dve', lambda e: e.tensor_copy(out=zk[:, 0:3], in_=zk[:, QW:QW + 3]), reads=[B_z], writes=[B_z])
            kb.dma('sp', zq[:, 3:3 + QW], FM_d[4][:, c * QW:(c + 1) * QW], lds, reads=[B_fm], writes=[B_z])
            kb.dma('sp', zk[:, 3:3 + QW], FM_d[5][:, c * QW:(c + 1) * QW], lds, reads=[B_fm], writes=[B_z])
            for i4 in range(4):
                tsl = slice(c * 32 + i4 * 8, c * 32 + (i4 + 1) * 8)
                kb.dma('sp', v1[:, i4 * 8:(i4 + 1) * 8, 0:256], TMv[:, tsl, 256:512], lds, reads=[B_fm], writes=[B_v1])
                kb.dma('sp', og[:, i4 * 8:(i4 + 1) * 8, :], TMv[:, tsl, 512:768], lds, reads=[B_fm], writes=[B_og])
                kb.dma('sp', gi[:, i4 * 8:(i4 + 1) * 8, :], GTv[:, tsl, 2:4], lds, reads=[B_fm], writes=[B_gi])
            kb.op('dve', lambda e: e.memset(v1[:, :, 256:257], 1.0), writes=[B_v1])
            for which, (zz, oo) in enumerate(((zq, qT), (zk, kT))):
                for pc in range(QW // 512):
                    t0 = pc * 512
                    kb.op('dve', lambda e, zz=zz, t0=t0: e.tensor_scalar(out=acc[:], in0=zz[:, t0:t0 + 512], scalar1=cw[:, which * 4:which * 4 + 1],
                                                                       scalar2=None, op0=ALU.mult), reads=[B_z, B_c], writes=[B_acc])
                    for j_ in range(1, 4):
                        kb.op('dve', lambda e, zz=zz, t0=t0, j_=j_: e.scalar_tensor_tensor(
                            out=acc[:], in0=zz[:, t0 + j_:t0 + j_ + 512], scalar=cw[:, which * 4 + j_:which * 4 + j_ + 1], in1=acc[:],
                            op0=ALU.mult, op1=ALU.add), reads=[B_z, B_c], writes=[B_acc])
                    if which == 0:
                        kb.op('act', lambda e, oo=oo, t0=t0: e.activation(out=oo[:, t0:t0 + 512], in_=acc[:], func=AF.Silu, bias=cb[:, 0:1]),
                              reads=[B_acc, B_c], writes=[B_qk])
                    else:
                        kb.op('act', lambda e: e.activation(out=acc[:], in_=acc[:], func=AF.Silu, bias=cb[:, 1:2]), reads=[B_c], writes=[B_acc])
                        kb.op('dve', lambda e, oo=oo, t0=t0: e.tensor_scalar(out=oo[:, t0:t0 + 512], in0=acc[:], scalar1=SCALE, scalar2=None, op0=ALU.mult),
                              reads=[B_acc], writes=[B_qk])
            giv = gi[:].rearrange("p t c -> p c t")
            kb.op('dve', lambda e: e.tensor_copy(out=gsm[:, 6, :], in_=giv[:, 1, :]), reads=[B_gi], writes=[B_gsm])
            kb.op('dve', lambda e: e.tensor_copy(out=gsm[:, 7, :], in_=giv[:, 0, :]), reads=[B_gi], writes=[B_gsm])
            kb.op('pe', lambda e: e.matmul(ps[0][:, 0:32], lhsT=tri[:], rhs=gsm[:, 6, :], start=True, stop=True), reads=[B_gsm, B_const], writes=[PB[0]])
            kb.op('pe', lambda e: e.matmul(ps[1][:, 0:32], lhsT=onesf[:], rhs=gsm[:, 6, :], start=True, stop=True), reads=[B_gsm, B_const], writes=[PB[1]])
            kb.op('dve', lambda e: e.tensor_copy(out=gsm[:, 0, :], in_=ps[0][:, 0:32]), reads=[PB[0]], writes=[B_gsm])
            kb.op('dve', lambda e: e.tensor_copy(out=gsm[:, 1, :], in_=ps[1][:, 0:32]), reads=[PB[1]], writes=[B_gsm])
            kb.op('dve', lambda e: e.tensor_tensor(out=gsm[:, 2, :], in0=gsm[:, 7, :], in1=gsm[:, 0, :], op=ALU.subtract), reads=[B_gsm], writes=[B_gsm])
            kb.op('dve', lambda e: e.tensor_tensor(out=gsm[:, 3, :], in0=gsm[:, 2, :], in1=gsm[:, 1, :], op=ALU.add), reads=[B_gsm], writes=[B_gsm])
            kb.op('act', lambda e: e.activation(out=gsm[:, 3, :], in_=gsm[:, 3, :], func=AF.Exp), reads=[B_gsm], writes=[B_gsm])
            kb.op('act', lambda e: e.activation(out=gsm[:, 4, :], in_=gsm[:, 0, :], func=AF.Exp), reads=[B_gsm], writes=[B_gsm])
            kb.op('act', lambda e: e.activation(out=gsm[:, 5, :], in_=gsm[:, 1, :], func=AF.Exp), reads=[B_gsm], writes=[B_gsm])
            def stage_a(ch):
                par = ch % 2
                cs = slice(ch * 128, (ch + 1) * 128)
                kb.op('pe', lambda e: e.matmul(ps[par][:, 0:128], lhsT=kT[:, cs], rhs=qT[:, cs], start=True, stop=True),
                      reads=[B_qk], writes=[PB[par]], sig=False)
                kb.op('pe', lambda e: e.matmul(ps[par][:, 128:256], lhsT=gsm[:, 0, ch:ch + 1].to_broadcast([128, 128]), rhs=ident[:],
                                               start=True, stop=True), reads=[B_gsm, B_const], writes=[PB[par]])
                kb.op('pe', lambda e: e.matmul(ps[2 + par][:, 0:128], lhsT=kT[:, cs], rhs=identb[:], start=True, stop=True),
                      reads=[B_qk, B_const], writes=[PB[2 + par]])
                kb.op('dve', lambda e: e.tensor_tensor(out=e1p[par][:], in0=ps[par][:, 128:256], in1=trim[:], op=ALU.add),
                      reads=[PB[par], B_c], writes=[B_e1p[par]])
                kb.op('act', lambda e: e.activation(out=DTp[par][:], in_=e1p[par][:], func=AF.Exp, bias=gsm[:, 2, ch:ch + 1]),
                      reads=[B_e1p[par], B_gsm], writes=[B_DTp[par]])
                kb.op('dve', lambda e: e.tensor_tensor(out=SDp[par][:], in0=ps[par][:, 0:128], in1=DTp[par][:], op=ALU.mult),
                      reads=[PB[par], B_DTp[par]], writes=[B_SDp[par]])
                kb.op('act', lambda e: e.activation(out=kwp[par][:], in_=ps[2 + par][:, 0:128], func=AF.Identity, scale=gsm[:, 3, ch:ch + 1]),
                      reads=[PB[2 + par], B_gsm], writes=[B_kwp[par]])

            def stage_b(ch):
                par = ch % 2
                cs = slice(ch * 128, (ch + 1) * 128)
                kb.op('pe', lambda e: e.matmul(ps[4][:, 0:257], lhsT=SDp[par][:], rhs=v1[:, ch, :], start=True, stop=True),
                      reads=[B_SDp[par], B_v1], writes=[PB[4]])
                kb.op('pe', lambda e: e.matmul(ps[5][:, 0:257], lhsT=qT[:, cs], rhs=Cb[:], start=True, stop=True),
                      reads=[B_qk, B_Cb], writes=[PB[5]])
                kb.op('pe', lambda e: e.matmul(ps[7][:, 0:257], lhsT=kwp[par][:], rhs=v1[:, ch, :], start=True, stop=True),
                      reads=[B_kwp[par], B_v1], writes=[PB[7]])
                if ch > 0:
                    stage_t(ch - 1)
                kb.op('act', lambda e: e.activation(out=qc[:], in_=ps[5][:, 0:257], func=AF.Identity, scale=gsm[:, 4, ch:ch + 1]),
                      reads=[PB[5], B_gsm], writes=[B_qc])
                kb.op('dve', lambda e: e.scalar_tensor_tensor(out=Cs[:], in0=Cs[:], scalar=gsm[:, 5, ch:ch + 1], in1=ps[7][:, 0:257],
                                                             op0=ALU.mult, op1=ALU.add), reads=[PB[7], B_gsm], writes=[B_C])
                kb.op('dve', lambda e: e.tensor_copy(out=Cb[:], in_=Cs[:]), reads=[B_C], writes=[B_Cb])
                kb.op('dve', lambda e: e.tensor_tensor(out=nd[:], in0=ps[4][:, 0:257], in1=qc[:], op=ALU.add), reads=[PB[4], B_qc], writes=[B_nd])
                kb.op('act', lambda e: e.activation(out=sm[:, 5:6], in_=nd[:, 256:257], func=AF.Abs), reads=[B_nd], writes=[B_sm])
                kb.op('dve', lambda e: e.tensor_scalar(out=sm[:, 0:1], in0=sm[:, 5:6], scalar1=1.0, scalar2=None, op0=ALU.max),
                      reads=[B_sm], writes=[B_sm])
                kb.op('dve', lambda e: e.reciprocal(out=sm[:, 1:2], in_=sm[:, 0:1]), reads=[B_sm], writes=[B_sm])
                kb.op('dve', lambda e: e.tensor_scalar(out=hh_[:], in0=nd[:, 0:256], scalar1=sm[:, 1:2], scalar2=None, op0=ALU.mult),
                      reads=[B_nd, B_sm], writes=[B_h])
                kb.op('act', lambda e: e.activation(out=sq[:], in_=hh_[:], func=AF.Square, accum_out=sm[:, 2:3]), reads=[B_h], writes=[B_sq, B_sm])
                kb.op('act', lambda e: e.activation(out=sm[:, 4:5], in_=sm[:, 2:3], func=AF.Sqrt, scale=1.0 / 256.0, bias=epsc[:, 0:1]),
                      reads=[B_sm, B_const], writes=[B_sm])
                kb.op('dve', lambda e: e.reciprocal(out=sm[:, 3:4], in_=sm[:, 4:5]), reads=[B_sm], writes=[B_sm])
                kb.op('dve', lambda e: e.tensor_tensor(out=go[:], in0=og[:, ch, :], in1=ngt[:], op=ALU.mult), reads=[B_og, B_c], writes=[B_go])
                kb.op('dve', lambda e: e.scalar_tensor_tensor(out=yy[:], in0=hh_[:], scalar=sm[:, 3:4], in1=go[:], op0=ALU.mult, op1=ALU.mult),
                      reads=[B_h, B_sm, B_go], writes=[B_yy])

            def stage_t(ch):
                cs = slice(ch * 128, (ch + 1) * 128)
                for half in range(2):
                    kb.op('pe', lambda e, half=half: e.transpose(ps[6][:, half * 128:(half + 1) * 128], yy[:, half * 128:(half + 1) * 128], ident[:]),
                          reads=[B_yy, B_const], writes=[PB[6]], sig=(half == 1))
                kb.op('act', lambda e: e.activation(out=yTs[:, :, cs], in_=ps[6][:, 0:256].rearrange("p (a b) -> p a b", a=2), func=AF.Copy),
                      reads=[PB[6]], writes=[B_yTt[ch // 4]])
                if ch % 4 == 3:
                    tb = ch // 4
                    for a_ in range(2):
                        kb.dma('pool', YS[c * 8 + tb, 2 + a_], yTs[:, a_, tb * G:(tb + 1) * G], sts, reads=[B_yTt[tb]], writes=[B_ysp])
                        kb.cc(YS[c * 8 + tb, 2 + a_], YG[c * 8 + tb, 2 + a_], ccs, reads=[B_ysp], writes=[B_yg])

            stage_a(0)
            for ch in range(32):
                if ch + 1 < 32:
                    stage_a(ch + 1)
                stage_b(ch)
            stage_t(31)
        kb.barrier()


_CACHE = {}


def kernel(**inputs):
    f32 = np.float32
    x = np.asarray(inputs["x"], f32)
    c = np.asarray(inputs["c"], f32)
    sq = lambda k: np.ascontiguousarray(np.asarray(inputs[k], f32)[0])
    if "nc" not in _CACHE:
        _CACHE["nc"] = build()
    nc = _CACHE["nc"]
    lnp = np.stack([sq("ln1_g"), sq("ln1_b"), sq("ln2_g"), sq("ln2_b"), sq("ln3_g"), sq("ln3_b")], 0)
    gbias = np.concatenate([sq("fox_f_bias"), sq("mlstm_i_bias"), sq("mlstm_f_bias")])[None, :]
    conv_w = sq("mlstm_conv_w")
    conv_b = sq("mlstm_conv_b")
    norm_g = sq("mlstm_norm_g")
    b_adaT = np.ascontiguousarray(sq("b_ada").reshape(144, 128).T)
    ident = np.eye(128, dtype=f32)
    tri = np.triu(np.ones((128, 128), f32))
    sel = np.zeros((128, 128), f32)
    sel[127, :] = 1.0
    kk = np.arange(128)[:, None, None] + 128 * np.arange(4)[None, :, None]
    qq = np.arange(512)[None, None, :]
    masks = np.where(kk <= qq, 0.0, NEG).astype(f32)
    w_in_full = sq("w_in")
    shared = {"w_ada": sq("w_ada"), "b_adaT": b_adaT, "ffn1_w_in": sq("ffn1_w_in"), "ffn1_w_out": sq("ffn1_w_out"),
              "w_out": sq("w_out"), "ffn2_w_in": sq("ffn2_w_in"), "ffn2_w_out": sq("ffn2_w_out"),
              "lnp": lnp, "ident": ident, "tri": tri, "sel127": sel, "masks": masks}
    in_maps = []
    for core in range(8):
        b, j = core // 4, core % 4
        m = dict(shared)
        m["x"] = np.ascontiguousarray(x[b, j * TPC:(j + 1) * TPC, :])
        m["cT"] = np.ascontiguousarray(c[b].reshape(16, 128).T)
        cw = np.stack([conv_w[:, j * 128:(j + 1) * 128].T, conv_w[:, 512 + j * 128:512 + (j + 1) * 128].T], 1)
        m["convw"] = np.ascontiguousarray(cw.reshape(128, 8))
        m["convb"] = np.ascontiguousarray(np.stack([conv_b[j * 128:(j + 1) * 128], conv_b[512 + j * 128:512 + (j + 1) * 128]], 1))
        m["normg"] = np.ascontiguousarray(norm_g[j * 256:(j + 1) * 256][None, :])
        cols = np.concatenate([np.arange(2 * j * 128, 2 * j * 128 + 256), np.arange(1024 + 2 * j * 128, 1024 + 2 * j * 128 + 256),
                               np.arange(3080 + j * 128, 3080 + (j + 1) * 128), np.arange(3592 + j * 128, 3592 + (j + 1) * 128),
                               np.arange(2048 + 2 * j * 128, 2048 + 2 * j * 128 + 256), np.arange(4104 + j * 256, 4104 + (j + 1) * 256),
                               np.arange(5136 + j * 256, 5136 + (j + 1) * 256),
                               np.array([3072 + 2 * j, 3072 + 2 * j + 1, 5128 + j, 5132 + j])])
        m["win_own"] = np.ascontiguousarray(w_in_full[:, cols])
        m["gb_own"] = np.ascontiguousarray(gbias[0, [2 * j, 2 * j + 1, 8 + j, 12 + j]][None, :])
        in_maps.append(m)
    res = run_bass_kernel_spmd(nc, in_maps, core_ids=list(range(8)))
    out = np.empty((2, S, D), f32)
    for core in range(8):
        b, j = core // 4, core % 4
        out[b, j * TPC:(j + 1) * TPC, :] = res.results[core]["out"]
    if os.environ.get("KSTOP", ""):
        _CACHE["dbg"] = [{k: v for k, v in r.items() if k != "out"} for r in res.results]
    return out
```

```python
import contextlib
import os
import numpy as np
import ml_dtypes
import concourse.bass as bass
import concourse.mybir as mybir
from concourse.bass_utils import run_bass_kernel_spmd

F32 = mybir.dt.float32
BF16 = mybir.dt.bfloat16
AF = mybir.ActivationFunctionType
ALU = mybir.AluOpType

D = 2048
S = 16384
TPC = 4096
G = 512
NG = 8
DFF = 5632
NFF = 44
INW = 6160
ALPHA = 2.0 ** 0.25
EPS = 1e-5
NEG = -30000.0
RG = [[0, 1, 2, 3], [4, 5, 6, 7]]


_UC = [0]


def _u(name):
    _UC[0] += 1
    return "%s_u%d" % (name, _UC[0])


class Buf:
    def __init__(self, name):
        self.name = name
        self.w = None
        self.r = []


class Slot:
    def __init__(self, nc, name):
        self.sem = nc.alloc_semaphore(name)
        self.name = name
        self.cnt = 0


class KB:
    def __init__(self, nc):
        self.nc = nc
        self.eng = {'pe': nc.tensor, 'act': nc.scalar, 'dve': nc.vector, 'pool': nc.gpsimd, 'sp': nc.sync}
        self.sem = {e: nc.alloc_semaphore('c_' + e) for e in ('pe', 'act', 'dve', 'pool')}
        self.cnt = {e: 0 for e in self.sem}
        self.seen = {e: {} for e in self.eng}
        self.pend = []
        self.slots = []
        self.nslot = 0

    def slot(self, name):
        s = Slot(self.nc, 'd_%s_%d' % (name, self.nslot))
        self.nslot += 1
        self.slots.append(s)
        return s

    def wait(self, e, tok):
        if tok is None:
            return
        sem, key, val = tok
        if key == e and e == 'pe':
            return
        if self.seen[e].get(key, 0) >= val:
            return
        self.eng[e].wait_ge(sem, val)
        self.seen[e][key] = val

    def _deps(self, e, reads, writes, nowaw=False):
        for b in reads:
            self.wait(e, b.w)
        for b in writes:
            if not (nowaw and b.w is not None and b.w[1] == e):
                self.wait(e, b.w)
            for t in b.r:
                self.wait(e, t)

    def op(self, e, fn, reads=(), writes=(), sig=True, nowaw=False):
        self._deps(e, reads, writes, nowaw)
        ins = fn(self.eng[e])
        if sig:
            self.cnt[e] += 1
            ins.then_inc(self.sem[e], 1)
            tok = (self.sem[e], e, self.cnt[e])
            if e == 'pe':
                for (rb, wb) in self.pend:
                    for b in rb:
                        b.r.append(tok)
                    for b in wb:
                        b.w = tok
                        b.r = []
                self.pend = []
            for b in reads:
                b.r.append(tok)
            for b in writes:
                b.w = tok
                b.r = []
            return tok
        assert e == 'pe'
        self.pend.append((list(reads), list(writes)))
        return None

    def dma(self, q, out, in_, slot, reads=(), writes=(), **kw):
        self._deps(q, reads, writes)
        self.eng[q].dma_start(out=out, in_=in_, **kw).then_inc(slot.sem, 16)
        slot.cnt += 16
        tok = (slot.sem, slot.name, slot.cnt)
        for b in reads:
            b.r.append(tok)
        for b in writes:
            b.w = tok
            b.r = []
        return tok

    def cc(self, ins, outs, slot, reads=(), writes=()):
        self._deps('pool', reads, writes)
        self.nc.gpsimd.collective_compute("AllGather", ALU.bypass, replica_groups=RG,
                                          ins=[ins.opt()], outs=[outs.opt()]).then_inc(slot.sem, 1)
        slot.cnt += 1
        tok = (slot.sem, slot.name, slot.cnt)
        for b in reads:
            b.r.append(tok)
        for b in writes:
            b.w = tok
            b.r = []
        return tok

    def barrier(self, engines=('pe', 'act', 'dve', 'pool', 'sp'), exclude=()):
        toks = [(self.sem[e], e, self.cnt[e]) for e in self.sem if self.cnt[e] > 0]
        toks += [(s.sem, s.name, s.cnt) for s in self.slots if s.cnt > 0 and s not in exclude]
        for e in engines:
            for t in toks:
                self.wait(e, t)


def build(dbg=False):
    nc = bass.Bass("TRN2", target_bir_lowering=False)
    kb = KB(nc)

    def din(name, shape, dt=F32):
        return nc.dram_tensor(name, shape, dt, kind="ExternalInput").ap()

    def dscr(name, shape, dt):
        return nc.dram_tensor(name, shape, dt, kind="Internal").ap()

    x_in = din("x", [TPC, D])
    cT = din("cT", [128, 16])
    w_ada = din("w_ada", [D, 9 * D])
    b_adaT = din("b_adaT", [128, 144])
    wf = {"w1a": din("ffn1_w_in", [D, 2 * DFF]), "w2a": din("ffn1_w_out", [DFF, D]),
          "wout": din("w_out", [D, D]),
          "w1b": din("ffn2_w_in", [D, 2 * DFF]), "w2b": din("ffn2_w_out", [DFF, D])}
    lnp = din("lnp", [6, D])
    convw = din("convw", [128, 8])
    convb = din("convb", [128, 2])
    normg = din("normg", [1, 256])
    wino_f = din("win_own", [D, 1540])
    gbo = din("gb_own", [1, 4])
    wino = dscr("wino_bf", [D, 1540], BF16)
    ident_d = din("ident", [128, 128])
    tri_d = din("tri", [128, 128])
    sel_d = din("sel127", [128, 128])
    mask_d = din("masks", [128, 4, 512])
    out_d = nc.dram_tensor("out", [TPC, D], F32, kind="ExternalOutput").ap()

    wb = {k: dscr(k + "_bf", list(v.shape), BF16) for k, v in wf.items()}
    x1_d = dscr("x1s", [TPC, D], F32)
    gate_scr = dscr("gate_scr", [48, 128], F32)

    def sb(name, shape, dt):
        return nc.alloc_sbuf_tensor(_u(name), shape, dt)

    ident = sb("ident", [128, 128], F32)
    identb = sb("identb", [128, 128], BF16)
    onesb = sb("onesb", [128, 128], BF16)
    onesf = sb("onesf", [128, 128], F32)
    epsc = sb("epsc", [128, 1], F32)
    tri = sb("tri", [128, 128], F32)
    sel127 = sb("sel127", [128, 128], F32)
    adaT = sb("adaT", [128, 144], F32)
    modsc = sb("modsc", [128, 3, 16], F32)
    B_const = Buf("const")
    B_ada = Buf("ada")

    ps = [nc.alloc_psum_tensor("ps%d" % i, [128, 512], F32) for i in range(8)]
    PB = [Buf("ps%d" % i) for i in range(8)]

    ld = kb.slot("const")
    for t, dsrc in ((ident, ident_d), (tri, tri_d), (sel127, sel_d)):
        kb.dma('sp', t[:], dsrc, ld, writes=[B_const])
    kb.op('dve', lambda e: e.tensor_copy(out=identb[:], in_=ident[:]), reads=[B_const], writes=[B_const])
    kb.op('dve', lambda e: e.memset(onesb[:], 1.0), writes=[B_const])
    kb.op('dve', lambda e: e.memset(onesf[:], 1.0), writes=[B_const])
    kb.op('dve', lambda e: e.memset(epsc[:], EPS), writes=[B_const])

    WB = {k: Buf("wb_" + k) for k in wf}
    wslot = {k: kb.slot("wc_" + k) for k in wf}

    def cast_weight(k, nchunk):
        src, dst = wf[k], wb[k]
        rows = src.shape[0]
        r = rows // nchunk
        for i in range(nchunk):
            kb.dma('pool', dst[i * r:(i + 1) * r, :], src[i * r:(i + 1) * r, :], wslot[k], writes=[],
                   max_dma_last_dim=8192)
        WB[k].w = (wslot[k].sem, wslot[k].name, wslot[k].cnt)

    cast_weight("w1a", 8)
    cast_weight("w2a", 4)
    wf["wino"] = wino_f
    wb["wino"] = wino
    WB["wino"] = Buf("wb_wino")
    wslot["wino"] = kb.slot("wc_wino")
    cast_weight("wino", 2)

    _es = contextlib.ExitStack()
    aw0 = _es.enter_context(nc.sbuf_tensor(_u("ada_w0"), [128, 16, 512], F32))
    aw1 = _es.enter_context(nc.sbuf_tensor(_u("ada_w1"), [128, 16, 512], F32))
    scT = _es.enter_context(nc.sbuf_tensor(_u("scT"), [128, 16], F32))
    badaT = _es.enter_context(nc.sbuf_tensor(_u("badaT"), [128, 144], F32))
    gT = _es.enter_context(nc.sbuf_tensor(_u("gT"), [128, 48], F32))
    gTT = _es.enter_context(nc.sbuf_tensor(_u("gTT"), [48, 128], F32))
    with _es:
        aws = [aw0, aw1]
        AWB = [Buf("aw0"), Buf("aw1")]
        awslot = [kb.slot("aw0"), kb.slot("aw1")]
        B_sc = Buf("scT")
        ld2 = kb.slot("ld2")
        ld3 = kb.slot("ld3")
        kb.dma('sp', scT[:], cT, ld2, writes=[B_sc])
        kb.dma('sp', badaT[:], b_adaT, ld2, writes=[B_sc])
        kb.op('act', lambda e: e.activation(out=scT[:], in_=scT[:], func=AF.Silu), reads=[B_sc], writes=[B_sc])
        w_ada_v = w_ada.rearrange("(kc p) n -> p kc n", p=128)
        for blk in range(36):
            a = aws[blk % 2]
            kb.dma('sp', a[:], w_ada_v[:, :, blk * 512:(blk + 1) * 512], awslot[blk % 2], writes=[AWB[blk % 2]])
            for cc in range(4):
                ch = blk * 4 + cc
                for kc in range(16):
                    last = (kc == 15)
                    kb.op('pe', lambda e, a=a, cc=cc, kc=kc, ch=ch: e.matmul(
                        ps[0][:, ch:ch + 1], lhsT=a[:, kc, cc * 128:(cc + 1) * 128], rhs=scT[:, kc:kc + 1],
                        start=(kc == 0), stop=(kc == 15)),
                        reads=[AWB[blk % 2], B_sc], writes=[PB[0]], sig=(last and cc == 3))
        kb.op('dve', lambda e: e.tensor_tensor(out=adaT[:], in0=ps[0][:, 0:144], in1=badaT[:], op=ALU.add),
              reads=[PB[0], B_sc], writes=[B_ada])
        for k in range(3):
            kb.op('dve', lambda e, k=k: e.tensor_scalar(out=modsc[:, k, :], in0=adaT[:, (3 * k + 1) * 16:(3 * k + 2) * 16],
                                                       scalar1=1.0, scalar2=None, op0=ALU.add),
                  reads=[B_ada], writes=[B_ada])
        for k, coef in enumerate((0.5, 1.0, 0.5)):
            kb.op('dve', lambda e, k=k, coef=coef: e.tensor_scalar(
                out=gT[:, k * 16:(k + 1) * 16], in0=adaT[:, (3 * k + 2) * 16:(3 * k + 3) * 16],
                scalar1=1.0, scalar2=coef, op0=ALU.add, op1=ALU.mult), reads=[B_ada], writes=[B_sc])
        kb.op('pe', lambda e: e.transpose(ps[1][0:48, 0:128], gT[:, 0:48], ident[:]), reads=[B_sc, B_const], writes=[PB[1]])
        kb.op('dve', lambda e: e.tensor_copy(out=gTT[:], in_=ps[1][0:48, 0:128]), reads=[PB[1]], writes=[B_sc])
        B_gscr = Buf("gscr")
        kb.dma('sp', gate_scr, gTT[:], ld3, reads=[B_sc], writes=[B_gscr])
        kb.barrier(exclude=list(wslot.values()))

    cast_weight("wout", 2)
    cast_weight("w1b", 8)
    cast_weight("w2b", 4)
    STOP = os.environ.get("KSTOP", "")
    if STOP == "cast":
        kb.barrier()
        return nc

    gate_rows = gate_scr.rearrange("(k c) p -> k (c p)", k=3)

    def ffn_phase(phase):
        _es = contextlib.ExitStack()
        xt = _es.enter_context(nc.sbuf_tensor(_u("xt"), [128, 4, D], F32))
        xn = _es.enter_context(nc.sbuf_tensor(_u("xn"), [128, D], F32))
        hT = _es.enter_context(nc.sbuf_tensor(_u("hT"), [128, 16, G], BF16))
        HT = _es.enter_context(nc.sbuf_tensor(_u("HT"), [128, NFF, G], BF16))
        wA0 = _es.enter_context(nc.sbuf_tensor(_u("wA0"), [128, 16, 512], BF16))
        wA1 = _es.enter_context(nc.sbuf_tensor(_u("wA1"), [128, 16, 512], BF16))
        wA2 = _es.enter_context(nc.sbuf_tensor(_u("wA2"), [128, 16, 512], BF16))
        wB0 = _es.enter_context(nc.sbuf_tensor(_u("wB0"), [128, 1024], BF16))
        wB1 = _es.enter_context(nc.sbuf_tensor(_u("wB1"), [128, 1024], BF16))
        wB2 = _es.enter_context(nc.sbuf_tensor(_u("wB2"), [128, 1024], BF16))
        wB3 = _es.enter_context(nc.sbuf_tensor(_u("wB3"), [128, 1024], BF16))
        tabs = _es.enter_context(nc.sbuf_tensor(_u("tabs"), [128, 3, D], F32))
        tmp = _es.enter_context(nc.sbuf_tensor(_u("tmp"), [128, 2, 512], F32))
        sg = _es.enter_context(nc.sbuf_tensor(_u("sg"), [128, 2, 512], F32))
        stat = _es.enter_context(nc.sbuf_tensor(_u("stat"), [128, 4, 24], F32))
        mv = _es.enter_context(nc.sbuf_tensor(_u("mv"), [128, 4, 8], F32))
        gst = _es.enter_context(nc.sbuf_tensor(_u("gst"), [128, 4, 16], F32))
        gso = _es.enter_context(nc.sbuf_tensor(_u("gso"), [128, 4, 16], F32))
        wg = _es.enter_context(nc.sbuf_tensor(_u("wg"), [128, 16, 16], BF16))
        with _es:
            stage = HT[:, 0:24, :]
            wA = [wA0, wA1, wA2]
            WA = [Buf("wA%d" % i) for i in range(3)]
            wAs = [kb.slot("wA%d" % i) for i in range(3)]
            wBt = [wB0, wB1, wB2, wB3]
            WBb = [Buf("wB%d" % i) for i in range(4)]
            wBs = [kb.slot("wB%d" % i) for i in range(4)]
            st = {"a": 0, "b": 0}
            B_xt = [Buf("xt%d" % i) for i in range(4)]
            B_xn = Buf("xn")
            B_hT = Buf("hT")
            B_HT = [Buf("HT%d" % i) for i in range(NFF)]
            B_tabs = Buf("tabs")
            B_tmp = [Buf("tmp0"), Buf("tmp1")]
            B_sg = [Buf("sg0"), Buf("sg1")]
            B_statt = [Buf("stat%d" % i) for i in range(4)]
            B_stage = Buf("stage")
            B_gst = Buf("gst")
            B_gso = Buf("gso")
            B_gso = Buf("gso")
            B_wg = Buf("wg")
            xs = kb.slot("xld")
            tbs = kb.slot("tabs")
            sts = kb.slot("store")
            sts_a = kb.slot("store_a")
            sts_b = kb.slot("store_b")
            sts_t = kb.slot("store_t")
            wgs = kb.slot("wg")
            ccs = kb.slot("cc")
            B_x1d = Buf("x1d")

            def loadA(src_ap_fn, wkey):
                i = st["a"] % 3
                st["a"] += 1
                for (o, s_) in src_ap_fn(wA[i]):
                    kb.dma('sp', o, s_, wAs[i], reads=[WB[wkey]], writes=[WA[i]])
                return i

            def load_tabs(rows):
                for i, r in enumerate(rows):
                    kb.dma('sp', tabs[:, i, :], r.partition_broadcast(128), tbs, reads=[B_gscr], writes=[B_tabs])

            def ln_stats(tt, col):
                for q in range(4):
                    kb.op('dve', lambda e, q=q: e.bn_stats(out=stat[:, tt, q * 6:(q + 1) * 6], in_=xt[:, tt, q * 512:(q + 1) * 512]),
                          reads=[B_xt[tt]], writes=[B_statt[tt]])
                kb.op('dve', lambda e: e.bn_aggr(out=mv[:, tt, 0:2], in_=stat[:, tt, :]), reads=[B_statt[tt]], writes=[B_statt[tt]])
                kb.op('act', lambda e: e.activation(out=mv[:, tt, 3:4], in_=mv[:, tt, 1:2], func=AF.Sqrt, bias=epsc[:, 0:1]),
                      reads=[B_statt[tt], B_const], writes=[B_statt[tt]])
                kb.op('dve', lambda e: e.reciprocal(out=mv[:, tt, 2:3], in_=mv[:, tt, 3:4]), reads=[B_statt[tt]], writes=[B_statt[tt]])
                kb.op('dve', lambda e: e.tensor_scalar(out=mv[:, tt, 4:5], in0=mv[:, tt, 0:1], scalar1=-1.0, scalar2=mv[:, tt, 2:3],
                                                       op0=ALU.mult, op1=ALU.mult), reads=[B_statt[tt]], writes=[B_statt[tt]])

            def mod_transpose(k):
                for tt in range(4):
                    ln_stats(tt, 0)
                    kb.op('act', lambda e, tt=tt: e.activation(out=xn[:], in_=xt[:, tt, :], func=AF.Identity, scale=mv[:, tt, 2:3],
                                                              bias=mv[:, tt, 4:5]),
                          reads=[B_xt[tt], B_statt[tt]], writes=[B_xn])
                    for q in range(4):
                        pi = 4 + (q % 2)
                        for c4 in range(4):
                            kc = q * 4 + c4
                            kb.op('pe', lambda e, kc=kc, c4=c4, pi=pi: e.transpose(
                                ps[pi][:, c4 * 128:(c4 + 1) * 128], xn[:, kc * 128:(kc + 1) * 128], ident[:]),
                                reads=[B_xn, B_const], writes=[PB[pi]], sig=(c4 == 3))
                        for c4 in range(4):
                            kc = q * 4 + c4
                            kb.op('act', lambda e, kc=kc, c4=c4, pi=pi, tt=tt: e.activation(
                                out=hT[:, kc, tt * 128:(tt + 1) * 128], in_=ps[pi][:, c4 * 128:(c4 + 1) * 128],
                                func=AF.Identity, scale=modsc[:, k, kc:kc + 1],
                                bias=adaT[:, (3 * k) * 16 + kc:(3 * k) * 16 + kc + 1]),
                                reads=[PB[pi], B_ada], writes=[B_hT], nowaw=True)

            def ffn(w1k, w2k, k, gt_row, lng_row, lnb_row):
                w1 = wb[w1k].rearrange("(kc p) n -> p kc n", p=128)
                w2 = wb[w2k]
                mod_transpose(k)
                load_tabs([gate_rows[gt_row:gt_row + 1, :], lnp[lng_row:lng_row + 1, :], lnp[lnb_row:lnb_row + 1, :]])
                def srcs(i2):
                    return lambda t: [(t[:, :, 0:256], w1[:, :, i2 * 256:(i2 + 1) * 256]),
                                      (t[:, :, 256:512], w1[:, :, DFF + i2 * 256:DFF + (i2 + 1) * 256])]
                pre = [loadA(srcs(0), w1k), loadA(srcs(1), w1k)]
                for i2 in range(22):
                    wi = pre[i2]
                    if i2 + 2 < 22:
                        pre.append(loadA(srcs(i2 + 2), w1k))
                    for c in range(2):
                        i = i2 * 2 + c
                        pg, pu = (0, 1) if (i % 2 == 0) else (2, 3)
                        for kc in range(16):
                            kb.op('pe', lambda e, wi=wi, kc=kc, c=c, pg=pg: e.matmul(
                                ps[pg][:, :], lhsT=wA[wi][:, kc, c * 128:(c + 1) * 128], rhs=hT[:, kc, :],
                                start=(kc == 0), stop=(kc == 15)), reads=[WA[wi], B_hT], writes=[PB[pg]], sig=(kc == 15))
                        for kc in range(16):
                            kb.op('pe', lambda e, wi=wi, kc=kc, c=c, pu=pu: e.matmul(
                                ps[pu][:, :], lhsT=wA[wi][:, kc, 256 + c * 128:256 + (c + 1) * 128], rhs=hT[:, kc, :],
                                start=(kc == 0), stop=(kc == 15)), reads=[WA[wi], B_hT], writes=[PB[pu]], sig=(kc == 15))
                        kb.op('act', lambda e, pg=pg, i=i: e.activation(out=sg[:, i % 2, :], in_=ps[pg][:, :], func=AF.Silu),
                              reads=[PB[pg]], writes=[B_sg[i % 2]])
                        kb.op('dve', lambda e, pu=pu, i=i: e.tensor_tensor(out=HT[:, i, :], in0=sg[:, i % 2, :], in1=ps[pu][:, :],
                                                                          op=ALU.mult),
                              reads=[PB[pu], B_sg[i % 2]], writes=[B_HT[i]])
                def loadB(dh, i):
                    s_ = st["b"] % 4
                    st["b"] += 1
                    kb.dma('sp', wBt[s_][:], w2[i * 128:(i + 1) * 128, dh * 1024:(dh + 1) * 1024], wBs[s_],
                           reads=[WB[w2k]], writes=[WBb[s_]])
                    return s_
                seq = [(dh, i) for dh in range(2) for i in range(NFF)]
                preb = [loadB(*seq[0]), loadB(*seq[1]), loadB(*seq[2])]
                for n, (dh, i) in enumerate(seq):
                    s_ = preb[n]
                    if n + 3 < len(seq):
                        preb.append(loadB(*seq[n + 3]))
                    for tt in range(4):
                        for dgi in range(2):
                            pi = tt * 2 + dgi
                            kb.op('pe', lambda e, s_=s_, tt=tt, dgi=dgi, pi=pi, i=i: e.matmul(
                                ps[pi][:, :], lhsT=HT[:, i, tt * 128:(tt + 1) * 128], rhs=wBt[s_][:, dgi * 512:(dgi + 1) * 512],
                                start=(i == 0), stop=(i == NFF - 1)),
                                reads=[WBb[s_], B_HT[i]], writes=[PB[pi]], sig=(i == NFF - 1 or (tt == 3 and dgi == 1)))
                    if i == NFF - 1:
                        for tt in range(4):
                            for dgi in range(2):
                                pi = tt * 2 + dgi
                                c0 = dh * 1024 + dgi * 512
                                residual(tt, pi, c0)
                for tt in range(4):
                    ln_affine(tt)

            def residual(tt, pi, c0):
                j = pi % 2
                kb.op('dve', lambda e: e.tensor_tensor(out=tmp[:, j, :], in0=ps[pi][:, :], in1=tabs[:, 0, c0:c0 + 512], op=ALU.mult),
                      reads=[PB[pi], B_tabs], writes=[B_tmp[j]])
                kb.op('dve', lambda e: e.scalar_tensor_tensor(out=xt[:, tt, c0:c0 + 512], in0=xt[:, tt, c0:c0 + 512], scalar=ALPHA,
                                                             in1=tmp[:, j, :], op0=ALU.mult, op1=ALU.add),
                      reads=[B_tmp[j]], writes=[B_xt[tt]])

            def ln_affine(tt):
                ln_stats(tt, 0)
                kb.op('act', lambda e: e.activation(out=xt[:, tt, :], in_=xt[:, tt, :], func=AF.Identity, scale=mv[:, tt, 2:3], bias=mv[:, tt, 4:5]),
                      reads=[B_statt[tt]], writes=[B_xt[tt]])
                kb.op('dve', lambda e: e.tensor_tensor(out=xt[:, tt, :], in0=xt[:, tt, :], in1=tabs[:, 1, :], op=ALU.mult),
                      reads=[B_tabs], writes=[B_xt[tt]])
                eng = 'pool' if tt % 2 == 0 else 'dve'
                kb.op(eng, lambda e: e.tensor_tensor(out=xt[:, tt, :], in0=xt[:, tt, :], in1=tabs[:, 2, :], op=ALU.add),
                      reads=[B_tabs], writes=[B_xt[tt]])

            def send_h2(g):
                mod_transpose(1)
                kb.dma('pool', H2s[g].rearrange("kc p t -> p kc t"), hT[:], sts_a, reads=[B_hT], writes=[B_h2s[g]])
                for kc in range(16):
                    kb.cc(H2s[g][kc], H2g[g][kc], ccs, reads=[B_h2s[g]], writes=[B_h2g[g]])

            def attn_out(g):
                woutv = wb["wout"].rearrange("(kc p) n -> p kc n", p=128)
                for c in range(4):
                    pass
                for rb in range(4):
                    kb.dma('sp', hT[:].rearrange("p (r rb) t -> p rb r t", rb=4)[:, rb], myY[g, rb].rearrange("(r p) t -> p r t", p=128), xs,
                           reads=[B_yg], writes=[B_hT])
                load_tabs([gate_rows[1:2, :], lnp[2:3, :], lnp[3:4, :]])
                fch = []
                for r in range(4):
                    fch += [2 * r, 2 * r + 1, 8 + 2 * r, 9 + 2 * r]
                for dg in range(4):
                    wi = loadA(lambda t, dg=dg: [(t[:], woutv[:, :, dg * 512:(dg + 1) * 512])], "wout")
                    for tt in range(4):
                        pi = 4 * (dg % 2) + tt
                        for ch in range(16):
                            kb.op('pe', lambda e, wi=wi, ch=ch, tt=tt, pi=pi: e.matmul(
                                ps[pi][:, :], lhsT=hT[:, ch, tt * 128:(tt + 1) * 128], rhs=wA[wi][:, fch[ch], :],
                                start=(ch == 0), stop=(ch == 15)), reads=[WA[wi], B_hT], writes=[PB[pi]], sig=(ch == 15))
                        residual(tt, pi, dg * 512)
                for tt in range(4):
                    ln_affine(tt)

            for g in range(NG):
                rows = slice(g * G, (g + 1) * G)
                if phase == 1:
                    kb.dma('sp', xt[:], x_in[rows, :].rearrange("(tt p) d -> p tt d", p=128), xs, writes=B_xt)
                    ffn("w1a", "w2a", 0, 0, 0, 1)
                    if STOP == "g0ffn":
                        kb.dma('pool', out_d[rows, :].rearrange("(tt p) d -> p tt d", p=128), xt[:], sts, reads=B_xt, writes=[B_outd])
                        break
                    kb.dma('pool', x1_d[rows, :].rearrange("(tt p) d -> p tt d", p=128), xt[:], sts, reads=B_xt, writes=[B_x1d])
                    if STOP == "g0":
                        kb.dma('pool', out_d[rows, :].rearrange("(tt p) d -> p tt d", p=128), xt[:], sts, reads=B_xt, writes=[B_outd])
                    send_h2(g)
                    if STOP == "g0":
                        break
                else:
                    kb.dma('sp', xt[:], x1_d[rows, :].rearrange("(tt p) d -> p tt d", p=128), xs, reads=[B_x1d], writes=B_xt)
                    attn_out(g)
                    ffn("w1b", "w2b", 2, 2, 4, 5)
                    kb.dma('pool', out_d[rows, :].rearrange("(tt p) d -> p tt d", p=128), xt[:], sts, reads=B_xt, writes=[B_outd])
            kb.barrier()

    jsp = nc.sync.partition_id() % 4
    H2s = [dscr("h2s%d" % g, [16, 128, G], BF16) for g in range(NG)]
    H2g = [dscr("h2g%d" % g, [16, 512, G], BF16) for g in range(NG)]
    B_h2s = [Buf("h2s%d" % g) for g in range(NG)]
    B_h2g = [Buf("h2g%d" % g) for g in range(NG)]
    FM_d = dscr("fm_d", [6, 128, S], BF16)
    TM_d = dscr("tm_d", [S, 768], BF16)
    GT_d = dscr("gt_d", [S, 4], F32)
    YS = dscr("ys_p", [32, 4, 128, G], BF16)
    YG = dscr("yg_p", [32, 4, 512, G], BF16)
    myY = dscr("myY", [8, 4, 512, G], BF16)
    B_fm = Buf("fm")
    B_yg = Buf("yg")
    B_outd = Buf("outd")

    ffn_phase(1)
    if STOP in ("g0ffn", "g0", "p1a"):
        kb.barrier()
        return nc
    zproj_own(nc, kb, ps, PB, H2g, B_h2g, wino, gbo, FM_d, TM_d, GT_d, B_fm)
    if STOP == "p1":
        dF = nc.dram_tensor("dbgF", [6, 128, S], BF16, kind="ExternalOutput").ap()
        dT = nc.dram_tensor("dbgT", [S, 768], BF16, kind="ExternalOutput").ap()
        d32 = nc.dram_tensor("dbg32", [S, 4], F32, kind="ExternalOutput").ap()
        dx1 = nc.dram_tensor("dbgx1", [TPC, D], F32, kind="ExternalOutput").ap()
        dsl = kb.slot("dbg")
        kb.dma('sp', dF, FM_d, dsl, reads=[B_fm])
        kb.dma('sp', dT, TM_d, dsl, reads=[B_fm])
        kb.dma('sp', d32, GT_d, dsl, reads=[B_fm])
        kb.dma('sp', dx1, x1_d, dsl)
        kb.barrier()
        return nc
    attention(nc, kb, ps, PB, FM_d, TM_d, GT_d, B_fm, YS, YG, B_yg, ident, identb, onesb, onesf, tri, sel127,
              B_const, mask_d, convw, convb, normg, epsc)
    if STOP == "attn":
        dY = nc.dram_tensor("dbgY", [32, 4, 128, G], BF16, kind="ExternalOutput").ap()
        dsl = kb.slot("dbg")
        kb.dma('sp', dY, YS, dsl)
        kb.barrier()
        return nc
    ysl = kb.slot("myY")
    for t in range(8):
        kb.dma('sp', myY[t], YG[jsp * 8 + t], ysl, reads=[B_yg], writes=[B_yg])
    ffn_phase(3)
    kb.barrier()
    return nc


def zproj_own(nc, kb, ps, PB, H2g, B_h2g, wino, gbo, FM_d, TM_d, GT_d, B_fm):
    _es = contextlib.ExitStack()
    wo = _es.enter_context(nc.sbuf_tensor(_u("wo"), [128, 16, 1540], BF16))
    h2a = _es.enter_context(nc.sbuf_tensor(_u("h2a"), [128, 16, G], BF16))
    h2b = _es.enter_context(nc.sbuf_tensor(_u("h2b"), [128, 16, G], BF16))
    stFa = _es.enter_context(nc.sbuf_tensor(_u("stFa"), [128, 6, G], BF16))
    stFb = _es.enter_context(nc.sbuf_tensor(_u("stFb"), [128, 6, G], BF16))
    stTa = _es.enter_context(nc.sbuf_tensor(_u("stTa"), [128, 4, 768], BF16))
    stTb = _es.enter_context(nc.sbuf_tensor(_u("stTb"), [128, 4, 768], BF16))
    stGa = _es.enter_context(nc.sbuf_tensor(_u("stGa"), [128, 4, 4], F32))
    stGb = _es.enter_context(nc.sbuf_tensor(_u("stGb"), [128, 4, 4], F32))
    gb = _es.enter_context(nc.sbuf_tensor(_u("gbo"), [128, 4], F32))
    with _es:
        h2 = [h2a, h2b]
        stF = [stFa, stFb]
        stT = [stTa, stTb]
        stG = [stGa, stGb]
        B_h2 = [Buf("h2a"), Buf("h2b")]
        B_stF = [Buf("stFa"), Buf("stFb")]
        B_stT = [Buf("stTa"), Buf("stTb")]
        B_stG = [Buf("stGa"), Buf("stGb")]
        B_wo = Buf("wo")
        wsl = kb.slot("wo")
        hsl = [kb.slot("h2a"), kb.slot("h2b")]
        ssl = [kb.slot("zsta"), kb.slot("zstb")]
        kb.dma('sp', wo[:], wino.rearrange("(kc p) n -> p kc n", p=128), wsl, reads=[], writes=[B_wo])
        kb.dma('sp', gb[:], gbo.partition_broadcast(128), wsl, writes=[B_wo])
        FMv = FM_d.rearrange("c p t -> p c t")

        def load(tg):
            g, r = tg // 4, tg % 4
            b = tg % 2
            kb.dma('sp', h2[b][:], H2g[g][:, r * 128:(r + 1) * 128, :].rearrange("kc p t -> p kc t"), hsl[b],
                   reads=[B_h2g[g]], writes=[B_h2[b]])
        load(0)
        _ZS = os.environ.get('ZPSKIP', '')
        NB_ = int(os.environ.get('ZPNB', '260'))
        for tg in range(int(os.environ.get('ZPN', '32'))):
            b = tg % 2
            tok0 = ((tg % 4) * 8 + tg // 4) * G
            if tg + 1 < 32:
                load(tg + 1)
            for c in range(0 if 'fm' in _ZS else 6):
                pi = c % 4
                for kc in range(16):
                    kb.op('pe', lambda e, c=c, kc=kc, pi=pi, b=b: e.matmul(ps[pi][:, :], lhsT=wo[:, kc, c * 128:(c + 1) * 128], rhs=h2[b][:, kc, :],
                                                                      start=(kc == 0), stop=(kc == 15)),
                          reads=[B_wo, B_h2[b]], writes=[PB[pi]], sig=(kc == 15))
                if c % 2 == 0:
                    kb.op('act', lambda e, c=c, pi=pi, b=b: e.activation(out=stF[b][:, c, :], in_=ps[pi][:, :], func=AF.Copy),
                          reads=[PB[pi]], writes=[B_stF[b]])
                else:
                    kb.op('dve', lambda e, c=c, pi=pi, b=b: e.tensor_copy(out=stF[b][:, c, :], in_=ps[pi][:, :]),
                          reads=[PB[pi]], writes=[B_stF[b]])
            if 'fm' not in _ZS:
                kb.dma('pool', FMv[:, :, tok0:tok0 + G], stF[b][:], ssl[b], reads=[B_stF[b]], writes=[B_fm])
            for tt in range(0 if 'tm' in _ZS else 4):
                pa, pb_ = (4, 6) if tt % 2 == 0 else (5, 7)
                for kc in range(16):
                    kb.op('pe', lambda e, kc=kc, tt=tt, pa=pa, b=b: e.matmul(ps[pa][:, :], lhsT=h2[b][:, kc, tt * 128:(tt + 1) * 128], rhs=wo[:, kc, 768:1280],
                                                                        start=(kc == 0), stop=(kc == 15)),
                          reads=[B_wo, B_h2[b]], writes=[PB[pa]], sig=(kc == 15))
                for kc in range(16):
                    kb.op('pe', lambda e, kc=kc, tt=tt, pb_=pb_, b=b: e.matmul(ps[pb_][:, 0:NB_], lhsT=h2[b][:, kc, tt * 128:(tt + 1) * 128], rhs=wo[:, kc, 1280:1280 + NB_],
                                                                          start=(kc == 0), stop=(kc == 15)),
                          reads=[B_wo, B_h2[b]], writes=[PB[pb_]], sig=(kc == 15))
                kb.op('dve', lambda e, tt=tt, pa=pa, b=b: e.tensor_copy(out=stT[b][:, tt, 0:512], in_=ps[pa][:, :]), reads=[PB[pa]], writes=[B_stT[b]])
                if 'sig' not in _ZS:
                    kb.op('act', lambda e, tt=tt, pb_=pb_, b=b: e.activation(out=stT[b][:, tt, 512:768], in_=ps[pb_][:, 0:256], func=AF.Sigmoid),
                          reads=[PB[pb_]], writes=[B_stT[b]])
                if 'gadd' not in _ZS:
                    kb.op('act', lambda e, tt=tt, pb_=pb_, b=b: e.activation(out=stG[b][:, tt, :], in_=ps[pb_][:, 256:260], func=AF.Copy),
                          reads=[PB[pb_]], writes=[B_stG[b]])
                    kb.op('dve', lambda e, tt=tt, b=b: e.tensor_tensor(out=stG[b][:, tt, :], in0=stG[b][:, tt, :], in1=gb[:], op=ALU.add),
                          reads=[B_wo], writes=[B_stG[b]])
            for (a_, b_) in (() if ('tm' in _ZS or 'ls' in _ZS) else ((0, 2), (3, 4))):
                kb.op('act', lambda e, a_=a_, b_=b_, b=b: e.activation(out=stG[b][:, :, a_:b_], in_=stG[b][:, :, a_:b_], func=AF.Exp, scale=-1.0),
                      reads=[B_stG[b]], writes=[B_stG[b]])
                kb.op('act', lambda e, a_=a_, b_=b_, b=b: e.activation(out=stG[b][:, :, a_:b_], in_=stG[b][:, :, a_:b_], func=AF.Ln, bias=1.0),
                      reads=[B_stG[b]], writes=[B_stG[b]])
                kb.op('dve', lambda e, a_=a_, b_=b_, b=b: e.tensor_scalar(out=stG[b][:, :, a_:b_], in0=stG[b][:, :, a_:b_], scalar1=-1.0, scalar2=None,
                                                                         op0=ALU.mult), reads=[B_stG[b]], writes=[B_stG[b]])
            if 'tm' not in _ZS and 'tst' not in _ZS:
              kb.dma('pool', TM_d[tok0:tok0 + G, :].rearrange("(tt p) c -> p tt c", p=128), stT[b][:], ssl[b], reads=[B_stT[b]], writes=[B_fm])
            if 'tm' not in _ZS and 'gst' not in _ZS:
              kb.dma('pool', GT_d[tok0:tok0 + G, :].rearrange("(tt p) c -> p tt c", p=128), stG[b][:], ssl[b], reads=[B_stG[b]], writes=[B_fm])
        kb.barrier()


def attention(nc, kb, ps, PB, FM_d, TM_d, GT_d, B_fm, YS, YG, B_yg, ident, identb, onesb, onesf, tri, sel127,
              B_const, mask_d, convw_d, convb_d, normg_d, epsc):
    TMv = TM_d.rearrange("(t p) c -> p t c", p=128)
    GTv = GT_d.rearrange("(t p) c -> p t c", p=128)
    B_ysp = Buf("ysp")
    SCALE = 128.0 ** -0.5
    NT = 128
    ccs = kb.slot("ccy")
    sts = kb.slot("ysto")

    _es = contextlib.ExitStack()
    KT = _es.enter_context(nc.sbuf_tensor(_u("KT"), [128, S], BF16))
    QT = _es.enter_context(nc.sbuf_tensor(_u("QT"), [128, S], BF16))
    V = _es.enter_context(nc.sbuf_tensor(_u("V"), [128, NT, 128], BF16))
    lf = _es.enter_context(nc.sbuf_tensor(_u("lf"), [128, NT, 2], F32))
    Fm = _es.enter_context(nc.sbuf_tensor(_u("Fm"), [128, 2, NT], F32))
    scan = _es.enter_context(nc.sbuf_tensor(_u("scan"), [128, 2, 2 * NT], F32))
    offb = _es.enter_context(nc.sbuf_tensor(_u("offb"), [128, 2 * NT], F32))
    negF = _es.enter_context(nc.sbuf_tensor(_u("negF"), [128, 2, NT], F32))
    masks = _es.enter_context(nc.sbuf_tensor(_u("masks"), [128, 4, 512], F32))
    lg = _es.enter_context(nc.sbuf_tensor(_u("lg"), [128, 4, 512], F32))
    PT = _es.enter_context(nc.sbuf_tensor(_u("PT"), [128, 4, 512], BF16))
    fq = _es.enter_context(nc.sbuf_tensor(_u("fq"), [128, 2, 512], F32))
    rden = _es.enter_context(nc.sbuf_tensor(_u("rden"), [128, 512], F32))
    yst = _es.enter_context(nc.sbuf_tensor(_u("yst"), [128, 2, 512], BF16))
    with _es:
        lds = kb.slot("fxld")
        mks = kb.slot("fxmask")
        B_KT, B_QT, B_V, B_lf, B_F, B_scan, B_mask = Buf("KT"), Buf("QT"), Buf("V"), Buf("lf"), Buf("F"), Buf("scan"), Buf("mask")
        B_lg = [Buf("lg%d" % i) for i in range(4)]
        B_PT = [Buf("PT%d" % i) for i in range(4)]
        B_fq = [Buf("fq0"), Buf("fq1")]
        B_rden = Buf("rden")
        B_yst = [Buf("yst0"), Buf("yst1")]
        mks0 = kb.slot("fxmask0")
        kb.dma('sp', masks[:], mask_d, mks0, writes=[B_mask])
        for i8 in range(8):
            kb.dma('sp', lf[:, i8 * 16:(i8 + 1) * 16, :], GTv[:, i8 * 16:(i8 + 1) * 16, 0:2], mks, reads=[B_fm], writes=[B_lf])
        lfv = lf[:].rearrange("p t h -> p h t")
        kb.op('dve', lambda e: e.tensor_copy(out=scan[:, 0, :].rearrange("p (h t) -> p h t", h=2), in_=lfv), reads=[B_lf], writes=[B_scan])
        kb.op('pe', lambda e: e.matmul(ps[0][:, 0:256], lhsT=tri[:], rhs=scan[:, 0, :], start=True, stop=True),
              reads=[B_scan, B_const], writes=[PB[0]])
        kb.op('dve', lambda e: e.tensor_copy(out=Fm[:].rearrange("p h t -> p (h t)"), in_=ps[0][:, 0:256]), reads=[PB[0]], writes=[B_F])
        kb.op('pe', lambda e: e.matmul(ps[1][:, 0:256], lhsT=sel127[:], rhs=Fm[:].rearrange("p h t -> p (h t)"), start=True, stop=True),
              reads=[B_F, B_const], writes=[PB[1]])
        kb.op('dve', lambda e: e.tensor_copy(out=scan[:, 0, :], in_=ps[1][:, 0:256]), reads=[PB[1]], writes=[B_scan])
        kb.op('dve', lambda e: e.tensor_copy(out=offb[:], in_=scan[:, 0, :]), reads=[B_scan], writes=[B_scan])
        cur = 0
        sh = 1
        while sh < NT:
            nxt = 1 - cur
            sc3 = scan[:, cur, :].rearrange("p (h t) -> p h t", h=2)
            sn3 = scan[:, nxt, :].rearrange("p (h t) -> p h t", h=2)
            kb.op('dve', lambda e, sc3=sc3, sn3=sn3, sh=sh: e.tensor_copy(out=sn3[:, :, 0:sh], in_=sc3[:, :, 0:sh]),
                  reads=[B_scan], writes=[B_scan])
            kb.op('dve', lambda e, sc3=sc3, sn3=sn3, sh=sh: e.tensor_tensor(out=sn3[:, :, sh:NT], in0=sc3[:, :, sh:NT], in1=sc3[:, :, 0:NT - sh],
                                                                          op=ALU.add), reads=[B_scan], writes=[B_scan])
            cur = nxt
            sh *= 2
        kb.op('dve', lambda e: e.tensor_tensor(out=offb[:], in0=scan[:, cur, :], in1=offb[:], op=ALU.subtract), reads=[B_scan], writes=[B_scan])
        kb.op('dve', lambda e: e.tensor_tensor(out=Fm[:].rearrange("p h t -> p (h t)"), in0=Fm[:].rearrange("p h t -> p (h t)"), in1=offb[:],
                                               op=ALU.add), reads=[B_scan], writes=[B_F])
        kb.op('dve', lambda e: e.tensor_scalar(out=negF[:].rearrange("p h t -> p (h t)"), in0=Fm[:].rearrange("p h t -> p (h t)"),
                                               scalar1=-1.0, scalar2=None, op0=ALU.mult), reads=[B_F], writes=[B_F])

        for hh in range(0 if 'fox' in os.environ.get('ATSKIP', '') else 2):
            kb.dma('sp', QT[:], FM_d[hh], lds, reads=[B_fm], writes=[B_QT])
            kb.dma('sp', KT[:], FM_d[2 + hh], lds, reads=[B_fm], writes=[B_KT])
            for i8 in range(8):
                kb.dma('sp', V[:, i8 * 16:(i8 + 1) * 16, :], TMv[:, i8 * 16:(i8 + 1) * 16, hh * 128:(hh + 1) * 128], lds, reads=[B_fm], writes=[B_V])
            NS = 4
            pairs = []
            for qg in range(32):
                pairs.append((qg, -1))
                pairs += [(qg, kt) for kt in range(4 * qg + 4)]
            LA = NS

            def s_stage(n):
                qg, kt = pairs[n]
                fb = qg % 2
                si = n % NS
                if kt < 0:
                    for c4 in range(4):
                        kb.op('pe', lambda e, c4=c4, qg=qg: e.matmul(
                            ps[si][:, c4 * 128:(c4 + 1) * 128], lhsT=Fm[:, hh, 4 * qg + c4:4 * qg + c4 + 1].to_broadcast([128, 128]),
                            rhs=ident[:], start=True, stop=True), reads=[B_F, B_const], writes=[PB[si]], sig=(c4 == 3))
                    kb.op('dve', lambda e, fb=fb: e.tensor_copy(out=fq[:, fb, :], in_=ps[si][:, :]), reads=[PB[si]], writes=[B_fq[fb]])
                    return
                kb.op('pe', lambda e, kt=kt, qg=qg, si=si: e.matmul(ps[si][:, :], lhsT=KT[:, kt * 128:(kt + 1) * 128],
                                                                  rhs=QT[:, qg * 512:(qg + 1) * 512], start=True, stop=True),
                      reads=[B_KT, B_QT], writes=[PB[si]])
                kb.op('dve', lambda e, si=si, fb=fb: e.scalar_tensor_tensor(out=lg[:, si, :], in0=ps[si][:, :], scalar=SCALE, in1=fq[:, fb, :],
                                                                          op0=ALU.mult, op1=ALU.add),
                      reads=[PB[si], B_fq[fb]], writes=[B_lg[si]])
                if kt >= 4 * qg:
                    dd = kt - 4 * qg
                    kb.op('dve', lambda e, si=si, dd=dd: e.tensor_tensor(out=lg[:, si, :], in0=lg[:, si, :], in1=masks[:, dd, :], op=ALU.add),
                          reads=[B_mask], writes=[B_lg[si]])
                kb.op('act', lambda e, si=si, kt=kt: e.activation(out=PT[:, si, :], in_=lg[:, si, :], func=AF.Exp, bias=negF[:, hh, kt:kt + 1]),
                      reads=[B_lg[si], B_F], writes=[B_PT[si]])

            def p_stage(n):
                qg, kt = pairs[n]
                if kt < 0:
                    return
                fb = qg % 2
                si = n % NS
                nk = 4 * qg + 4
                po, pd = (4, 5) if qg % 2 == 0 else (6, 7)
                last = (kt == nk - 1)
                kb.op('pe', lambda e, si=si, kt=kt, po=po, last=last: e.matmul(ps[po][:, :], lhsT=V[:, kt, :], rhs=PT[:, si, :],
                                                                            start=(kt == 0), stop=last),
                      reads=[B_V, B_PT[si]], writes=[PB[po]], sig=False)
                kb.op('pe', lambda e, si=si, kt=kt, pd=pd, last=last: e.matmul(ps[pd][:, :], lhsT=onesb[:], rhs=PT[:, si, :],
                                                                            start=(kt == 0), stop=last),
                      reads=[B_const, B_PT[si]], writes=[PB[pd]], sig=True)
                if last:
                    kb.op('dve', lambda e, pd=pd: e.reciprocal(out=rden[:], in_=ps[pd][:, :]), reads=[PB[pd]], writes=[B_rden])
                    kb.op('dve', lambda e, po=po, fb=fb: e.tensor_tensor(out=yst[:, fb, :], in0=ps[po][:, :], in1=rden[:], op=ALU.mult),
                          reads=[PB[po], B_rden], writes=[B_yst[fb]])
                    kb.dma('pool', YS[qg, hh], yst[:, fb, :], sts, reads=[B_yst[fb]], writes=[B_ysp])
                    kb.cc(YS[qg, hh], YG[qg, hh], ccs, reads=[B_ysp], writes=[B_yg])

            for n in range(LA):
                s_stage(n)
            for n in range(len(pairs)):
                p_stage(n)
                if n + LA < len(pairs):
                    s_stage(n + LA)
        kb.barrier()

    QW = 4096
    _es = contextlib.ExitStack()
    zq = _es.enter_context(nc.sbuf_tensor(_u("zq"), [128, QW + 3], BF16))
    zk = _es.enter_context(nc.sbuf_tensor(_u("zk"), [128, QW + 3], BF16))
    qT = _es.enter_context(nc.sbuf_tensor(_u("qT"), [128, QW], BF16))
    kT = _es.enter_context(nc.sbuf_tensor(_u("kT"), [128, QW], BF16))
    v1 = _es.enter_context(nc.sbuf_tensor(_u("v1"), [128, 32, 257], BF16))
    og = _es.enter_context(nc.sbuf_tensor(_u("og"), [128, 32, 256], BF16))
    gi = _es.enter_context(nc.sbuf_tensor(_u("gi"), [128, 32, 2], F32))
    gsm = _es.enter_context(nc.sbuf_tensor(_u("gsm"), [128, 8, 32], F32))
    cw = _es.enter_context(nc.sbuf_tensor(_u("cw"), [128, 8], F32))
    cb = _es.enter_context(nc.sbuf_tensor(_u("cb"), [128, 2], F32))
    ngt = _es.enter_context(nc.sbuf_tensor(_u("ngt"), [128, 256], F32))
    acc = _es.enter_context(nc.sbuf_tensor(_u("acc"), [128, 512], F32))
    trim = _es.enter_context(nc.sbuf_tensor(_u("trim"), [128, 128], F32))
    e1 = _es.enter_context(nc.sbuf_tensor(_u("e1"), [128, 128], F32))
    DT = _es.enter_context(nc.sbuf_tensor(_u("DT"), [128, 128], F32))
    SD = _es.enter_context(nc.sbuf_tensor(_u("SD"), [128, 128], BF16))
    qc = _es.enter_context(nc.sbuf_tensor(_u("qc"), [128, 257], F32))
    nd = _es.enter_context(nc.sbuf_tensor(_u("nd"), [128, 257], F32))
    sm = _es.enter_context(nc.sbuf_tensor(_u("sm"), [128, 8], F32))
    hh_ = _es.enter_context(nc.sbuf_tensor(_u("hh_"), [128, 256], F32))
    sq = _es.enter_context(nc.sbuf_tensor(_u("sq"), [128, 256], F32))
    go = _es.enter_context(nc.sbuf_tensor(_u("go"), [128, 256], F32))
    yy = _es.enter_context(nc.sbuf_tensor(_u("yy"), [128, 256], F32))
    kw = _es.enter_context(nc.sbuf_tensor(_u("kw"), [128, 128], BF16))
    Cs = _es.enter_context(nc.sbuf_tensor(_u("Cs"), [128, 257], F32))
    Cb = _es.enter_context(nc.sbuf_tensor(_u("Cb"), [128, 257], BF16))
    yTs = _es.enter_context(nc.sbuf_tensor(_u("yTs"), [128, 2, QW], BF16))
    with _es:
        lds = kb.slot("mlld")
        B_z, B_qk, B_v1, B_og, B_gi, B_gsm, B_c = Buf("z"), Buf("qk"), Buf("v1"), Buf("og"), Buf("gi"), Buf("gsm"), Buf("mc")
        B_acc, B_e1, B_DT, B_SD, B_qc, B_nd, B_sm, B_h, B_sq, B_go, B_yy, B_kw, B_C, B_Cb, B_yT = [Buf(n) for n in
            ("acc", "e1", "DT", "SD", "qc", "nd", "sm", "h", "sq", "go", "yy", "kw", "C", "Cb", "yT")]
        B_yTt = [Buf("yT%d" % i) for i in range(8)]
        e1p = [e1, _es.enter_context(nc.sbuf_tensor(_u("e1b"), [128, 128], F32))]
        DTp = [DT, _es.enter_context(nc.sbuf_tensor(_u("DTb"), [128, 128], F32))]
        SDp = [SD, _es.enter_context(nc.sbuf_tensor(_u("SDb"), [128, 128], BF16))]
        kwp = [kw, _es.enter_context(nc.sbuf_tensor(_u("kwb"), [128, 128], BF16))]
        B_e1p = [Buf("e1a"), Buf("e1b")]
        B_DTp = [Buf("DTa"), Buf("DTb")]
        B_SDp = [Buf("SDa"), Buf("SDb")]
        B_kwp = [Buf("kwa"), Buf("kwb")]
        kb.dma('sp', cw[:], convw_d, lds, writes=[B_c])
        kb.dma('sp', cb[:], convb_d, lds, writes=[B_c])
        kb.dma('sp', ngt[:], normg_d.partition_broadcast(128), lds, writes=[B_c])
        kb.dma('sp', trim[:], mask_d[:, 0, 0:128], lds, writes=[B_c])
        kb.op('dve', lambda e: e.memset(Cs[:], 0.0), writes=[B_C])
        kb.op('dve', lambda e: e.memset(Cb[:], 0.0), writes=[B_Cb])
        kb.op('dve', lambda e: e.memset(zq[:, 0:3], 0.0), writes=[B_z])
        kb.op('dve', lambda e: e.memset(zk[:, 0:3], 0.0), writes=[B_z])
        for c in range(0 if 'mlstm' in os.environ.get('ATSKIP', '') else 4):
            if c > 0:
                kb.op('dve', lambda e: e.tensor_copy(out=zq[:, 0:3], in_=zq[:, QW:QW + 3]), reads=[B_z], writes=[B_z])
                kb.op('dve', lambda e: e.tensor_copy(out=zk[:, 0:3], in_=zk[:, QW:QW + 3]), reads=[B_z], writes=[B_z])
            kb.dma('sp', zq[:, 3:3 + QW], FM_d[4][:, c * QW:(c + 1) * QW], lds, reads=[B_fm], writes=[B_z])
            kb.dma('sp', zk[:, 3:3 + QW], FM_d[5][:, c * QW:(c + 1) * QW], lds, reads=[B_fm], writes=[B_z])
            for i4 in range(4):
                tsl = slice(c * 32 + i4 * 8, c * 32 + (i4 + 1) * 8)
                kb.dma('sp', v1[:, i4 * 8:(i4 + 1) * 8, 0:256], TMv[:, tsl, 256:512], lds, reads=[B_fm], writes=[B_v1])
                kb.dma('sp', og[:, i4 * 8:(i4 + 1) * 8, :], TMv[:, tsl, 512:768], lds, reads=[B_fm], writes=[B_og])
                kb.dma('sp', gi[:, i4 * 8:(i4 + 1) * 8, :], GTv[:, tsl, 2:4], lds, reads=[B_fm], writes=[B_gi])
            kb.op('dve', lambda e: e.memset(v1[:, :, 256:257], 1.0), writes=[B_v1])
            for which, (zz, oo) in enumerate(((zq, qT), (zk, kT))):
                for pc in range(QW // 512):
                    t0 = pc * 512
                    kb.op('dve', lambda e, zz=zz, t0=t0: e.tensor_scalar(out=acc[:], in0=zz[:, t0:t0 + 512], scalar1=cw[:, which * 4:which * 4 + 1],
                                                                       scalar2=None, op0=ALU.mult), reads=[B_z, B_c], writes=[B_acc])
                    for j_ in range(1, 4):
                        kb.op('dve', lambda e, zz=zz, t0=t0, j_=j_: e.scalar_tensor_tensor(
                            out=acc[:], in0=zz[:, t0 + j_:t0 + j_ + 512], scalar=cw[:, which * 4 + j_:which * 4 + j_ + 1], in1=acc[:],
                            op0=ALU.mult, op1=ALU.add), reads=[B_z, B_c], writes=[B_acc])
                    if which == 0:
                        kb.op('act', lambda e, oo=oo, t0=t0: e.activation(out=oo[:, t0:t0 + 512], in_=acc[:], func=AF.Silu, bias=cb[:, 0:1]),
                              reads=[B_acc, B_c], writes=[B_qk])
                    else:
                        kb.op('act', lambda e: e.activation(out=acc[:], in_=acc[:], func=AF.Silu, bias=cb[:, 1:2]), reads=[B_c], writes=[B_acc])
                        kb.op('dve', lambda e, oo=oo, t0=t0: e.tensor_scalar(out=oo[:, t0:t0 + 512], in0=acc[:], scalar1=SCALE, scalar2=None, op0=ALU.mult),
                              reads=[B_acc], writes=[B_qk])
            giv = gi[:].rearrange("p t c -> p c t")
            kb.op('dve', lambda e: e.tensor_copy(out=gsm[:, 6, :], in_=giv[:, 1, :]), reads=[B_gi], writes=[B_gsm])
            kb.op('dve', lambda e: e.tensor_copy(out=gsm[:, 7, :], in_=giv[:, 0, :]), reads=[B_gi], writes=[B_gsm])
            kb.op('pe', lambda e: e.matmul(ps[0][:, 0:32], lhsT=tri[:], rhs=gsm[:, 6, :], start=True, stop=True), reads=[B_gsm, B_const], writes=[PB[0]])
            kb.op('pe', lambda e: e.matmul(ps[1][:, 0:32], lhsT=onesf[:], rhs=gsm[:, 6, :], start=True, stop=True), reads=[B_gsm, B_const], writes=[PB[1]])
            kb.op('dve', lambda e: e.tensor_copy(out=gsm[:, 0, :], in_=ps[0][:, 0:32]), reads=[PB[0]], writes=[B_gsm])
            kb.op('dve', lambda e: e.tensor_copy(out=gsm[:, 1, :], in_=ps[1][:, 0:32]), reads=[PB[1]], writes=[B_gsm])
            kb.op('dve', lambda e: e.tensor_tensor(out=gsm[:, 2, :], in0=gsm[:, 7, :], in1=gsm[:, 0, :], op=ALU.subtract), reads=[B_gsm], writes=[B_gsm])
            kb.op('dve', lambda e: e.tensor_tensor(out=gsm[:, 3, :], in0=gsm[:, 2, :], in1=gsm[:, 1, :], op=ALU.add), reads=[B_gsm], writes=[B_gsm])
            kb.op('act', lambda e: e.activation(out=gsm[:, 3, :], in_=gsm[:, 3, :], func=AF.Exp), reads=[B_gsm], writes=[B_gsm])
            kb.op('act', lambda e: e.activation(out=gsm[:, 4, :], in_=gsm[:, 0, :], func=AF.Exp), reads=[B_gsm], writes=[B_gsm])
            kb.op('act', lambda e: e.activation(out=gsm[:, 5, :], in_=gsm[:, 1, :], func=AF.Exp), reads=[B_gsm], writes=[B_gsm])
            def stage_a(ch):
                par = ch % 2
                cs = slice(ch * 128, (ch + 1) * 128)
                kb.op('pe', lambda e: e.matmul(ps[par][:, 0:128], lhsT=kT[:, cs], rhs=qT[:, cs], start=True, stop=True),
                      reads=[B_qk], writes=[PB[par]], sig=False)
                kb.op('pe', lambda e: e.matmul(ps[par][:, 128:256], lhsT=gsm[:, 0, ch:ch + 1].to_broadcast([128, 128]), rhs=ident[:],
                                               start=True, stop=True), reads=[B_gsm, B_const], writes=[PB[par]])
                kb.op('pe', lambda e: e.matmul(ps[2 + par][:, 0:128], lhsT=kT[:, cs], rhs=identb[:], start=True, stop=True),
                      reads=[B_qk, B_const], writes=[PB[2 + par]])
                kb.op('dve', lambda e: e.tensor_tensor(out=e1p[par][:], in0=ps[par][:, 128:256], in1=trim[:], op=ALU.add),
                      reads=[PB[par], B_c], writes=[B_e1p[par]])
                kb.op('act', lambda e: e.activation(out=DTp[par][:], in_=e1p[par][:], func=AF.Exp, bias=gsm[:, 2, ch:ch + 1]),
                      reads=[B_e1p[par], B_gsm], writes=[B_DTp[par]])
                kb.op('dve', lambda e: e.tensor_tensor(out=SDp[par][:], in0=ps[par][:, 0:128], in1=DTp[par][:], op=ALU.mult),
                      reads=[PB[par], B_DTp[par]], writes=[B_SDp[par]])
                kb.op('act', lambda e: e.activation(out=kwp[par][:], in_=ps[2 + par][:, 0:128], func=AF.Identity, scale=gsm[:, 3, ch:ch + 1]),
                      reads=[PB[2 + par], B_gsm], writes=[B_kwp[par]])

            def stage_b(ch):
                par = ch % 2
                cs = slice(ch * 128, (ch + 1) * 128)
                kb.op('pe', lambda e: e.matmul(ps[4][:, 0:257], lhsT=SDp[par][:], rhs=v1[:, ch, :], start=True, stop=True),
                      reads=[B_SDp[par], B_v1], writes=[PB[4]])
                kb.op('pe', lambda e: e.matmul(ps[5][:, 0:257], lhsT=qT[:, cs], rhs=Cb[:], start=True, stop=True),
                      reads=[B_qk, B_Cb], writes=[PB[5]])
                kb.op('pe', lambda e: e.matmul(ps[7][:, 0:257], lhsT=kwp[par][:], rhs=v1[:, ch, :], start=True, stop=True),
                      reads=[B_kwp[par], B_v1], writes=[PB[7]])
                if ch > 0:
                    stage_t(ch - 1)
                kb.op('act', lambda e: e.activation(out=qc[:], in_=ps[5][:, 0:257], func=AF.Identity, scale=gsm[:, 4, ch:ch + 1]),
                      reads=[PB[5], B_gsm], writes=[B_qc])
                kb.op('dve', lambda e: e.scalar_tensor_tensor(out=Cs[:], in0=Cs[:], scalar=gsm[:, 5, ch:ch + 1], in1=ps[7][:, 0:257],
                                                             op0=ALU.mult, op1=ALU.add), reads=[PB[7], B_gsm], writes=[B_C])
                kb.op('dve', lambda e: e.tensor_copy(out=Cb[:], in_=Cs[:]), reads=[B_C], writes=[B_Cb])
                kb.op('dve', lambda e: e.tensor_tensor(out=nd[:], in0=ps[4][:, 0:257], in1=qc[:], op=ALU.add), reads=[PB[4], B_qc], writes=[B_nd])
                kb.op('act', lambda e: e.activation(out=sm[:, 5:6], in_=nd[:, 256:257], func=AF.Abs), reads=[B_nd], writes=[B_sm])
                kb.op('dve', lambda e: e.tensor_scalar(out=sm[:, 0:1], in0=sm[:, 5:6], scalar1=1.0, scalar2=None, op0=ALU.max),
                      reads=[B_sm], writes=[B_sm])
                kb.op('dve', lambda e: e.reciprocal(out=sm[:, 1:2], in_=sm[:, 0:1]), reads=[B_sm], writes=[B_sm])
                kb.op('dve', lambda e: e.tensor_scalar(out=hh_[:], in0=nd[:, 0:256], scalar1=sm[:, 1:2], scalar2=None, op0=ALU.mult),
                      reads=[B_nd, B_sm], writes=[B_h])
                kb.op('act', lambda e: e.activation(out=sq[:], in_=hh_[:], func=AF.Square, accum_out=sm[:, 2:3]), reads=[B_h], writes=[B_sq, B_sm])
                kb.op('act', lambda e: e.activation(out=sm[:, 4:5], in_=sm[:, 2:3], func=AF.Sqrt, scale=1.0 / 256.0, bias=epsc[:, 0:1]),
                      reads=[B_sm, B_const], writes=[B_sm])
                kb.op('dve', lambda e: e.reciprocal(out=sm[:, 3:4], in_=sm[:, 4:5]), reads=[B_sm], writes=[B_sm])
                kb.op('dve', lambda e: e.tensor_tensor(out=go[:], in0=og[:, ch, :], in1=ngt[:], op=ALU.mult), reads=[B_og, B_c], writes=[B_go])
                kb.op('dve', lambda e: e.scalar_tensor_tensor(out=yy[:], in0=hh_[:], scalar=sm[:, 3:4], in1=go[:], op0=ALU.mult, op1=ALU.mult),
                      reads=[B_h, B_sm, B_go], writes=[B_yy])

            def stage_t(ch):
                cs = slice(ch * 128, (ch + 1) * 128)
                for half in range(2):
                    kb.op('pe', lambda e, half=half: e.transpose(ps[6][:, half * 128:(half + 1) * 128], yy[:, half * 128:(half + 1) * 128], ident[:]),
                          reads=[B_yy, B_const], writes=[PB[6]], sig=(half == 1))
                kb.op('act', lambda e: e.activation(out=yTs[:, :, cs], in_=ps[6][:, 0:256].rearrange("p (a b) -> p a b", a=2), func=AF.Copy),
                      reads=[PB[6]], writes=[B_yTt[ch // 4]])
                if ch % 4 == 3:
                    tb = ch // 4
                    for a_ in range(2):
                        kb.dma('pool', YS[c * 8 + tb, 2 + a_], yTs[:, a_, tb * G:(tb + 1) * G], sts, reads=[B_yTt[tb]], writes=[B_ysp])
                        kb.cc(YS[c * 8 + tb, 2 + a_], YG[c * 8 + tb, 2 + a_], ccs, reads=[B_ysp], writes=[B_yg])

            stage_a(0)
            for ch in range(32):
                if ch + 1 < 32:
                    stage_a(ch + 1)
                stage_b(ch)
            stage_t(31)
        kb.barrier()


_CACHE = {}


def kernel(**inputs):
    f32 = np.float32
    x = np.asarray(inputs["x"], f32)
    c = np.asarray(inputs["c"], f32)
    sq = lambda k: np.ascontiguousarray(np.asarray(inputs[k], f32)[0])
    if "nc" not in _CACHE:
        _CACHE["nc"] = build()
    nc = _CACHE["nc"]
    lnp = np.stack([sq("ln1_g"), sq("ln1_b"), sq("ln2_g"), sq("ln2_b"), sq("ln3_g"), sq("ln3_b")], 0)
    gbias = np.concatenate([sq("fox_f_bias"), sq("mlstm_i_bias"), sq("mlstm_f_bias")])[None, :]
    conv_w = sq("mlstm_conv_w")
    conv_b = sq("mlstm_conv_b")
    norm_g = sq("mlstm_norm_g")
    b_adaT = np.ascontiguousarray(sq("b_ada").reshape(144, 128).T)
    ident = np.eye(128, dtype=f32)
    tri = np.triu(np.ones((128, 128), f32))
    sel = np.zeros((128, 128), f32)
    sel[127, :] = 1.0
    kk = np.arange(128)[:, None, None] + 128 * np.arange(4)[None, :, None]
    qq = np.arange(512)[None, None, :]
    masks = np.where(kk <= qq, 0.0, NEG).astype(f32)
    w_in_full = sq("w_in")
    shared = {"w_ada": sq("w_ada"), "b_adaT": b_adaT, "ffn1_w_in": sq("ffn1_w_in"), "ffn1_w_out": sq("ffn1_w_out"),
              "w_out": sq("w_out"), "ffn2_w_in": sq("ffn2_w_in"), "ffn2_w_out": sq("ffn2_w_out"),
              "lnp": lnp, "ident": ident, "tri": tri, "sel127": sel, "masks": masks}
    in_maps = []
    for core in range(8):
        b, j = core // 4, core % 4
        m = dict(shared)
        m["x"] = np.ascontiguousarray(x[b, j * TPC:(j + 1) * TPC, :])
        m["cT"] = np.ascontiguousarray(c[b].reshape(16, 128).T)
        cw = np.stack([conv_w[:, j * 128:(j + 1) * 128].T, conv_w[:, 512 + j * 128:512 + (j + 1) * 128].T], 1)
        m["convw"] = np.ascontiguousarray(cw.reshape(128, 8))
        m["convb"] = np.ascontiguousarray(np.stack([conv_b[j * 128:(j + 1) * 128], conv_b[512 + j * 128:512 + (j + 1) * 128]], 1))
        m["normg"] = np.ascontiguousarray(norm_g[j * 256:(j + 1) * 256][None, :])
        cols = np.concatenate([np.arange(2 * j * 128, 2 * j * 128 + 256), np.arange(1024 + 2 * j * 128, 1024 + 2 * j * 128 + 256),
                               np.arange(3080 + j * 128, 3080 + (j + 1) * 128), np.arange(3592 + j * 128, 3592 + (j + 1) * 128),
                               np.arange(2048 + 2 * j * 128, 2048 + 2 * j * 128 + 256), np.arange(4104 + j * 256, 4104 + (j + 1) * 256),
                               np.arange(5136 + j * 256, 5136 + (j + 1) * 256),
                               np.array([3072 + 2 * j, 3072 + 2 * j + 1, 5128 + j, 5132 + j])])
        m["win_own"] = np.ascontiguousarray(w_in_full[:, cols])
        m["gb_own"] = np.ascontiguousarray(gbias[0, [2 * j, 2 * j + 1, 8 + j, 12 + j]][None, :])
        in_maps.append(m)
    res = run_bass_kernel_spmd(nc, in_maps, core_ids=list(range(8)))
    out = np.empty((2, S, D), f32)
    for core in range(8):
        b, j = core // 4, core % 4
        out[b, j * TPC:(j + 1) * TPC, :] = res.results[core]["out"]
    if os.environ.get("KSTOP", ""):
        _CACHE["dbg"] = [{k: v for k, v in r.items() if k != "out"} for r in res.results]
    return out
```
